# Optimizing a Trainium2 kernel written in Bass

```python
import jax, jax.numpy as jnp
from jax import lax
import numpy as np

D_MODEL = 2048
BATCH = 4
SEQ = 2048
DEPTH = 1
DEC_BATCH = 128
DEC_SEQ = 1
PAST_LEN = 16384
PAGE_SIZE = 128

N_MEM = 256
CONV_WIDTH = 1024
CONV_K = 3
RNN_WIDTH = 1024
RNN_HEADS = 8
RNN_HEAD_DIM = RNN_WIDTH // RNN_HEADS
RNN_CONV_K = 4
RG_C = 8.0
ATTN_HEADS = 4
ATTN_HEAD_DIM = 256
ATTN_WIDTH = ATTN_HEADS * ATTN_HEAD_DIM
N_BRANCH = 3
IN_SIZES = (CONV_WIDTH, CONV_WIDTH, CONV_WIDTH, RNN_WIDTH, RNN_WIDTH, ATTN_WIDTH, N_BRANCH * D_MODEL)
IN_SPLITS = tuple(int(s) for s in np.cumsum(IN_SIZES)[:-1])
IN_WIDTH = int(sum(IN_SIZES))
PEER_HEADS = 8
PEER_NKEYS = 128
PEER_EXPERTS = PEER_NKEYS * PEER_NKEYS
PEER_TOPK = 16
PEER_QDIM = 256
PEER_HALF = PEER_QDIM // 2
PEER_BLOCK = 128
DN_ALPHA = (2.0 * DEPTH) ** 0.25
DN_BETA = (8.0 * DEPTH) ** -0.25
LN_EPS = 1e-5

kernel_name = "hybrid_conv_rglru_mem_peer_step"


def layernorm(x, g, b):
    xf = x.astype(jnp.float32)
    mu = jnp.mean(xf, axis=-1, keepdims=True)
    var = jnp.mean(jnp.square(xf - mu), axis=-1, keepdims=True)
    return ((xf - mu) * lax.rsqrt(var + LN_EPS) * g.astype(jnp.float32) + b.astype(jnp.float32)).astype(x.dtype)


def causal_dwconv(buf, x, w):
    xp = jnp.concatenate([buf.astype(x.dtype), x], axis=1)
    k_w = w.shape[0]
    s = x.shape[1]
    y = sum(xp[:, k:k + s] * w[k] for k in range(k_w))
    return y, xp[:, xp.shape[1] - (k_w - 1):]


def rg_lru(xr, h0, wa, ba, wx, bx, lam):
    bn, s, _ = xr.shape
    xh = xr.reshape(bn, s, RNN_HEADS, RNN_HEAD_DIM)
    r = jax.nn.sigmoid(jnp.einsum('bshi,hij->bshj', xh, wa).reshape(bn, s, RNN_WIDTH) + ba)
    i = jax.nn.sigmoid(jnp.einsum('bshi,hij->bshj', xh, wx).reshape(bn, s, RNN_WIDTH) + bx)
    log_a = -RG_C * r.astype(jnp.float32) * jax.nn.softplus(-lam.astype(jnp.float32))
    a = jnp.exp(log_a)
    u = jnp.sqrt(-jnp.expm1(2.0 * log_a)) * (i * xr).astype(jnp.float32)

    def step(h, au):
        a_t, u_t = au
        h = a_t * h + u_t
        return h, h

    h_last, hs = lax.scan(step, h0.astype(jnp.float32), (jnp.swapaxes(a, 0, 1), jnp.swapaxes(u, 0, 1)))
    return jnp.swapaxes(hs, 0, 1).astype(xr.dtype), h_last.astype(h0.dtype)


def mem_attention(q, mem_k, mem_v):
    bn, s, _ = q.shape
    qh = q.reshape(bn, s, ATTN_HEADS, ATTN_HEAD_DIM)
    sc = jnp.einsum('bqhd,bkhd->bhqk', qh, mem_k.astype(q.dtype)).astype(jnp.float32) * (ATTN_HEAD_DIM ** -0.5)
    p = jax.nn.softmax(sc, axis=-1)
    o = jnp.einsum('bhqk,bkhd->bqhd', p.astype(q.dtype), mem_v.astype(q.dtype))
    return o.reshape(bn, s, ATTN_WIDTH)


def mixer_sublayer(x, mem_k, mem_v, conv_buf, rg_buf, rg_h,
                   w_in, conv_w, rg_conv_w, rg_conv_b, rg_wa, rg_ba, rg_wx, rg_bx, rg_lambda,
                   w_br_conv, w_br_rnn, w_br_attn, w_o):
    bn, s, _ = x.shape
    z = x @ w_in
    cb, cc, cx, rx, rgate, q, g = jnp.split(z, IN_SPLITS, axis=-1)
    conv_y, conv_buf_new = causal_dwconv(conv_buf, cc * cx, conv_w)
    y_conv = cb * conv_y
    xr, rg_buf_new = causal_dwconv(rg_buf, rx, rg_conv_w)
    xr = xr + rg_conv_b
    h_seq, h_new = rg_lru(xr, rg_h, rg_wa, rg_ba, rg_wx, rg_bx, rg_lambda)
    y_rnn = h_seq * jax.nn.gelu(rgate)
    y_att = mem_attention(q, mem_k, mem_v)
    gates = jax.nn.sigmoid(g.reshape(bn, s, N_BRANCH, D_MODEL))
    merged = (gates[:, :, 0] * (y_conv @ w_br_conv)
              + gates[:, :, 1] * (y_rnn @ w_br_rnn)
              + gates[:, :, 2] * (y_att @ w_br_attn))
    return merged @ w_o, conv_buf_new, rg_buf_new, h_new


def peer_block(xb, wq, keys, u_tab, v_tab):
    t = xb.shape[0]
    q = (xb @ wq).reshape(t, PEER_HEADS, 2, PEER_HALF)
    sc = jnp.einsum('thcd,hckd->thck', q, keys)
    s1, i1 = lax.top_k(sc[:, :, 0], PEER_TOPK)
    s2, i2 = lax.top_k(sc[:, :, 1], PEER_TOPK)
    comb = (s1[..., :, None] + s2[..., None, :]).reshape(t, PEER_HEADS, PEER_TOPK * PEER_TOPK)
    cand = (i1[..., :, None] * PEER_NKEYS + i2[..., None, :]).reshape(t, PEER_HEADS, PEER_TOPK * PEER_TOPK)
    top, pos = lax.top_k(comb, PEER_TOPK)
    e = jnp.take_along_axis(cand, pos, axis=-1)
    gw = jax.nn.softmax(top.astype(jnp.float32), axis=-1).astype(xb.dtype)
    u = jnp.take(u_tab, e, axis=0)
    act = jax.nn.gelu(jnp.einsum('thkd,td->thk', u, xb))
    v = jnp.take(v_tab, e, axis=0)
    return jnp.einsum('thk,thkd->td', gw * act, v)


def peer_ffn(x, wq, keys, u_tab, v_tab):
    shp = x.shape
    xf = x.reshape(-1, D_MODEL)
    n = xf.shape[0]
    pad = (-n) % PEER_BLOCK
    xb = jnp.pad(xf, ((0, pad), (0, 0))).reshape(-1, PEER_BLOCK, D_MODEL)
    yb = lax.map(lambda blk: peer_block(blk, wq, keys, u_tab, v_tab), xb)
    return yb.reshape(-1, D_MODEL)[:n].reshape(shp)


def setup_inputs(seed: int = 0) -> dict:
    key = jax.random.key(seed)
    ks = jax.random.split(key, 40)
    f32 = jnp.float32

    def nrm(k, shape, scale):
        return jax.random.normal(k, shape, f32) * scale

    a0 = jax.random.uniform(ks[19], (DEPTH, RNN_WIDTH), f32, minval=0.9, maxval=0.999)
    s0 = a0 ** (1.0 / RG_C)
    rg_lambda = jnp.log(s0) - jnp.log1p(-s0)
    return {
        "x_prompt": nrm(ks[0], (BATCH, SEQ, D_MODEL), 1.0),
        "x_sample": nrm(ks[1], (DEC_BATCH, DEC_SEQ, D_MODEL), 1.0),
        "mem_prompt": nrm(ks[2], (BATCH, N_MEM, D_MODEL), 1.0),
        "cache_mem_k": nrm(ks[3], (DEPTH, DEC_BATCH, N_MEM, ATTN_HEADS, ATTN_HEAD_DIM), 1.0),
        "cache_mem_v": nrm(ks[4], (DEPTH, DEC_BATCH, N_MEM, ATTN_HEADS, ATTN_HEAD_DIM), 1.0),
        "state_conv_z": nrm(ks[5], (DEPTH, DEC_BATCH, CONV_K - 1, CONV_WIDTH), 1.0),
        "state_rglru_conv": nrm(ks[6], (DEPTH, DEC_BATCH, RNN_CONV_K - 1, RNN_WIDTH), 1.0),
        "state_rglru_h": nrm(ks[7], (DEPTH, DEC_BATCH, RNN_WIDTH), 0.5),
        "w_in": nrm(ks[8], (DEPTH, D_MODEL, IN_WIDTH), D_MODEL ** -0.5),
        "conv_w": nrm(ks[9], (DEPTH, CONV_K, CONV_WIDTH), CONV_K ** -0.5),
        "rg_conv_w": nrm(ks[10], (DEPTH, RNN_CONV_K, RNN_WIDTH), RNN_CONV_K ** -0.5),
        "rg_conv_b": nrm(ks[11], (DEPTH, RNN_WIDTH), 0.01),
        "rg_wa": nrm(ks[12], (DEPTH, RNN_HEADS, RNN_HEAD_DIM, RNN_HEAD_DIM), RNN_HEAD_DIM ** -0.5),
        "rg_ba": nrm(ks[13], (DEPTH, RNN_WIDTH), 0.01),
        "rg_wx": nrm(ks[14], (DEPTH, RNN_HEADS, RNN_HEAD_DIM, RNN_HEAD_DIM), RNN_HEAD_DIM ** -0.5),
        "rg_bx": nrm(ks[15], (DEPTH, RNN_WIDTH), 0.01),
        "rg_lambda": rg_lambda,
        "w_mk": nrm(ks[16], (DEPTH, D_MODEL, ATTN_WIDTH), D_MODEL ** -0.5),
        "w_mv": nrm(ks[17], (DEPTH, D_MODEL, ATTN_WIDTH), D_MODEL ** -0.5 * DN_BETA),
        "w_br_conv": nrm(ks[18], (DEPTH, CONV_WIDTH, D_MODEL), CONV_WIDTH ** -0.5 * DN_BETA),
        "w_br_rnn": nrm(ks[20], (DEPTH, RNN_WIDTH, D_MODEL), RNN_WIDTH ** -0.5 * DN_BETA),
        "w_br_attn": nrm(ks[21], (DEPTH, ATTN_WIDTH, D_MODEL), ATTN_WIDTH ** -0.5 * DN_BETA),
        "w_o": nrm(ks[22], (DEPTH, D_MODEL, D_MODEL), D_MODEL ** -0.5 * DN_BETA),
        "ln1_g": 1.0 + nrm(ks[23], (DEPTH, D_MODEL), 0.01),
        "ln1_b": nrm(ks[24], (DEPTH, D_MODEL), 0.01),
        "peer_wq": nrm(ks[25], (DEPTH, D_MODEL, PEER_HEADS * PEER_QDIM), D_MODEL ** -0.5),
        "peer_keys": nrm(ks[26], (DEPTH, PEER_HEADS, 2, PEER_NKEYS, PEER_HALF), PEER_HALF ** -0.5),
        "peer_u": nrm(ks[27], (DEPTH, PEER_EXPERTS, D_MODEL), D_MODEL ** -0.5),
        "peer_v": nrm(ks[28], (DEPTH, PEER_EXPERTS, D_MODEL), PEER_HEADS ** -0.5 * DN_BETA),
        "ln2_g": 1.0 + nrm(ks[29], (DEPTH, D_MODEL), 0.01),
        "ln2_b": nrm(ks[30], (DEPTH, D_MODEL), 0.01),
    }


def reference(x_prompt, x_sample, mem_prompt, cache_mem_k, cache_mem_v, state_conv_z, state_rglru_conv,
              state_rglru_h, w_in, conv_w, rg_conv_w, rg_conv_b, rg_wa, rg_ba, rg_wx, rg_bx, rg_lambda,
              w_mk, w_mv, w_br_conv, w_br_rnn, w_br_attn, w_o, ln1_g, ln1_b,
              peer_wq, peer_keys, peer_u, peer_v, ln2_g, ln2_b):
    xp, xs = x_prompt, x_sample
    bp = xp.shape[0]
    dt = xp.dtype
    mk_l, mv_l, czp_l, rcp_l, hp_l, czs_l, rcs_l, hs_l = [], [], [], [], [], [], [], []
    for l in range(DEPTH):
        lp = (w_in[l], conv_w[l], rg_conv_w[l], rg_conv_b[l], rg_wa[l], rg_ba[l], rg_wx[l], rg_bx[l],
              rg_lambda[l], w_br_conv[l], w_br_rnn[l], w_br_attn[l], w_o[l])
        mk = (mem_prompt @ w_mk[l]).reshape(bp, N_MEM, ATTN_HEADS, ATTN_HEAD_DIM)
        mv = (mem_prompt @ w_mv[l]).reshape(bp, N_MEM, ATTN_HEADS, ATTN_HEAD_DIM)
        zc = jnp.zeros((bp, CONV_K - 1, CONV_WIDTH), dt)
        zr = jnp.zeros((bp, RNN_CONV_K - 1, RNN_WIDTH), dt)
        zh = jnp.zeros((bp, RNN_WIDTH), dt)
        o_p, cz_p, rc_p, h_p = mixer_sublayer(xp, mk, mv, zc, zr, zh, *lp)
        o_s, cz_s, rc_s, h_s = mixer_sublayer(xs, cache_mem_k[l], cache_mem_v[l], state_conv_z[l],
                                              state_rglru_conv[l], state_rglru_h[l], *lp)
        xp = layernorm(DN_ALPHA * xp + o_p, ln1_g[l], ln1_b[l])
        xs = layernorm(DN_ALPHA * xs + o_s, ln1_g[l], ln1_b[l])
        pp = (peer_wq[l], peer_keys[l], peer_u[l], peer_v[l])
        xp = layernorm(DN_ALPHA * xp + peer_ffn(xp, *pp), ln2_g[l], ln2_b[l])
        xs = layernorm(DN_ALPHA * xs + peer_ffn(xs, *pp), ln2_g[l], ln2_b[l])
        mk_l.append(mk); mv_l.append(mv)
        czp_l.append(cz_p); rcp_l.append(rc_p); hp_l.append(h_p)
        czs_l.append(cz_s); rcs_l.append(rc_s); hs_l.append(h_s)
    mem_k_prompt = jnp.stack(mk_l)
    mem_v_prompt = jnp.stack(mv_l)
    conv_z_prompt = jnp.stack(czp_l)
    rglru_conv_prompt = jnp.stack(rcp_l)
    rglru_h_prompt = jnp.stack(hp_l)
    conv_z_sample = jnp.stack(czs_l)
    rglru_conv_sample = jnp.stack(rcs_l)
    rglru_h_sample = jnp.stack(hs_l)
    return (xp, xs, mem_k_prompt, mem_v_prompt, conv_z_prompt, rglru_conv_prompt, rglru_h_prompt,
            conv_z_sample, rglru_conv_sample, rglru_h_sample)
```

```python
from contextlib import ExitStack
import numpy as np
import concourse.bass as bass
import concourse.mybir as mybir
from concourse.bass_utils import run_bass_kernel_spmd

F32 = mybir.dt.float32
BF16 = mybir.dt.bfloat16
I32 = mybir.dt.int32
U32 = mybir.dt.uint32
U8 = mybir.dt.uint8
ALU = mybir.AluOpType
AF = mybir.ActivationFunctionType
AX = mybir.AxisListType

NCORES = 8
D = 2048
TT = 1043
C0 = 3
CS = 1027
NTOK = 1040
BLKS = [(0, 512), (512, 512), (1024, 19)]
ALPHA = 2.0 ** 0.25
LN_EPS = 1e-5
NEG = -1.0e30


class Res:
    __slots__ = ("w", "rs", "x")

    def __init__(self):
        self.w = None
        self.rs = {}
        self.x = False


class Stream:
    __slots__ = ("name", "sem", "inc", "n")

    def __init__(self, name, sem, inc):
        self.name = name
        self.sem = sem
        self.inc = inc
        self.n = 0


class Sched:
    ENGS = ("pe", "act", "dve", "pool", "sp")

    def __init__(self, nc, stack):
        self.nc = nc
        self.items = {e: [] for e in self.ENGS}
        self.clock = {e: {} for e in self.ENGS}
        self.cstream = {}
        self.all_streams = []
        for e in self.ENGS:
            if e == "sp":
                continue
            s = Stream(e, stack.enter_context(nc.semaphore("c_" + e)), 1)
            self.cstream[e] = s
            self.all_streams.append(s)
        self.dstreams = {}
        self.dcount = {}
        for q, k in (("sp", 8), ("pool", 8), ("act", 2)):
            self.dstreams[q] = [Stream(f"d_{q}{i}", stack.enter_context(nc.semaphore(f"d_{q}{i}")), 16)
                                for i in range(k)]
            self.all_streams += self.dstreams[q]
            self.dcount[q] = 0
        self.nops = 0

    def _collect(self, eng, reads, writes, is_dma=False):
        need = {}

        def add(tok):
            if tok is None:
                return
            s, n, _ = tok
            if need.get(s, (0, None))[0] < n:
                need[s] = (n, tok)

        own = self.cstream.get(eng)
        for r in reads:
            add(r.w)
            if r.x:
                for t in r.rs.values():
                    if t[0] is not own:
                        add(t)
        for w in writes:
            add(w.w)
            for t in w.rs.values():
                if eng == "pe" and (not is_dma) and t[0] is own:
                    continue
                add(t)
        clk = self.clock[eng]
        for s, (n, tok) in need.items():
            if clk.get(s.name, 0) >= n:
                continue
            if eng == "pe" and s is own:
                continue
            self.items[eng].append(("w", s.sem, n * s.inc))
            for k, v in tok[2].items():
                if clk.get(k, 0) < v:
                    clk[k] = v
            clk[s.name] = n

    def _finish(self, eng, stream, fn, reads, writes):
        stream.n += 1
        tok = (stream, stream.n, dict(self.clock[eng]))
        self.items[eng].append(("i", fn, stream.sem, stream.inc))
        for r in reads:
            r.rs[stream] = tok
        for w in writes:
            w.w = tok
            w.rs = {}
        self.nops += 1
        return tok

    def op(self, eng, fn, reads=(), writes=()):
        self._collect(eng, reads, writes)
        return self._finish(eng, self.cstream[eng], fn, reads, writes)

    def dma(self, q, fn, reads=(), writes=()):
        ds = self.dstreams[q]
        st = ds[self.dcount[q] % len(ds)]
        self.dcount[q] += 1
        clk = self.clock[q]
        if st.n > 0 and clk.get(st.name, 0) < st.n:
            self.items[q].append(("w", st.sem, st.n * st.inc))
            clk[st.name] = st.n
        self._collect(q, reads, writes, is_dma=True)
        return self._finish(q, st, fn, reads, writes)

    def wait_all(self, eng, toks):
        clk = self.clock[eng]
        for tok in toks:
            s, n, _ = tok
            if n == 0 or clk.get(s.name, 0) >= n:
                continue
            self.items[eng].append(("w", s.sem, n * s.inc))
            clk[s.name] = n

    def barrier(self):
        toks = [(s, s.n, {}) for s in self.all_streams]
        for e in self.ENGS:
            self.wait_all(e, toks)

    def emit(self, block):
        def run(e, items):
            for it in items:
                if it[0] == "w":
                    e.wait_ge(it[1], it[2])
                else:
                    it[1](e).then_inc(it[2], it[3])

        items = self.items

        @block.sync
        def _(e):
            run(e, items["sp"])

        @block.tensor
        def _(e):
            run(e, items["pe"])

        @block.scalar
        def _(e):
            run(e, items["act"])

        @block.vector
        def _(e):
            run(e, items["dve"])

        @block.gpsimd
        def _(e):
            run(e, items["pool"])


class Buf:
    __slots__ = ("ap", "r", "off")

    def __init__(self, ap, off=None, r=None):
        self.ap = ap
        self.r = r if r is not None else Res()
        self.off = off


class _Stop(Exception):
    pass


KNOB = {"mk": 9, "conv_from": 24, "conv_n": None}
PHASES = ["SETUP", "MK", "B0", "A", "B", "ST", "C", "D", "E", "F1", "F2", "G"]


def build_program(debug=(), stop_after="G"):
    nc = bass.Bass("TRN2", target_bir_lowering=False)

    def din(name, shape, dt=F32):
        return nc.dram_tensor(name, list(shape), dt, kind="ExternalInput").ap()

    def dout(name, shape, dt=F32):
        return nc.dram_tensor(name, list(shape), dt, kind="ExternalOutput").ap()

    i_xT = din("xT", [D, TT])
    i_xprevT = din("xprevT", [D, 1024])
    i_xtok = din("xtok", [NTOK, D])
    i_flag = din("flag", [128, 1])
    i_memT = din("memT", [D, 256])
    i_ck = din("ck", [16, 256, 1024])
    i_cv = din("cv", [16, 256, 1024])
    i_scz = din("scz", [128, 8 * 2 * 16])
    i_src = din("src", [128, 8 * 3 * 16])
    i_sh = din("sh", [128, 8 * 16])
    i_scz_tok = din("scz_tok", [16, 2, 1024])
    i_src_tok = din("src_tok", [16, 3, 1024])
    i_chp = din("chp", [128, 8 * 11])
    i_win = din("win", [96, 128, 16 * 128])
    i_wa = din("wa", [128, 8 * 128])
    i_wx = din("wx", [128, 8 * 128])
    i_wmk = din("wmk", [8, 128, 16 * 128])
    i_wmv = din("wmv", [8, 128, 16 * 128])
    i_wbc = din("wbc", [16, 128, 8 * 128])
    i_wbr = din("wbr", [16, 128, 8 * 128])
    i_wba = din("wba", [16, 128, 8 * 128])
    i_wo = din("wo", [16, 128, 16 * 128])
    i_wq = din("wq", [16, 128, 16 * 128])
    i_keysT = din("keysT", [128, 16 * 128])
    i_ln = din("ln", [4, D])
    i_pu = din("pu", [16384, D])
    i_pv = din("pv", [16384, D])
    i_ident = din("ident", [128, 128])
    i_sel = din("sel", [16, 16 * 128])
    i_iota = din("iota16", [128, 16])
    i_selm = din("selm", [128, 16])

    o_y = dout("y", [NTOK, D])
    o_mk = dout("mk_out", [256, 1024])
    o_mv = dout("mv_out", [256, 1024])
    o_st = dout("st_out", [54, 1024])
    o_czs = dout("czs_out", [16, 2, 1024])
    o_rcs = dout("rcs_out", [16, 3, 1024])
    dbg_out = {}

    vscr = nc.dram_tensor("vscr", [NTOK, D], F32, kind="Internal").ap()
    x1scr = nc.dram_tensor("x1scr", [NTOK, D], F32, kind="Internal").ap()
    puv_bf = nc.dram_tensor("puv_bf", [16384, 2 * D], BF16, kind="Internal").ap()

    st = ExitStack()
    with st:
        S = Sched(nc, st)
        ARENA_BYTES = 207 * 1024
        arena_t = st.enter_context(nc.sbuf_tensor("arena", [128, ARENA_BYTES], U8))
        state = {"off": 0, "lim": ARENA_BYTES}

        def alloc(nbytes):
            off = state["off"]
            state["off"] = off + (nbytes + 63) // 64 * 64
            assert state["off"] <= state["lim"], ("arena overflow", state["off"], state["lim"])
            return off

        def view(off, shape, dt, parts=128):
            es = 2 if dt == BF16 else 4
            n = int(np.prod(shape[1:]))
            v = arena_t[0:shape[0], off:off + n * es].bitcast(dt)
            if len(shape) == 3:
                v = v.rearrange("p (a b) -> p a b", b=shape[2])
            elif len(shape) == 4:
                v = v.rearrange("p (a b c) -> p a b c", b=shape[2], c=shape[3])
            return v

        def new(shape, dt):
            es = 2 if dt == BF16 else 4
            off = alloc(int(np.prod(shape[1:])) * es)
            return Buf(view(off, shape, dt), off)

        def alt(buf, shape, dt):
            return Buf(view(buf.off, shape, dt), buf.off, buf.r)

        banks = [Buf(st.enter_context(nc.psum_tensor(f"bank{i}", [128, 512], F32))[:]) for i in range(8)]
        for b_ in banks:
            b_.r.x = True
        bstate = {"i": 0, "reserved": set()}

        def pb():
            while True:
                i = bstate["i"] % 8
                bstate["i"] += 1
                if i not in bstate["reserved"]:
                    return banks[i]

        def mm(out, lhsT, rhs, start, stop, reads, writes):
            S.op("pe", lambda e: e.matmul(out, lhsT, rhs, start=start, stop=stop), reads, writes)

        def act(out, in_, func, reads, writes, bias=0.0, scale=1.0, accum_out=None):
            if accum_out is None:
                S.op("act", lambda e: e.activation(out=out, in_=in_, func=func, bias=bias, scale=scale),
                     reads, writes)
            else:
                S.op("act", lambda e: e.activation(out=out, in_=in_, func=func, bias=bias, scale=scale,
                                                   accum_out=accum_out), reads, writes)

        def tt(eng, out, in0, in1, op, reads, writes):
            S.op(eng, lambda e: e.tensor_tensor(out=out, in0=in0, in1=in1, op=op), reads, writes)

        def ts(eng, out, in0, s1, s2, op0, op1, reads, writes):
            if op1 is None:
                S.op(eng, lambda e: e.tensor_scalar(out=out, in0=in0, scalar1=s1, scalar2=None, op0=op0),
                     reads, writes)
            else:
                S.op(eng, lambda e: e.tensor_scalar(out=out, in0=in0, scalar1=s1, scalar2=s2, op0=op0, op1=op1),
                     reads, writes)

        def stt(out, in0, scalar, in1, op0, op1, reads, writes, accum_out=None):
            if accum_out is None:
                S.op("dve", lambda e: e.scalar_tensor_tensor(out=out, in0=in0, scalar=scalar, in1=in1,
                                                             op0=op0, op1=op1), reads, writes)
            else:
                S.op("dve", lambda e: e.scalar_tensor_tensor(out=out, in0=in0, scalar=scalar, in1=in1,
                                                             op0=op0, op1=op1, accum_out=accum_out),
                     reads, writes)

        def cp(eng, out, in_, reads, writes):
            if eng == "act":
                S.op("act", lambda e: e.activation(out=out, in_=in_, func=AF.Copy), reads, writes)
            else:
                S.op(eng, lambda e: e.tensor_copy(out=out, in_=in_), reads, writes)

        def memset(eng, ap, val, writes):
            S.op(eng, lambda e: e.memset(ap, val), (), writes)

        def dma(q, out, in_, reads, writes):
            return S.dma(q, lambda e: e.dma_start(out=out, in_=in_), reads, writes)

        def dbg(name, buf, shape):
            if name in debug:
                o = dout("dbg_" + name, shape, buf.ap.dtype)
                dbg_out[name] = dma("sp", o, buf.ap, [buf.r], [])

        out_toks = []

        ident_f = new([128, 128], F32)
        ident_b = new([128, 128], BF16)
        ones_b = new([128, 128], BF16)
        sel_b = new([16, 16, 128], BF16)
        iota16 = new([128, 16], F32)
        chp = new([128, 8, 11], F32)
        nba = new([128, 8], F32)
        nbx = new([128, 8], F32)
        cA = new([128, 8], F32)
        c2A = new([128, 8], F32)
        flag = new([128, 1], F32)
        scz = new([128, 8, 2, 16], F32)
        src = new([128, 8, 3, 16], F32)
        sh = new([128, 8, 16], F32)
        wa_b = new([128, 8, 128], BF16)
        wx_b = new([128, 8, 128], BF16)
        hmid = new([128, 8], F32)
        rxhist = new([128, 8, 3], F32)
        ST = new([128, 8, 64], F32)
        mkT = new([128, 8, 256], BF16)
        mv_b = new([128, 2, 1024], BF16)

        R1 = alloc(16 * TT * 2)
        R2 = alloc(16 * TT * 2)
        R3 = alloc(8 * TT * 2)
        R5 = alloc(16 * TT * 2)
        xT = Buf(view(R1, [128, 16, TT], BF16))
        ycT = Buf(view(R2, [128, 8, TT], BF16))
        yrT = Buf(view(R2 + 8 * TT * 2, [128, 8, TT], BF16))
        yaT = Buf(view(R3, [128, 8, TT], BF16))
        xprevT = Buf(view(R5, [128, 16, 1024], BF16))
        qT = Buf(view(R5, [128, 8, TT], BF16))
        mergedT = Buf(view(R5, [128, 16, TT], BF16))
        PH2 = state["off"]

        slabs = [new([128, 16 * 128], BF16) for _ in range(6)]
        sstate = {"i": 0}

        conv = {"i": 0, "r": Res()}
        CONV_ROWS = 128
        NCONV = 2 * 16384 // CONV_ROWS

        def conv_step(n):
            for _ in range(n):
                i = conv["i"]
                if i >= NCONV:
                    return
                conv["i"] += 1
                src_t, c0_ = (i_pu, 0) if i % 2 == 0 else (i_pv, D)
                r0 = (i // 2) * CONV_ROWS
                dma("pool", puv_bf[r0:r0 + CONV_ROWS, c0_:c0_ + D], src_t[r0:r0 + CONV_ROWS, :], [], [])

        def load_slab(src_ap, kc):
            sb = slabs[sstate["i"] % len(slabs)]
            sstate["i"] += 1
            dma("pool", sb.ap[:, 0:kc * 128], src_ap, [], [sb.r])
            if sstate["i"] > KNOB["conv_from"]:
                if KNOB["conv_n"] is None:
                    conv_step(2 if sstate["i"] % 3 == 0 else 1)
                else:
                    conv_step(KNOB["conv_n"])
            return sb, sb.ap[:, 0:kc * 128].rearrange("p (k c) -> p k c", c=128)

        s4 = [new([128, TT + 5], F32) for _ in range(5)]
        s2 = [new([128, 512], F32) for _ in range(6)]
        hb = [new([128, 512], F32) for _ in range(2)]
        s2b = [new([128, 512], BF16) for _ in range(4)]
        st4 = {"i": 0}
        st2 = {"i": 0}
        st2b = {"i": 0}

        def t4():
            b = s4[st4["i"] % len(s4)]
            st4["i"] += 1
            return b

        def t2():
            b = s2[st2["i"] % len(s2)]
            st2["i"] += 1
            return b

        def t2b():
            b = s2b[st2b["i"] % len(s2b)]
            st2b["i"] += 1
            return b

        dma("sp", ident_f.ap, i_ident, [], [ident_f.r])
        dma("sp", iota16.ap, i_iota, [], [iota16.r])
        dma("sp", chp.ap, i_chp.rearrange("p (a b) -> p a b", b=11), [], [chp.r])
        dma("sp", flag.ap, i_flag, [], [flag.r])
        dma("sp", scz.ap, i_scz.rearrange("p (a b c) -> p a b c", b=2, c=16), [], [scz.r])
        dma("sp", src.ap, i_src.rearrange("p (a b c) -> p a b c", b=3, c=16), [], [src.r])
        dma("sp", sh.ap, i_sh.rearrange("p (a b) -> p a b", b=16), [], [sh.r])
        dma("pool", sel_b.ap, i_sel.rearrange("p (a b) -> p a b", b=128), [], [sel_b.r])
        dma("pool", wa_b.ap, i_wa.rearrange("p (a b) -> p a b", b=128), [], [wa_b.r])
        dma("pool", wx_b.ap, i_wx.rearrange("p (a b) -> p a b", b=128), [], [wx_b.r])
        for g in range(4):
            dma("pool", xprevT.ap[:, 4 * g:4 * g + 4, :],
                i_xprevT[512 * g:512 * g + 512, :].rearrange("(k p) n -> p k n", p=128), [], [xprevT.r])
        for g in range(4):
            dma("pool", xT.ap[:, 4 * g:4 * g + 4, :],
                i_xT[512 * g:512 * g + 512, :].rearrange("(k p) n -> p k n", p=128), [], [xT.r])
        cp("dve", ident_b.ap, ident_f.ap, [ident_f.r], [ident_b.r])
        memset("dve", ones_b.ap, 1.0, [ones_b.r])
        memset("dve", ST.ap, 0.0, [ST.r])
        ts("dve", nba.ap, chp.ap[:, :, 8], -1.0, None, ALU.mult, None, [chp.r], [nba.r])
        ts("dve", nbx.ap, chp.ap[:, :, 9], -1.0, None, ALU.mult, None, [chp.r], [nbx.r])
        tmpc = new([128, 8], F32)
        act(tmpc.ap, chp.ap[:, :, 10], AF.Exp, [chp.r], [tmpc.r], scale=-1.0)
        act(tmpc.ap, tmpc.ap, AF.Ln, [tmpc.r], [tmpc.r], bias=1.0)
        ts("dve", cA.ap, tmpc.ap, -8.0, None, ALU.mult, None, [tmpc.r], [cA.r])
        ts("dve", c2A.ap, tmpc.ap, -16.0, None, ALU.mult, None, [tmpc.r], [c2A.r])

        def phase(name):
            if PHASES.index(name) > PHASES.index(stop_after):
                raise _Stop()

        def body():
            def sigmoid_from_psum(ps, nbias_ap, n, out_buf):
                act(out_buf.ap[:, 0:n], ps.ap[:, 0:n], AF.Exp, [ps.r, nba.r, nbx.r], [out_buf.r], bias=nbias_ap, scale=-1.0)
                act(out_buf.ap[:, 0:n], out_buf.ap[:, 0:n], AF.Ln, [out_buf.r], [out_buf.r], bias=1.0)
                act(out_buf.ap[:, 0:n], out_buf.ap[:, 0:n], AF.Exp, [out_buf.r], [out_buf.r], scale=-1.0)

            def rg_block(j, xr_ap, xr_res, n, h_init, h_out_buf, h_init_res=None):
                xb = t2b()
                cp("act", xb.ap[:, 0:n], xr_ap, [xr_res], [xb.r])
                pr = pb()
                mm(pr.ap[:, 0:n], wa_b.ap[:, j, :], xb.ap[:, 0:n], True, True, [wa_b.r, xb.r], [pr.r])
                pi = pb()
                mm(pi.ap[:, 0:n], wx_b.ap[:, j, :], xb.ap[:, 0:n], True, True, [wx_b.r, xb.r], [pi.r])
                r = t2()
                sigmoid_from_psum(pr, nba.ap[:, j:j + 1], n, r)
                ig = t2()
                sigmoid_from_psum(pi, nbx.ap[:, j:j + 1], n, ig)
                a2 = t2()
                act(a2.ap[:, 0:n], r.ap[:, 0:n], AF.Exp, [r.r, c2A.r], [a2.r], scale=c2A.ap[:, j:j + 1])
                a = t2()
                act(a.ap[:, 0:n], r.ap[:, 0:n], AF.Exp, [r.r, cA.r], [a.r], scale=cA.ap[:, j:j + 1])
                ts("dve", a2.ap[:, 0:n], a2.ap[:, 0:n], -1.0, 1.0, ALU.mult, ALU.add, [a2.r], [a2.r])
                ts("dve", a2.ap[:, 0:n], a2.ap[:, 0:n], 1e-30, None, ALU.max, None, [a2.r], [a2.r])
                act(a2.ap[:, 0:n], a2.ap[:, 0:n], AF.Ln, [a2.r], [a2.r])
                act(a2.ap[:, 0:n], a2.ap[:, 0:n], AF.Exp, [a2.r], [a2.r], scale=0.5)
                tt("dve", ig.ap[:, 0:n], ig.ap[:, 0:n], xr_ap, ALU.mult, [ig.r, xr_res], [ig.r])
                tt("dve", ig.ap[:, 0:n], ig.ap[:, 0:n], a2.ap[:, 0:n], ALU.mult, [ig.r, a2.r], [ig.r])
                S.op("dve", lambda e: e.tensor_tensor_scan(out=h_out_buf.ap[:, 0:n], data0=a.ap[:, 0:n],
                                                           data1=ig.ap[:, 0:n], initial=h_init,
                                                           op0=ALU.mult, op1=ALU.add),
                     [a.r, ig.r, hmid.r] + ([h_init_res] if h_init_res is not None else []), [h_out_buf.r])

            def rg_gates(j, xr, blocks):
                nb = len(blocks)
                xb = [t2b() for _ in range(nb)]
                for b, (c0, n) in enumerate(blocks):
                    cp("act", xb[b].ap[:, 0:n], xr.ap[:, c0:c0 + n], [xr.r], [xb[b].r])
                pr = [pb() for _ in range(nb)]
                pi = [pb() for _ in range(nb)]
                for b, (c0, n) in enumerate(blocks):
                    mm(pr[b].ap[:, 0:n], wa_b.ap[:, j, :], xb[b].ap[:, 0:n], True, True, [wa_b.r, xb[b].r], [pr[b].r])
                    mm(pi[b].ap[:, 0:n], wx_b.ap[:, j, :], xb[b].ap[:, 0:n], True, True, [wx_b.r, xb[b].r], [pi[b].r])
                r = [t2() for _ in range(nb)]
                ig = [t2() for _ in range(nb)]
                a2 = [t2() for _ in range(nb)]
                a = [t2() for _ in range(nb)]
                for b, (c0, n) in enumerate(blocks):
                    act(r[b].ap[:, 0:n], pr[b].ap[:, 0:n], AF.Exp, [pr[b].r, nba.r], [r[b].r], bias=nba.ap[:, j:j + 1], scale=-1.0)
                    act(ig[b].ap[:, 0:n], pi[b].ap[:, 0:n], AF.Exp, [pi[b].r, nbx.r], [ig[b].r], bias=nbx.ap[:, j:j + 1], scale=-1.0)
                for b, (c0, n) in enumerate(blocks):
                    act(r[b].ap[:, 0:n], r[b].ap[:, 0:n], AF.Ln, [r[b].r], [r[b].r], bias=1.0)
                    act(ig[b].ap[:, 0:n], ig[b].ap[:, 0:n], AF.Ln, [ig[b].r], [ig[b].r], bias=1.0)
                for b, (c0, n) in enumerate(blocks):
                    act(r[b].ap[:, 0:n], r[b].ap[:, 0:n], AF.Exp, [r[b].r], [r[b].r], scale=-1.0)
                    act(ig[b].ap[:, 0:n], ig[b].ap[:, 0:n], AF.Exp, [ig[b].r], [ig[b].r], scale=-1.0)
                for b, (c0, n) in enumerate(blocks):
                    act(a2[b].ap[:, 0:n], r[b].ap[:, 0:n], AF.Exp, [r[b].r, c2A.r], [a2[b].r], scale=c2A.ap[:, j:j + 1])
                    act(a[b].ap[:, 0:n], r[b].ap[:, 0:n], AF.Exp, [r[b].r, cA.r], [a[b].r], scale=cA.ap[:, j:j + 1])
                for b, (c0, n) in enumerate(blocks):
                    ts("dve", a2[b].ap[:, 0:n], a2[b].ap[:, 0:n], -1.0, 1.0, ALU.mult, ALU.add, [a2[b].r], [a2[b].r])
                    tt("dve", ig[b].ap[:, 0:n], ig[b].ap[:, 0:n], xr.ap[:, c0:c0 + n], ALU.mult, [ig[b].r, xr.r], [ig[b].r])
                for b, (c0, n) in enumerate(blocks):
                    ts("dve", a2[b].ap[:, 0:n], a2[b].ap[:, 0:n], 1e-30, None, ALU.max, None, [a2[b].r], [a2[b].r])
                for b, (c0, n) in enumerate(blocks):
                    act(a2[b].ap[:, 0:n], a2[b].ap[:, 0:n], AF.Ln, [a2[b].r], [a2[b].r])
                for b, (c0, n) in enumerate(blocks):
                    act(a2[b].ap[:, 0:n], a2[b].ap[:, 0:n], AF.Exp, [a2[b].r], [a2[b].r], scale=0.5)
                for b, (c0, n) in enumerate(blocks):
                    tt("dve", ig[b].ap[:, 0:n], ig[b].ap[:, 0:n], a2[b].ap[:, 0:n], ALU.mult, [ig[b].r, a2[b].r], [ig[b].r])
                return a, ig

            def scan(a_b, u_b, n, h_init, h_out, extra_reads):
                S.op("dve", lambda e: e.tensor_tensor_scan(out=h_out.ap[:, 0:n], data0=a_b.ap[:, 0:n],
                                                           data1=u_b.ap[:, 0:n], initial=h_init,
                                                           op0=ALU.mult, op1=ALU.add),
                     [a_b.r, u_b.r] + extra_reads, [h_out.r])

            def dwconv4(j, rxf, n, xr):
                w = lambda k: chp.ap[:, j, 3 + k:4 + k]
                ts("dve", xr.ap[:, 0:n], rxf.ap[:, 3:3 + n], w(3), chp.ap[:, j, 7:8], ALU.mult, ALU.add,
                   [rxf.r, chp.r], [xr.r])
                for k in range(3):
                    stt(xr.ap[:, 0:n], rxf.ap[:, k:k + n], w(k), xr.ap[:, 0:n], ALU.mult, ALU.add,
                        [rxf.r, chp.r, xr.r], [xr.r])

            phase("MK")
            memT = Buf(view(R2, [128, 16, 256], BF16))
            for g in range(4):
                dma("pool", memT.ap[:, 4 * g:4 * g + 4, :],
                    i_memT[512 * g:512 * g + 512, :].rearrange("(k p) n -> p k n", p=128), [], [memT.r])
            mk_tok = Buf(view(R3, [128, 2, 1024], F32))
            mv_tok = Buf(view(R3 + 8192, [128, 2, 1024], F32))
            for c8 in range(8):
                if KNOB["mk"] < 1:
                    break
                sb, w = load_slab(i_wmk[c8], 16)
                if KNOB["mk"] < 2:
                    continue
                ps = pb()
                for k in range(16):
                    mm(ps.ap[:, 0:256], w[:, k, :], memT.ap[:, k, :], k == 0, k == 15, [sb.r, memT.r], [ps.r])
                if KNOB["mk"] < 3:
                    continue
                cp("act", mkT.ap[:, c8, :], ps.ap[:, 0:256], [ps.r], [mkT.r])
                if KNOB["mk"] < 4:
                    continue
                ps2 = pb()
                for mc in range(2):
                    for k in range(16):
                        mm(ps2.ap[:, mc * 128:mc * 128 + 128], memT.ap[:, k, mc * 128:mc * 128 + 128], w[:, k, :],
                           k == 0, k == 15, [sb.r, memT.r], [ps2.r])
                if KNOB["mk"] < 5:
                    continue
                cp("dve", mk_tok.ap[:, :, c8 * 128:c8 * 128 + 128],
                   ps2.ap[:, 0:256].rearrange("p (a b) -> p a b", b=128), [ps2.r], [mk_tok.r])
            if KNOB["mk"] < 6:
                raise _Stop()
            out_toks.append(dma("sp", o_mk.rearrange("(c p) n -> p c n", p=128), mk_tok.ap, [mk_tok.r], []))
            if KNOB["mk"] < 7:
                raise _Stop()
            for c8 in range(8):
                sb, w = load_slab(i_wmv[c8], 16)
                ps2 = pb()
                for mc in range(2):
                    for k in range(16):
                        mm(ps2.ap[:, mc * 128:mc * 128 + 128], memT.ap[:, k, mc * 128:mc * 128 + 128], w[:, k, :],
                           k == 0, k == 15, [sb.r, memT.r], [ps2.r])
                cp("dve", mv_tok.ap[:, :, c8 * 128:c8 * 128 + 128],
                   ps2.ap[:, 0:256].rearrange("p (a b) -> p a b", b=128), [ps2.r], [mv_tok.r])
                if KNOB["mk"] < 8:
                    continue
                for mc in range(2):
                    cp("act", mv_b.ap[:, mc, c8 * 128:c8 * 128 + 128], mv_tok.ap[:, mc, c8 * 128:c8 * 128 + 128],
                       [mv_tok.r], [mv_b.r])
            if KNOB["mk"] < 9:
                raise _Stop()
            out_toks.append(dma("sp", o_mv.rearrange("(c p) n -> p c n", p=128), mv_tok.ap, [mv_tok.r], []))

            phase("B0")
            S.barrier()
            s2_base = list(s2)
            s2b_base = list(s2b)
            s2[:] = s2_base + [Buf(view(R2 + i * 2048, [128, 512], F32), R2 + i * 2048) for i in range(10)]
            s2b[:] = s2b_base + [Buf(view(R2 + 20480 + i * 1024, [128, 512], BF16), R2 + 20480 + i * 1024) for i in range(4)]
            for j in range(8):
                sb, w = load_slab(i_win[24 + j], 16)
                rxf = t4()
                memset("dve", rxf.ap[:, 0:3], 0.0, [rxf.r])
                for b in range(2):
                    ps = pb()
                    for k in range(16):
                        mm(ps.ap, w[:, k, :], xprevT.ap[:, k, b * 512:b * 512 + 512], k == 0, k == 15,
                           [sb.r, xprevT.r], [ps.r])
                    cp("act", rxf.ap[:, 3 + b * 512:3 + b * 512 + 512], ps.ap, [ps.r], [rxf.r])
                cp("dve", rxhist.ap[:, j, :], rxf.ap[:, 1024:1027], [rxf.r], [rxhist.r])
                xr = t4()
                dwconv4(j, rxf, 1024, xr)
                h0 = hb[0]
                h1 = hb[1]
                a_, u_ = rg_gates(j, xr, [(0, 512), (512, 512)])
                scan(a_[0], u_[0], 512, 0.0, h0, [])
                scan(a_[1], u_[1], 512, h0.ap[:, 511:512], h1, [h0.r])
                ts("dve", hmid.ap[:, j:j + 1], h1.ap[:, 511:512], flag.ap[:, 0:1], None, ALU.mult, None,
                   [h1.r, flag.r], [hmid.r])
            dbg("hmid", hmid, [128, 8])

            S.barrier()
            s2[:] = s2_base + [Buf(view(R5 + i * 2048, [128, 512], F32), R5 + i * 2048) for i in range(10)]
            s2b[:] = s2b_base + [Buf(view(R5 + 20480 + i * 1024, [128, 512], BF16), R5 + 20480 + i * 1024) for i in range(4)]

            phase("A")
            def zchunk(cidx, consume):
                sb, w = load_slab(i_win[cidx], 16)
                for bi, (c0, n) in enumerate(BLKS):
                    ps = pb()
                    for k in range(16):
                        mm(ps.ap[:, 0:n], w[:, k, :], xT.ap[:, k, c0:c0 + n], k == 0, k == 15, [sb.r, xT.r], [ps.r])
                    consume(bi, c0, n, ps)

            for j in range(8):
                ccs = t4()
                cz = t4()
                cbs = t4()
                cy = t4()
                zchunk(8 + j, lambda bi, c0, n, ps: cp("act", ccs.ap[:, c0:c0 + n], ps.ap[:, 0:n], [ps.r], [ccs.r]))
                zchunk(16 + j, lambda bi, c0, n, ps: tt("dve", cz.ap[:, c0:c0 + n], ccs.ap[:, c0:c0 + n],
                                                         ps.ap[:, 0:n], ALU.mult, [ccs.r, ps.r], [cz.r]))
                zchunk(j, lambda bi, c0, n, ps: cp("act", cbs.ap[:, c0:c0 + n], ps.ap[:, 0:n], [ps.r], [cbs.r]))
                w = lambda k: chp.ap[:, j, k:k + 1]
                memset("dve", cy.ap[:, 0:2], 0.0, [cy.r])
                ts("dve", cy.ap[:, 2:CS], cz.ap[:, 2:CS], w(2), None, ALU.mult, None, [cz.r, chp.r], [cy.r])
                stt(cy.ap[:, 2:CS], cz.ap[:, 1:CS - 1], w(1), cy.ap[:, 2:CS], ALU.mult, ALU.add, [cz.r, chp.r, cy.r], [cy.r])
                stt(cy.ap[:, 2:CS], cz.ap[:, 0:CS - 2], w(0), cy.ap[:, 2:CS], ALU.mult, ALU.add, [cz.r, chp.r, cy.r], [cy.r])
                ts("dve", cy.ap[:, CS:TT], cz.ap[:, CS:TT], w(2), None, ALU.mult, None, [cz.r, chp.r], [cy.r])
                stt(cy.ap[:, CS:TT], scz.ap[:, j, 1, :], w(1), cy.ap[:, CS:TT], ALU.mult, ALU.add, [scz.r, chp.r, cy.r], [cy.r])
                stt(cy.ap[:, CS:TT], scz.ap[:, j, 0, :], w(0), cy.ap[:, CS:TT], ALU.mult, ALU.add, [scz.r, chp.r, cy.r], [cy.r])
                tt("dve", ycT.ap[:, j, :], cbs.ap[:, 0:TT], cy.ap[:, 0:TT], ALU.mult, [cbs.r, cy.r], [ycT.r])
                cp("act", ST.ap[:, j, 0:2], cz.ap[:, CS - 2:CS], [cz.r], [ST.r])
                cp("act", ST.ap[:, j, 6:22], cz.ap[:, CS:TT], [cz.r], [ST.r])
            dbg("ycT", ycT, [128, 8, TT])

            phase("B")
            for j in range(8):
                rxf = t4()
                cp("dve", rxf.ap[:, 0:3], rxhist.ap[:, j, :], [rxhist.r], [rxf.r])
                rxall = t4()
                zchunk(24 + j, lambda bi, c0, n, ps: cp("act", rxall.ap[:, c0:c0 + n], ps.ap[:, 0:n], [ps.r], [rxall.r]))
                cp("dve", rxf.ap[:, 3:3 + 1024], rxall.ap[:, C0:CS], [rxall.r], [rxf.r])
                xr = t4()
                dwconv4(j, rxf, 1024, xr)
                wk = lambda k: chp.ap[:, j, 3 + k:4 + k]
                ts("dve", xr.ap[:, 1024:1040], rxall.ap[:, CS:TT], wk(3), chp.ap[:, j, 7:8], ALU.mult, ALU.add,
                   [rxall.r, chp.r], [xr.r])
                for k in range(3):
                    stt(xr.ap[:, 1024:1040], src.ap[:, j, k, :], wk(k), xr.ap[:, 1024:1040], ALU.mult, ALU.add,
                        [src.r, chp.r, xr.r], [xr.r])
                gl = t4()

                def gelu_consume(bi, c0, n, ps):
                    x = t2()
                    cp("act", x.ap[:, 0:n], ps.ap[:, 0:n], [ps.r], [x.r])
                    p = t2()
                    tt("pool", p.ap[:, 0:n], x.ap[:, 0:n], x.ap[:, 0:n], ALU.mult, [x.r], [p.r])
                    ts("pool", p.ap[:, 0:n], p.ap[:, 0:n], 0.044715, 1.0, ALU.mult, ALU.add, [p.r], [p.r])
                    tt("pool", p.ap[:, 0:n], p.ap[:, 0:n], x.ap[:, 0:n], ALU.mult, [p.r, x.r], [p.r])
                    act(p.ap[:, 0:n], p.ap[:, 0:n], AF.Exp, [p.r], [p.r], scale=-1.5957691216057308)
                    act(p.ap[:, 0:n], p.ap[:, 0:n], AF.Ln, [p.r], [p.r], bias=1.0)
                    act(p.ap[:, 0:n], p.ap[:, 0:n], AF.Exp, [p.r], [p.r], scale=-1.0)
                    tt("dve", gl.ap[:, c0:c0 + n], p.ap[:, 0:n], x.ap[:, 0:n], ALU.mult, [p.r, x.r], [gl.r])

                zchunk(32 + j, gelu_consume)
                h0 = hb[0]
                h1 = hb[1]
                a_, u_ = rg_gates(j, xr, [(0, 512), (512, 512), (1024, 16)])
                scan(a_[0], u_[0], 512, hmid.ap[:, j:j + 1], h0, [hmid.r])
                tt("dve", yrT.ap[:, j, C0:C0 + 512], h0.ap[:, 0:512], gl.ap[:, C0:C0 + 512], ALU.mult,
                   [h0.r, gl.r], [yrT.r])
                scan(a_[1], u_[1], 512, h0.ap[:, 511:512], h1, [h0.r])
                tt("dve", yrT.ap[:, j, C0 + 512:CS], h1.ap[:, 0:512], gl.ap[:, C0 + 512:CS], ALU.mult,
                   [h1.r, gl.r], [yrT.r])
                cp("act", ST.ap[:, j, 5:6], h1.ap[:, 511:512], [h1.r], [ST.r])
                cp("act", ST.ap[:, j, 2:5], rxall.ap[:, CS - 3:CS], [rxall.r], [ST.r])
                cp("act", ST.ap[:, j, 22:38], rxall.ap[:, CS:TT], [rxall.r], [ST.r])
                hs = t2()
                tt("dve", hs.ap[:, 0:16], a_[2].ap[:, 0:16], sh.ap[:, j, :], ALU.mult, [a_[2].r, sh.r], [hs.r])
                tt("dve", hs.ap[:, 0:16], hs.ap[:, 0:16], u_[2].ap[:, 0:16], ALU.add, [hs.r, u_[2].r], [hs.r])
                cp("act", ST.ap[:, j, 38:54], hs.ap[:, 0:16], [hs.r], [ST.r])
                tt("dve", yrT.ap[:, j, CS:TT], hs.ap[:, 0:16], gl.ap[:, CS:TT], ALU.mult, [hs.r, gl.r], [yrT.r])
                memset("dve", yrT.ap[:, j, 0:C0], 0.0, [yrT.r])
            dbg("yrT", yrT, [128, 8, TT])

            phase("ST")
            stp = [pb(), pb()]
            for j in range(8):
                b = stp[j // 4]
                S.op("pe", lambda e, b=b, j=j: e.transpose(b.ap[0:54, (j % 4) * 128:(j % 4) * 128 + 128],
                                                            ST.ap[:, j, 0:54], ident_f.ap),
                     [ST.r, ident_f.r], [b.r])
            st_sb = alt(t4(), [54, 1024], F32)
            for hh in range(2):
                cp("dve", st_sb.ap[:, hh * 512:hh * 512 + 512], stp[hh].ap[0:54, :], [stp[hh].r], [st_sb.r])
            out_toks.append(dma("sp", o_st, st_sb.ap, [st_sb.r], []))
            out_toks.append(dma("sp", o_czs[:, 0, :], i_scz_tok[:, 1, :], [], []))
            out_toks.append(dma("sp", o_czs[:, 1, :], st_sb.ap[6:22, :], [st_sb.r], []))
            out_toks.append(dma("sp", o_rcs[:, 0:2, :], i_src_tok[:, 1:3, :], [], []))
            out_toks.append(dma("sp", o_rcs[:, 2, :], st_sb.ap[22:38, :], [st_sb.r], []))

            phase("C")
            S.barrier()
            s2[:] = s2_base
            s2b[:] = s2b_base
            for j in range(8):
                zchunk(40 + j, lambda bi, c0, n, ps, j=j: cp("act", qT.ap[:, j, c0:c0 + n], ps.ap[:, 0:n], [ps.r], [qT.r]))
            qs_b = new([16, 1024], BF16)
            for half in range(2):
                ps = pb()
                for c4 in range(4):
                    sb, w = load_slab(i_win[40 + half * 4 + c4], 16)
                    for k in range(16):
                        mm(ps.ap[0:16, c4 * 128:c4 * 128 + 128], xT.ap[:, k, CS:TT], w[:, k, :], k == 0, k == 15,
                           [sb.r, xT.r], [ps.r])
                cp("act", qs_b.ap[:, half * 512:half * 512 + 512], ps.ap[0:16, :], [ps.r], [qs_b.r])
            for h in range(4):
                for (c0, n) in BLKS:
                    pTs = []
                    for kc in range(2):
                        ps = pb()
                        for dc in range(2):
                            mm(ps.ap[:, 0:n], mkT.ap[:, h * 2 + dc, kc * 128:kc * 128 + 128], qT.ap[:, h * 2 + dc, c0:c0 + n],
                               dc == 0, dc == 1, [mkT.r, qT.r], [ps.r])
                        p = t2b()
                        act(p.ap[:, 0:n], ps.ap[:, 0:n], AF.Exp, [ps.r], [p.r], scale=1.0 / 16.0)
                        pTs.append(p)
                    pd = pb()
                    for kc in range(2):
                        mm(pd.ap[:, 0:n], ones_b.ap, pTs[kc].ap[:, 0:n], kc == 0, kc == 1, [ones_b.r, pTs[kc].r], [pd.r])
                    rec = t2()
                    act(rec.ap[:, 0:n], pd.ap[:, 0:n], AF.Ln, [pd.r], [rec.r])
                    act(rec.ap[:, 0:n], rec.ap[:, 0:n], AF.Exp, [rec.r], [rec.r], scale=-1.0)
                    for dc in range(2):
                        po = pb()
                        for kc in range(2):
                            mm(po.ap[:, 0:n], mv_b.ap[:, kc, h * 256 + dc * 128:h * 256 + dc * 128 + 128], pTs[kc].ap[:, 0:n],
                               kc == 0, kc == 1, [mv_b.r, pTs[kc].r], [po.r])
                        tt("dve", yaT.ap[:, h * 2 + dc, c0:c0 + n], po.ap[:, 0:n], rec.ap[:, 0:n], ALU.mult,
                           [po.r, rec.r], [yaT.r])
            yrow = new([1, 1024], BF16)
            pys = banks[7]
            bstate["reserved"].add(7)
            for t in range(16):
                vb = alt(t4(), [128, 2, 1024], BF16)
                dma("pool", vb.ap, i_cv[t].rearrange("(c p) n -> p c n", p=128), [], [vb.r])
                qb = [pb(), pb()]
                for half in range(2):
                    mm(qb[half].ap, sel_b.ap[:, t, :], qs_b.ap[:, half * 512:half * 512 + 512], True, True,
                       [sel_b.r, qs_b.r], [qb[half].r])
                sc = t2()
                for c in range(2):
                    kb = t4()
                    dma("sp", kb.ap[:, 0:1024], i_ck[t, c * 128:c * 128 + 128, :], [], [kb.r])
                    prod = t4()
                    for half in range(2):
                        tt("dve", prod.ap[:, half * 512:half * 512 + 512], kb.ap[:, half * 512:half * 512 + 512],
                           qb[half].ap, ALU.mult, [kb.r, qb[half].r], [prod.r])
                    S.op("dve", lambda e, c=c, sc=sc, prod=prod: e.tensor_reduce(
                        out=sc.ap[:, c * 4:c * 4 + 4], in_=prod.ap[:, 0:1024].rearrange("p (h d) -> p h d", d=256),
                        axis=AX.X, op=ALU.add), [prod.r], [sc.r])
                pp = t2b()
                act(pp.ap[:, 0:8], sc.ap[:, 0:8], AF.Exp, [sc.r], [pp.r], scale=1.0 / 16.0)
                po = [pb(), pb(), pb()]
                for h in range(4):
                    for c in range(2):
                        mm(po[h // 2].ap[0:1, (h % 2) * 256:(h % 2) * 256 + 256], pp.ap[:, c * 4 + h:c * 4 + h + 1],
                           vb.ap[:, c, h * 256:h * 256 + 256], c == 0, c == 1, [pp.r, vb.r], [po[h // 2].r])
                        mm(po[2].ap[0:1, h:h + 1], pp.ap[:, c * 4 + h:c * 4 + h + 1], ones_b.ap[:, 0:1],
                           c == 0, c == 1, [pp.r, ones_b.r], [po[2].r])
                rec = t2()
                S.op("dve", lambda e, rec=rec, po=po: e.reciprocal(out=rec.ap[0:1, 0:4], in_=po[2].ap[0:1, 0:4]),
                     [po[2].r], [rec.r])
                for half in range(2):
                    S.op("dve", lambda e, half=half, rec=rec, po=po: e.tensor_tensor(
                        out=yrow.ap[0:1, half * 512:half * 512 + 512].rearrange("p (h d) -> p h d", d=256),
                        in0=po[half].ap[0:1, :].rearrange("p (h d) -> p h d", d=256),
                        in1=bass.AP(rec.ap.tensor, rec.ap[0:1, half * 2:half * 2 + 2].offset,
                                    [list(rec.ap[0:1, 0:2].ap[0]), [1, 2], [0, 256]]),
                        op=ALU.mult), [po[half].r, rec.r], [yrow.r])
                for j in range(8):
                    mm(pys.ap[:, j * 16 + t:j * 16 + t + 1], yrow.ap[0:1, j * 128:j * 128 + 128], ones_b.ap[0:1, 0:1],
                       True, True, [yrow.r, ones_b.r], [pys.r])
            cp("dve", yaT.ap[:, :, CS:TT], pys.ap[:, 0:128].rearrange("p (j t) -> p j t", t=16), [pys.r], [yaT.r])
            bstate["reserved"].discard(7)
            dbg("yaT", yaT, [128, 8, TT])

            S.barrier()

            phase("D")
            for m in range(16):
                acc = t4()
                for br, (wsrc, yT) in enumerate(((i_wbc, ycT), (i_wbr, yrT), (i_wba, yaT))):
                    sbg, wg = load_slab(i_win[48 + br * 16 + m], 16)
                    sbp, wp = load_slab(wsrc[m], 8)
                    for (c0, n) in BLKS:
                        pg = pb()
                        for k in range(16):
                            mm(pg.ap[:, 0:n], wg[:, k, :], xT.ap[:, k, c0:c0 + n], k == 0, k == 15, [sbg.r, xT.r], [pg.r])
                        pp_ = pb()
                        for k in range(8):
                            mm(pp_.ap[:, 0:n], wp[:, k, :], yT.ap[:, k, c0:c0 + n], k == 0, k == 7, [sbp.r, yT.r], [pp_.r])
                        sg = t2()
                        act(sg.ap[:, 0:n], pg.ap[:, 0:n], AF.Exp, [pg.r], [sg.r], scale=-1.0)
                        act(sg.ap[:, 0:n], sg.ap[:, 0:n], AF.Ln, [sg.r], [sg.r], bias=1.0)
                        act(sg.ap[:, 0:n], sg.ap[:, 0:n], AF.Exp, [sg.r], [sg.r], scale=-1.0)
                        if br == 0:
                            tt("dve", acc.ap[:, c0:c0 + n], sg.ap[:, 0:n], pp_.ap[:, 0:n], ALU.mult, [sg.r, pp_.r], [acc.r])
                        else:
                            tt("dve", sg.ap[:, 0:n], sg.ap[:, 0:n], pp_.ap[:, 0:n], ALU.mult, [sg.r, pp_.r], [sg.r])
                            if br == 1:
                                tt("dve", acc.ap[:, c0:c0 + n], acc.ap[:, c0:c0 + n], sg.ap[:, 0:n], ALU.add,
                                   [acc.r, sg.r], [acc.r])
                            else:
                                tt("dve", mergedT.ap[:, m, c0:c0 + n], acc.ap[:, c0:c0 + n], sg.ap[:, 0:n], ALU.add,
                                   [acc.r, sg.r], [mergedT.r])
            dbg("mergedT", mergedT, [128, 16, TT])

            phase("E")
            TILES = [(C0 + 128 * i, 128, 128 * i) for i in range(8)] + [(CS, 16, 1024)]
            S.barrier()
            vres = view(R1, [128, 9, D], F32)
            vres_r = [Res() for _ in range(9)]
            for i, (col0, nt, row0) in enumerate(TILES):
                dma("sp", vres[0:nt, i, :], i_xtok[row0:row0 + nt, :], [], [vres_r[i]])
            for cb in range(16):
                sb, w = load_slab(i_wo[cb], 16)
                for i, (col0, nt, row0) in enumerate(TILES):
                    ps = pb()
                    for k in range(16):
                        mm(ps.ap[0:nt, 0:128], mergedT.ap[:, k, col0:col0 + nt], w[:, k, :], k == 0, k == 15,
                           [sb.r, mergedT.r], [ps.r])
                    vsl = vres[0:nt, i, cb * 128:cb * 128 + 128]
                    stt(vsl, vsl, ALPHA, ps.ap[0:nt, 0:128], ALU.mult, ALU.add, [vres_r[i], ps.r], [vres_r[i]])

            S.barrier()
            phase("F1")
            state["off"] = PH2
            x1T = Buf(view(R5, [128, 16, NTOK], BF16))
            qpT = Buf(view(R1, [128, 16, NTOK], BF16))
            keysT = new([128, 16, 128], BF16)
            lng = [new([128, D], F32) for _ in range(2)]
            vtile = [new([128, D], F32) for _ in range(2)]
            stats = new([128, 24], F32)
            mv2 = new([128, 2], F32)
            rstd = new([128, 1], F32)
            F_ONLY = state["off"]
            slabs[:] = [new([128, 16 * 128], BF16) for _ in range(4)]
            x1b = new([128, D], BF16)
            conv_step(NCONV)

            dma("pool", keysT.ap, i_keysT.rearrange("p (a b) -> p a b", b=128), [], [keysT.r])

            def layernorm(vb, nt, gi, out_buf):
                for q4 in range(4):
                    S.op("dve", lambda e, q4=q4: e.bn_stats(out=stats.ap[0:nt, q4 * 6:q4 * 6 + 6], in_=vb.ap[0:nt, q4 * 512:q4 * 512 + 512]),
                         [vb.r], [stats.r])
                S.op("dve", lambda e: e.bn_aggr(out=mv2.ap[0:nt, :], in_=stats.ap[0:nt, :]), [stats.r], [mv2.r])
                ts("dve", rstd.ap[0:nt, :], mv2.ap[0:nt, 1:2], LN_EPS, None, ALU.add, None, [mv2.r], [rstd.r])
                act(rstd.ap[0:nt, :], rstd.ap[0:nt, :], AF.Ln, [rstd.r], [rstd.r])
                act(rstd.ap[0:nt, :], rstd.ap[0:nt, :], AF.Exp, [rstd.r], [rstd.r], scale=-0.5)
                ts("dve", out_buf.ap[0:nt, :], vb.ap[0:nt, :], mv2.ap[0:nt, 0:1], rstd.ap[0:nt, 0:1], ALU.subtract, ALU.mult,
                   [vb.r, mv2.r, rstd.r], [out_buf.r])
                tt("dve", out_buf.ap[0:nt, :], out_buf.ap[0:nt, :], lng[0].ap[0:nt, :], ALU.mult, [out_buf.r, lng[0].r], [out_buf.r])
                tt("dve", out_buf.ap[0:nt, :], out_buf.ap[0:nt, :], lng[1].ap[0:nt, :], ALU.add, [out_buf.r, lng[1].r], [out_buf.r])

            dma("sp", lng[0].ap, bass.AP(i_ln.tensor, 0 * D, [[0, 128], [1, D]]), [], [lng[0].r])
            dma("sp", lng[1].ap, bass.AP(i_ln.tensor, 1 * D, [[0, 128], [1, D]]), [], [lng[1].r])
            TILES2 = [(128 * i, 128) for i in range(8)] + [(1024, 16)]
            def ln1_tile(ti, row0, nt):
                    vb = Buf(vres[:, ti, :], None, vres_r[ti])
                    layernorm(vb, nt, 0, vb)
                    dma("sp", x1scr[row0:row0 + nt, :], vb.ap[0:nt, :], [vb.r], [])
                    cp("act", x1b.ap[0:nt, :], vb.ap[0:nt, :], [vb.r], [x1b.r])
                    for g in range(4):
                        ps = pb()
                        psb = ps.ap.bitcast(BF16)
                        for kk in range(4):
                            k = g * 4 + kk
                            S.op("pe", lambda e, k=k, kk=kk, psb=psb: e.transpose(
                                psb[:, kk * 128:kk * 128 + nt], x1b.ap[0:nt, k * 128:k * 128 + 128], ident_b.ap[0:nt, 0:nt]),
                                [x1b.r, ident_b.r], [ps.r])
                        cp("dve", x1T.ap[:, g * 4:g * 4 + 4, row0:row0 + nt],
                           psb[:, 0:512].rearrange("p (a b) -> p a b", b=128)[:, :, 0:nt], [ps.r], [x1T.r])

            for ti, (row0, nt) in enumerate(TILES2):
                ln1_tile(ti, row0, nt)

            phase("F2")
            S.barrier()
            QBL = [(0, 512), (512, 512), (1024, 16)]
            for hc in range(16):
                sb, w = load_slab(i_wq[hc], 16)
                for (c0, n) in QBL:
                    ps = pb()
                    for k in range(16):
                        mm(ps.ap[:, 0:n], w[:, k, :], x1T.ap[:, k, c0:c0 + n], k == 0, k == 15, [sb.r, x1T.r], [ps.r])
                    cp("act", qpT.ap[:, hc, c0:c0 + n], ps.ap[:, 0:n], [ps.r], [qpT.r])

            phase("G")
            dma("sp", lng[0].ap, bass.AP(i_ln.tensor, 2 * D, [[0, 128], [1, D]]), [], [lng[0].r])
            dma("sp", lng[1].ap, bass.AP(i_ln.tensor, 3 * D, [[0, 128], [1, D]]), [], [lng[1].r])
            S.barrier()
            state["off"] = F_ONLY
            state["lim"] = ARENA_BYTES
            scs = new([128, D], F32)
            scs2 = new([128, D], F32)
            vals1 = new([128, 16, 16], F32)
            idx1 = new([128, 16, 16], U32)
            idx1f = new([128, 16, 16], F32)
            vals2 = new([128, 8, 16], F32)
            pos = new([128, 8, 16], U32)
            posf = new([128, 128], F32)
            posAf = new([128, 128], F32)
            posBf = new([128, 128], F32)
            thr16 = new([128, 16], F32)
            ts("dve", thr16.ap, iota16.ap, 16.0, 16.0, ALU.mult, ALU.add, [iota16.r], [thr16.r])
            selI = new([128, 128], F32)
            selJ = new([128, 128], F32)
            gw2 = [new([128, 128], F32) for _ in range(2)]
            gsum = new([128, 8], F32)
            actv = new([128, 128], F32)
            diag = [new([128, 128], BF16) for _ in range(4)]
            state["off"] = R2
            state["lim"] = R5 + 16 * TT * 2
            NPB = 6
            NVB = 7
            pbufs = [new([128, 2 * D], BF16) for _ in range(NPB)]
            pbV_r = [Res() for _ in range(NPB)]
            vgb = [new([128, D], BF16) for _ in range(NVB)]
            eidx2 = [new([128, 128], U32) for _ in range(2)]
            cco2 = [new([128, 128], F32) for _ in range(2)]
            tA = new([128, 128], F32)
            gA = new([128, 128], F32)
            rA = new([128, 128], F32)

            def bc(buf, off_elems, dims):
                return bass.AP(buf.ap.tensor, buf.ap.offset + off_elems, [list(buf.ap.ap[0])] + dims)

            def top16(src_ap, src_res, scratch_ap, scratch_res, vout, iout, res_v, res_i, nt):
                S.op("dve", lambda e: e.max(out=vout[0:nt, 0:8], in_=src_ap), [src_res], [res_v])
                S.op("dve", lambda e: e.max_index(out=iout[0:nt, 0:8], in_max=vout[0:nt, 0:8], in_values=src_ap),
                     [src_res, res_v], [res_i])
                S.op("dve", lambda e: e.match_replace(out=scratch_ap, in_to_replace=vout[0:nt, 0:8], in_values=src_ap,
                                                      imm_value=NEG), [src_res, res_v], [scratch_res])
                S.op("dve", lambda e: e.max(out=vout[0:nt, 8:16], in_=scratch_ap), [scratch_res], [res_v])
                S.op("dve", lambda e: e.max_index(out=iout[0:nt, 8:16], in_max=vout[0:nt, 8:16], in_values=scratch_ap),
                     [scratch_res, res_v], [res_i])

            def route(ti, row0, nt):
                    x1 = vtile[ti % 2]
                    eidx = eidx2[ti % 2]
                    gw = gw2[ti % 2]
                    dma("sp", x1.ap[0:nt, :], x1scr[row0:row0 + nt, :], [], [x1.r])
                    for g in range(4):
                        ps = banks[g]
                        for kk in range(4):
                            hc = g * 4 + kk
                            mm(ps.ap[0:nt, kk * 128:kk * 128 + 128], qpT.ap[:, hc, row0:row0 + nt], keysT.ap[:, hc, :], True, True,
                               [qpT.r, keysT.r], [ps.r])
                        cp("act", scs.ap[0:nt, g * 512:g * 512 + 512], ps.ap[0:nt, :], [ps.r], [scs.r])
                    for hc in range(16):
                        top16(scs.ap[0:nt, hc * 128:hc * 128 + 128], scs.r, scs2.ap[0:nt, hc * 128:hc * 128 + 128], scs2.r,
                              vals1.ap[:, hc, :], idx1.ap[:, hc, :], vals1.r, idx1.r, nt)
                        yield
                    cp("dve", idx1f.ap[0:nt], idx1.ap[0:nt], [idx1.r], [idx1f.r])
                    pstep = [list(vals1.ap.ap[0])[0], nt]
                    S.op("dve", lambda e, pstep=pstep: e.tensor_tensor(
                        out=scs.ap[0:nt, :].rearrange("p (h a b) -> p h a b", a=16, b=16),
                        in0=bass.AP(vals1.ap.tensor, vals1.ap.offset, [pstep, [32, 8], [1, 16], [0, 16]]),
                        in1=bass.AP(vals1.ap.tensor, vals1.ap.offset + 16, [pstep, [32, 8], [0, 16], [1, 16]]),
                        op=ALU.add), [vals1.r, scs.r], [scs.r])
                    for h in range(8):
                        top16(scs.ap[0:nt, h * 256:h * 256 + 256], scs.r, scs2.ap[0:nt, h * 256:h * 256 + 256], scs2.r,
                              vals2.ap[:, h, :], pos.ap[:, h, :], vals2.r, pos.r, nt)
                        yield
                    cp("dve", posf.ap[0:nt, :], pos.ap[0:nt].rearrange("p h k -> p (h k)"), [pos.r], [posf.r])
                    pfp = [list(posf.ap.ap[0])[0], nt]
                    tpp = [list(thr16.ap.ap[0])[0], nt]
                    S.op("dve", lambda e, pfp=pfp, tpp=tpp: e.tensor_tensor(
                        out=scs2.ap[0:nt, :].rearrange("p (s a) -> p s a", a=16),
                        in0=bass.AP(posf.ap.tensor, posf.ap.offset, [pfp, [1, 128], [0, 16]]),
                        in1=bass.AP(thr16.ap.tensor, thr16.ap.offset, [tpp, [0, 128], [1, 16]]),
                        op=ALU.is_ge), [posf.r, thr16.r, scs2.r], [scs2.r])
                    S.op("dve", lambda e: e.tensor_reduce(
                        out=posAf.ap[0:nt, :], in_=scs2.ap[0:nt, :].rearrange("p (s a) -> p s a", a=16),
                        axis=AX.X, op=ALU.add), [scs2.r], [posAf.r])
                    stt(posBf.ap[0:nt, :], posAf.ap[0:nt, :], -16.0, posf.ap[0:nt, :], ALU.mult, ALU.add,
                        [posAf.r, posf.r], [posBf.r])
                    ipart = [list(iota16.ap.ap[0])[0], nt]
                    for (pf, half_off, outb) in ((posAf, 0, selI), (posBf, 16, selJ)):
                        pp2 = [list(pf.ap.ap[0])[0], nt]
                        S.op("dve", lambda e, pf=pf, pp2=pp2: e.tensor_tensor(
                            out=scs2.ap[0:nt, :].rearrange("p (s a) -> p s a", a=16),
                            in0=bass.AP(pf.ap.tensor, pf.ap.offset, [pp2, [1, 128], [0, 16]]),
                            in1=bass.AP(iota16.ap.tensor, iota16.ap.offset, [ipart, [0, 128], [1, 16]]),
                            op=ALU.is_equal), [pf.r, iota16.r, scs2.r], [scs2.r])
                        ip2 = [list(idx1f.ap.ap[0])[0], nt]
                        S.op("dve", lambda e, half_off=half_off, ip2=ip2: e.tensor_tensor(
                            out=scs2.ap[0:nt, :].rearrange("p (h k a) -> p h k a", k=16, a=16),
                            in0=scs2.ap[0:nt, :].rearrange("p (h k a) -> p h k a", k=16, a=16),
                            in1=bass.AP(idx1f.ap.tensor, idx1f.ap.offset + half_off, [ip2, [32, 8], [0, 16], [1, 16]]),
                            op=ALU.mult), [scs2.r, idx1f.r], [scs2.r])
                        S.op("dve", lambda e, outb=outb: e.tensor_reduce(
                            out=outb.ap[0:nt, :], in_=scs2.ap[0:nt, :].rearrange("p (s a) -> p s a", a=16),
                            axis=AX.X, op=ALU.add), [scs2.r], [outb.r])
                    stt(selI.ap[0:nt, :], selI.ap[0:nt, :], 128.0, selJ.ap[0:nt, :], ALU.mult, ALU.add, [selI.r, selJ.r], [selI.r])
                    yield
                    cp("dve", eidx.ap[0:nt, :], selI.ap[0:nt, :], [selI.r], [eidx.r])
                    yield
                    vp = [list(vals2.ap.ap[0])[0], nt]
                    S.op("dve", lambda e, vp=vp: e.tensor_tensor(
                        out=gw.ap[0:nt, :].rearrange("p (h k) -> p h k", k=16), in0=vals2.ap[0:nt],
                        in1=bass.AP(vals2.ap.tensor, vals2.ap.offset, [vp, [16, 8], [0, 16]]), op=ALU.subtract),
                        [vals2.r], [gw.r])
                    act(gw.ap[0:nt, :], gw.ap[0:nt, :], AF.Exp, [gw.r], [gw.r])
                    S.op("dve", lambda e: e.tensor_reduce(out=gsum.ap[0:nt, :], in_=gw.ap[0:nt, :].rearrange("p (h k) -> p h k", k=16),
                                                          axis=AX.X, op=ALU.add), [gw.r], [gsum.r])
                    S.op("dve", lambda e: e.reciprocal(out=gsum.ap[0:nt, :], in_=gsum.ap[0:nt, :]), [gsum.r], [gsum.r])
                    gp = [list(gsum.ap.ap[0])[0], nt]
                    S.op("dve", lambda e, gp=gp: e.tensor_tensor(
                        out=gw.ap[0:nt, :].rearrange("p (h k) -> p h k", k=16), in0=gw.ap[0:nt, :].rearrange("p (h k) -> p h k", k=16),
                        in1=bass.AP(gsum.ap.tensor, gsum.ap.offset, [gp, [1, 8], [0, 16]]), op=ALU.mult),
                        [gw.r, gsum.r], [gw.r])
                    yield


            ybanks = banks[4:8]
            GS = 2
            NG = 128 // GS
            ucnt = {"i": 0}
            vcnt = {"i": 0}
            dcnt = {"i": 0}

            selm = new([128, 16], F32)
            selc = [new([128, 16], BF16) for _ in range(4)]
            eidxP = new([128, 16], U32)
            gwP = new([128, 16], F32)
            dma("sp", selm.ap, i_selm, [], [selm.r])

            def peer_tile(ti, row0, nt, gen, packed=False):
                x1 = vtile[ti % 2]
                eidx = eidx2[ti % 2]
                gw = gw2[ti % 2]
                cco = cco2[ti % 2]
                P = nt
                nslots = 128
                xd = x1
                if packed:
                    P = 128
                    nslots = 16
                    xd = alt(scs, [128, D], F32)
                    for h in range(8):
                        dma("sp", xd.ap[h * 16:h * 16 + 16, :], x1scr[row0:row0 + 16, :], [], [xd.r])
                        dma("sp", eidxP.ap[h * 16:h * 16 + 16, :], eidx.ap[0:16, h * 16:h * 16 + 16], [eidx.r], [eidxP.r])
                        dma("sp", gwP.ap[h * 16:h * 16 + 16, :], gw.ap[0:16, h * 16:h * 16 + 16], [gw.r], [gwP.r])
                    eidx = eidxP
                    gw = gwP
                ng = nslots // GS
                actv_r = [Res() for _ in range(ng)]
                tA_r = [Res() for _ in range(ng)]
                gA_r = [Res() for _ in range(ng)]
                rA_r = [Res() for _ in range(ng)]
                cco_r = [Res() for _ in range(ng)]
                vslots = {}

                def stage_c(g):
                    cs = slice(g * GS, g * GS + GS)
                    tt("dve", cco.ap[0:P, cs], rA.ap[0:P, cs], gA.ap[0:P, cs], ALU.mult, [rA_r[g], gA_r[g], cco.r], [cco_r[g]])
                    for s_ in range(g * GS, g * GS + GS):
                        vb = vslots.pop(s_)
                        if packed:
                            dg = selc[dcnt["i"] % 4]
                            dcnt["i"] += 1
                            S.op("act", lambda e, dg=dg, s_=s_: e.activation(
                                out=dg.ap[:, :], in_=selm.ap[:, :], func=AF.Copy,
                                scale=cco.ap[0:P, s_:s_ + 1]), [selm.r, cco_r[g]], [dg.r])
                            lhs = dg.ap[:, :]
                        else:
                            dg = diag[dcnt["i"] % 4]
                            dcnt["i"] += 1
                            S.op("act", lambda e, dg=dg, s_=s_: e.activation(
                                out=dg.ap[0:nt, 0:nt], in_=ident_f.ap[0:nt, 0:nt], func=AF.Copy,
                                scale=cco.ap[0:nt, s_:s_ + 1]), [ident_f.r, cco_r[g]], [dg.r])
                            lhs = dg.ap[0:nt, 0:nt]
                        for q4 in range(4):
                            mm(ybanks[q4].ap[0:nt, :], lhs, vb.ap[0:P, q4 * 512:q4 * 512 + 512],
                               s_ == 0, s_ == nslots - 1, [dg.r, vb.r], [ybanks[q4].r])

                for s_ in range(nslots):
                    g = s_ // GS
                    pi_ = ucnt["i"] % NPB
                    pb_ = pbufs[pi_]
                    rV = pbV_r[pi_]
                    ucnt["i"] += 1
                    vb = vgb[vcnt["i"] % NVB]
                    vcnt["i"] += 1
                    vslots[s_] = vb
                    S.dma("pool", lambda e, pb_=pb_, s_=s_: e.indirect_dma_start(
                        out=pb_.ap[0:P, :], out_offset=None, in_=puv_bf,
                        in_offset=bass.IndirectOffsetOnAxis(ap=eidx.ap[0:P, s_:s_ + 1], axis=0)),
                        [eidx.r], [pb_.r, rV])
                    stt(pb_.ap[0:P, 0:D], pb_.ap[0:P, 0:D], 1.0, xd.ap[0:P, :], ALU.mult, ALU.mult,
                        [pb_.r, xd.r], [pb_.r] + ([actv_r[g]] if s_ % GS == GS - 1 else []),
                        accum_out=actv.ap[0:P, s_:s_ + 1])
                    cp("act", vb.ap[0:P, :], pb_.ap[0:P, D:2 * D], [rV], [vb.r])
                    if s_ % GS == GS - 1:
                        cs = slice(g * GS, g * GS + GS)
                        a_ = actv.ap[0:P, cs]
                        tt("dve", tA.ap[0:P, cs], a_, a_, ALU.mult, [actv_r[g]], [tA_r[g]])
                        ts("dve", tA.ap[0:P, cs], tA.ap[0:P, cs], 0.044715, 1.0, ALU.mult, ALU.add, [tA_r[g]], [tA_r[g]])
                        tt("dve", tA.ap[0:P, cs], tA.ap[0:P, cs], a_, ALU.mult, [tA_r[g], actv_r[g]], [tA_r[g]])
                        tt("dve", gA.ap[0:P, cs], gw.ap[0:P, cs], a_, ALU.mult, [gw.r, actv_r[g]], [gA_r[g]])
                        act(rA.ap[0:P, cs], tA.ap[0:P, cs], AF.Exp, [tA_r[g]], [rA_r[g]], scale=-1.5957691216057308)
                        act(rA.ap[0:P, cs], rA.ap[0:P, cs], AF.Ln, [rA_r[g]], [rA_r[g]], bias=1.0)
                        act(rA.ap[0:P, cs], rA.ap[0:P, cs], AF.Exp, [rA_r[g]], [rA_r[g]], scale=-1.0)
                        if g >= 1:
                            stage_c(g - 1)
                    if s_ >= 8 and s_ % 2 == 0:
                        next(gen, None)
                stage_c(ng - 1)
                for _ in gen:
                    pass
                for q4 in range(4):
                    stt(x1.ap[0:nt, q4 * 512:q4 * 512 + 512], x1.ap[0:nt, q4 * 512:q4 * 512 + 512], ALPHA,
                        ybanks[q4].ap[0:nt, :], ALU.mult, ALU.add, [x1.r, ybanks[q4].r], [x1.r])
                layernorm(x1, nt, 1, x1)
                out_toks.append(dma("sp", o_y[row0:row0 + nt, :], x1.ap[0:nt, :], [x1.r], []))

            NT_ = len(TILES2)
            for _ in route(0, *TILES2[0]):
                pass
            for it in range(NT_):
                gen = route(it + 1, *TILES2[it + 1]) if it + 1 < NT_ else iter(())
                peer_tile(it, TILES2[it][0], TILES2[it][1], gen, packed=(TILES2[it][1] == 16))

        try:
            body()
        except _Stop:
            pass
        out_toks += list(dbg_out.values())
        S.wait_all("sp", out_toks)
        S.barrier()
        with nc.Block() as block:
            S.emit(block)
    return nc


def _tile_w(w, kc):
    n = w.shape[1] // 128
    return np.ascontiguousarray(w.reshape(kc, 128, n, 128).transpose(2, 1, 0, 3)).reshape(n, 128, kc * 128)


def _fm(a):
    lead = a.shape[:-1]
    a = a.reshape(-1, 8, 128)
    return np.ascontiguousarray(a.transpose(2, 1, 0)).reshape(128, 8, *lead)


_PROGRAM = {}


def prep_inputs(inp):
    f = np.float32
    g = lambda k: np.asarray(inp[k], dtype=f)
    x_prompt, x_sample, mem = g("x_prompt"), g("x_sample"), g("mem_prompt")
    shared = {
        "win": _tile_w(g("w_in")[0], 16),
        "wa": np.ascontiguousarray(g("rg_wa")[0].transpose(1, 0, 2)).reshape(128, 1024),
        "wx": np.ascontiguousarray(g("rg_wx")[0].transpose(1, 0, 2)).reshape(128, 1024),
        "wmk": _tile_w(g("w_mk")[0], 16),
        "wmv": _tile_w(g("w_mv")[0], 16),
        "wbc": _tile_w(g("w_br_conv")[0], 8),
        "wbr": _tile_w(g("w_br_rnn")[0], 8),
        "wba": _tile_w(g("w_br_attn")[0], 8),
        "wo": _tile_w(g("w_o")[0], 16),
        "wq": _tile_w(g("peer_wq")[0], 16),
        "keysT": np.ascontiguousarray(g("peer_keys")[0].reshape(16, 128, 128).transpose(2, 0, 1)).reshape(128, 2048),
        "ln": np.stack([g("ln1_g")[0], g("ln1_b")[0], g("ln2_g")[0], g("ln2_b")[0]]),
        "pu": g("peer_u")[0],
        "pv": g("peer_v")[0],
        "ident": np.eye(128, dtype=f),
        "sel": np.ascontiguousarray(np.broadcast_to(np.eye(16, dtype=f)[:, :, None], (16, 16, 128))).reshape(16, 2048),
        "iota16": np.ascontiguousarray(np.broadcast_to(np.arange(16, dtype=f)[None, :], (128, 16))),
        "selm": np.ascontiguousarray(np.tile(np.eye(16, dtype=f), (8, 1))),
    }
    chp = np.concatenate([_fm(g("conv_w")[0]), _fm(g("rg_conv_w")[0]), _fm(g("rg_conv_b")), _fm(g("rg_ba")),
                          _fm(g("rg_bx")), _fm(g("rg_lambda"))], axis=2)
    shared["chp"] = np.ascontiguousarray(chp).reshape(128, 88)
    maps = []
    for c in range(NCORES):
        b, half = c // 2, c % 2
        cur = x_prompt[b, half * 1024:(half + 1) * 1024]
        prev = x_prompt[b, 0:1024] if half == 1 else np.zeros((1024, D), f)
        xs = x_sample[c * 16:(c + 1) * 16, 0]
        xall = np.concatenate([prev[-3:], cur, xs], axis=0)
        m = dict(shared)
        m["xT"] = np.ascontiguousarray(xall.T)
        m["xprevT"] = np.ascontiguousarray(prev.T)
        m["xtok"] = np.ascontiguousarray(np.concatenate([cur, xs], axis=0))
        m["flag"] = np.full((128, 1), float(half), f)
        m["memT"] = np.ascontiguousarray(mem[b].T)
        m["ck"] = np.ascontiguousarray(g("cache_mem_k")[0, c * 16:(c + 1) * 16].reshape(16, 256, 1024))
        m["cv"] = np.ascontiguousarray(g("cache_mem_v")[0, c * 16:(c + 1) * 16].reshape(16, 256, 1024))
        scz = g("state_conv_z")[0, c * 16:(c + 1) * 16]
        src = g("state_rglru_conv")[0, c * 16:(c + 1) * 16]
        sh = g("state_rglru_h")[0, c * 16:(c + 1) * 16]
        m["scz"] = np.ascontiguousarray(_fm(scz.transpose(1, 0, 2))).reshape(128, 8 * 2 * 16)
        m["src"] = np.ascontiguousarray(_fm(src.transpose(1, 0, 2))).reshape(128, 8 * 3 * 16)
        m["sh"] = np.ascontiguousarray(_fm(sh)).reshape(128, 8 * 16)
        m["scz_tok"] = np.ascontiguousarray(scz)
        m["src_tok"] = np.ascontiguousarray(src)
        maps.append(m)
    return maps


def assemble(results):
    f = np.float32
    y_prompt = np.zeros((4, 2048, D), f)
    y_sample = np.zeros((128, 1, D), f)
    mk = np.zeros((1, 4, 256, 4, 256), f)
    mv = np.zeros((1, 4, 256, 4, 256), f)
    czp = np.zeros((1, 4, 2, 1024), f)
    rcp = np.zeros((1, 4, 3, 1024), f)
    hp = np.zeros((1, 4, 1024), f)
    czs = np.zeros((1, 128, 2, 1024), f)
    rcs = np.zeros((1, 128, 3, 1024), f)
    hs = np.zeros((1, 128, 1024), f)
    for c, r in enumerate(results):
        b, half = c // 2, c % 2
        y = r["y"]
        y_prompt[b, half * 1024:(half + 1) * 1024] = y[0:1024]
        y_sample[c * 16:(c + 1) * 16, 0] = y[1024:1040]
        stv = r["st_out"]
        if half == 1:
            czp[0, b] = stv[0:2]
            rcp[0, b] = stv[2:5]
            hp[0, b] = stv[5]
        else:
            mk[0, b] = r["mk_out"].reshape(256, 4, 256)
            mv[0, b] = r["mv_out"].reshape(256, 4, 256)
        czs[0, c * 16:(c + 1) * 16] = r["czs_out"]
        rcs[0, c * 16:(c + 1) * 16] = r["rcs_out"]
        hs[0, c * 16:(c + 1) * 16] = stv[38:54]
    return (y_prompt, y_sample, mk, mv, czp, rcp, hp, czs, rcs, hs)


def kernel(**inputs):
    if "nc" not in _PROGRAM:
        _PROGRAM["nc"] = build_program()
    maps = prep_inputs(inputs)
    res = run_bass_kernel_spmd(_PROGRAM["nc"], maps, core_ids=list(range(NCORES)))
    return assemble(res.results)
```

```python
from contextlib import ExitStack
import numpy as np
import concourse.bass as bass
import concourse.mybir as mybir
from concourse.bass_utils import run_bass_kernel_spmd

F32 = mybir.dt.float32
BF16 = mybir.dt.bfloat16
I32 = mybir.dt.int32
U32 = mybir.dt.uint32
U8 = mybir.dt.uint8
ALU = mybir.AluOpType
AF = mybir.ActivationFunctionType
AX = mybir.AxisListType

NCORES = 8
D = 2048
TT = 1043
C0 = 3
CS = 1027
NTOK = 1040
BLKS = [(0, 512), (512, 512), (1024, 19)]
ALPHA = 2.0 ** 0.25
LN_EPS = 1e-5
NEG = -1.0e30


class Res:
    __slots__ = ("w", "rs", "x")

    def __init__(self):
        self.w = None
        self.rs = {}
        self.x = False


class Stream:
    __slots__ = ("name", "sem", "inc", "n")

    def __init__(self, name, sem, inc):
        self.name = name
        self.sem = sem
        self.inc = inc
        self.n = 0


class Sched:
    ENGS = ("pe", "act", "dve", "pool", "sp")

    def __init__(self, nc, stack):
        self.nc = nc
        self.items = {e: [] for e in self.ENGS}
        self.clock = {e: {} for e in self.ENGS}
        self.cstream = {}
        self.all_streams = []
        for e in self.ENGS:
            if e == "sp":
                continue
            s = Stream(e, stack.enter_context(nc.semaphore("c_" + e)), 1)
            self.cstream[e] = s
            self.all_streams.append(s)
        self.dstreams = {}
        self.dcount = {}
        for q, k in (("sp", 8), ("pool", 8), ("act", 2)):
            self.dstreams[q] = [Stream(f"d_{q}{i}", stack.enter_context(nc.semaphore(f"d_{q}{i}")), 16)
                                for i in range(k)]
            self.all_streams += self.dstreams[q]
            self.dcount[q] = 0
        self.nops = 0

    def _collect(self, eng, reads, writes, is_dma=False):
        need = {}

        def add(tok):
            if tok is None:
                return
            s, n, _ = tok
            if need.get(s, (0, None))[0] < n:
                need[s] = (n, tok)

        own = self.cstream.get(eng)
        for r in reads:
            add(r.w)
            if r.x:
                for t in r.rs.values():
                    if t[0] is not own:
                        add(t)
        for w in writes:
            add(w.w)
            for t in w.rs.values():
                if eng == "pe" and (not is_dma) and t[0] is own:
                    continue
                add(t)
        clk = self.clock[eng]
        for s, (n, tok) in need.items():
            if clk.get(s.name, 0) >= n:
                continue
            if eng == "pe" and s is own:
                continue
            self.items[eng].append(("w", s.sem, n * s.inc))
            for k, v in tok[2].items():
                if clk.get(k, 0) < v:
                    clk[k] = v
            clk[s.name] = n

    def _finish(self, eng, stream, fn, reads, writes):
        stream.n += 1
        tok = (stream, stream.n, dict(self.clock[eng]))
        self.items[eng].append(("i", fn, stream.sem, stream.inc))
        for r in reads:
            r.rs[stream] = tok
        for w in writes:
            w.w = tok
            w.rs = {}
        self.nops += 1
        return tok

    def op(self, eng, fn, reads=(), writes=()):
        self._collect(eng, reads, writes)
        return self._finish(eng, self.cstream[eng], fn, reads, writes)

    def dma(self, q, fn, reads=(), writes=()):
        ds = self.dstreams[q]
        st = ds[self.dcount[q] % len(ds)]
        self.dcount[q] += 1
        clk = self.clock[q]
        if st.n > 0 and clk.get(st.name, 0) < st.n:
            self.items[q].append(("w", st.sem, st.n * st.inc))
            clk[st.name] = st.n
        self._collect(q, reads, writes, is_dma=True)
        return self._finish(q, st, fn, reads, writes)

    def wait_all(self, eng, toks):
        clk = self.clock[eng]
        for tok in toks:
            s, n, _ = tok
            if n == 0 or clk.get(s.name, 0) >= n:
                continue
            self.items[eng].append(("w", s.sem, n * s.inc))
            clk[s.name] = n

    def barrier(self):
        toks = [(s, s.n, {}) for s in self.all_streams]
        for e in self.ENGS:
            self.wait_all(e, toks)

    def emit(self, block):
        def run(e, items):
            for it in items:
                if it[0] == "w":
                    e.wait_ge(it[1], it[2])
                else:
                    it[1](e).then_inc(it[2], it[3])

        items = self.items

        @block.sync
        def _(e):
            run(e, items["sp"])

        @block.tensor
        def _(e):
            run(e, items["pe"])

        @block.scalar
        def _(e):
            run(e, items["act"])

        @block.vector
        def _(e):
            run(e, items["dve"])

        @block.gpsimd
        def _(e):
            run(e, items["pool"])


class Buf:
    __slots__ = ("ap", "r", "off")

    def __init__(self, ap, off=None, r=None):
        self.ap = ap
        self.r = r if r is not None else Res()
        self.off = off


class _Stop(Exception):
    pass


KNOB = {"mk": 9, "conv_from": 24, "conv_n": None}
PHASES = ["SETUP", "MK", "B0", "A", "B", "ST", "C", "D", "E", "F1", "F2", "G"]


def build_program(debug=(), stop_after="G"):
    nc = bass.Bass("TRN2", target_bir_lowering=False)

    def din(name, shape, dt=F32):
        return nc.dram_tensor(name, list(shape), dt, kind="ExternalInput").ap()

    def dout(name, shape, dt=F32):
        return nc.dram_tensor(name, list(shape), dt, kind="ExternalOutput").ap()

    i_xT = din("xT", [D, TT])
    i_xprevT = din("xprevT", [D, 1024])
    i_xtok = din("xtok", [NTOK, D])
    i_flag = din("flag", [128, 1])
    i_memT = din("memT", [D, 256])
    i_ck = din("ck", [16, 256, 1024])
    i_cv = din("cv", [16, 256, 1024])
    i_scz = din("scz", [128, 8 * 2 * 16])
    i_src = din("src", [128, 8 * 3 * 16])
    i_sh = din("sh", [128, 8 * 16])
    i_scz_tok = din("scz_tok", [16, 2, 1024])
    i_src_tok = din("src_tok", [16, 3, 1024])
    i_chp = din("chp", [128, 8 * 11])
    i_win = din("win", [96, 128, 16 * 128])
    i_wa = din("wa", [128, 8 * 128])
    i_wx = din("wx", [128, 8 * 128])
    i_wmk = din("wmk", [8, 128, 16 * 128])
    i_wmv = din("wmv", [8, 128, 16 * 128])
    i_wbc = din("wbc", [16, 128, 8 * 128])
    i_wbr = din("wbr", [16, 128, 8 * 128])
    i_wba = din("wba", [16, 128, 8 * 128])
    i_wo = din("wo", [16, 128, 16 * 128])
    i_wq = din("wq", [16, 128, 16 * 128])
    i_keysT = din("keysT", [128, 16 * 128])
    i_ln = din("ln", [4, D])
    i_pu = din("pu", [16384, D])
    i_pv = din("pv", [16384, D])
    i_ident = din("ident", [128, 128])
    i_sel = din("sel", [16, 16 * 128])
    i_iota = din("iota16", [128, 16])
    i_selm = din("selm", [128, 16])

    o_y = dout("y", [NTOK, D])
    o_mk = dout("mk_out", [256, 1024])
    o_mv = dout("mv_out", [256, 1024])
    o_st = dout("st_out", [54, 1024])
    o_czs = dout("czs_out", [16, 2, 1024])
    o_rcs = dout("rcs_out", [16, 3, 1024])
    dbg_out = {}

    vscr = nc.dram_tensor("vscr", [NTOK, D], F32, kind="Internal").ap()
    x1scr = nc.dram_tensor("x1scr", [NTOK, D], F32, kind="Internal").ap()
    puv_bf = nc.dram_tensor("puv_bf", [16384, 2 * D], BF16, kind="Internal").ap()

    st = ExitStack()
    with st:
        S = Sched(nc, st)
        ARENA_BYTES = 207 * 1024
        arena_t = st.enter_context(nc.sbuf_tensor("arena", [128, ARENA_BYTES], U8))
        state = {"off": 0, "lim": ARENA_BYTES}

        def alloc(nbytes):
            off = state["off"]
            state["off"] = off + (nbytes + 63) // 64 * 64
            assert state["off"] <= state["lim"], ("arena overflow", state["off"], state["lim"])
            return off

        def view(off, shape, dt, parts=128):
            es = 2 if dt == BF16 else 4
            n = int(np.prod(shape[1:]))
            v = arena_t[0:shape[0], off:off + n * es].bitcast(dt)
            if len(shape) == 3:
                v = v.rearrange("p (a b) -> p a b", b=shape[2])
            elif len(shape) == 4:
                v = v.rearrange("p (a b c) -> p a b c", b=shape[2], c=shape[3])
            return v

        def new(shape, dt):
            es = 2 if dt == BF16 else 4
            off = alloc(int(np.prod(shape[1:])) * es)
            return Buf(view(off, shape, dt), off)

        def alt(buf, shape, dt):
            return Buf(view(buf.off, shape, dt), buf.off, buf.r)

        banks = [Buf(st.enter_context(nc.psum_tensor(f"bank{i}", [128, 512], F32))[:]) for i in range(8)]
        for b_ in banks:
            b_.r.x = True
        bstate = {"i": 0, "reserved": set()}

        def pb():
            while True:
                i = bstate["i"] % 8
                bstate["i"] += 1
                if i not in bstate["reserved"]:
                    return banks[i]

        def mm(out, lhsT, rhs, start, stop, reads, writes):
            S.op("pe", lambda e: e.matmul(out, lhsT, rhs, start=start, stop=stop), reads, writes)

        def act(out, in_, func, reads, writes, bias=0.0, scale=1.0, accum_out=None):
            if accum_out is None:
                S.op("act", lambda e: e.activation(out=out, in_=in_, func=func, bias=bias, scale=scale),
                     reads, writes)
            else:
                S.op("act", lambda e: e.activation(out=out, in_=in_, func=func, bias=bias, scale=scale,
                                                   accum_out=accum_out), reads, writes)

        def tt(eng, out, in0, in1, op, reads, writes):
            S.op(eng, lambda e: e.tensor_tensor(out=out, in0=in0, in1=in1, op=op), reads, writes)

        def ts(eng, out, in0, s1, s2, op0, op1, reads, writes):
            if op1 is None:
                S.op(eng, lambda e: e.tensor_scalar(out=out, in0=in0, scalar1=s1, scalar2=None, op0=op0),
                     reads, writes)
            else:
                S.op(eng, lambda e: e.tensor_scalar(out=out, in0=in0, scalar1=s1, scalar2=s2, op0=op0, op1=op1),
                     reads, writes)

        def stt(out, in0, scalar, in1, op0, op1, reads, writes, accum_out=None):
            if accum_out is None:
                S.op("dve", lambda e: e.scalar_tensor_tensor(out=out, in0=in0, scalar=scalar, in1=in1,
                                                             op0=op0, op1=op1), reads, writes)
            else:
                S.op("dve", lambda e: e.scalar_tensor_tensor(out=out, in0=in0, scalar=scalar, in1=in1,
                                                             op0=op0, op1=op1, accum_out=accum_out),
                     reads, writes)

        def cp(eng, out, in_, reads, writes):
            if eng == "act":
                S.op("act", lambda e: e.activation(out=out, in_=in_, func=AF.Copy), reads, writes)
            else:
                S.op(eng, lambda e: e.tensor_copy(out=out, in_=in_), reads, writes)

        def memset(eng, ap, val, writes):
            S.op(eng, lambda e: e.memset(ap, val), (), writes)

        def dma(q, out, in_, reads, writes):
            return S.dma(q, lambda e: e.dma_start(out=out, in_=in_), reads, writes)

        def dbg(name, buf, shape):
            if name in debug:
                o = dout("dbg_" + name, shape, buf.ap.dtype)
                dbg_out[name] = dma("sp", o, buf.ap, [buf.r], [])

        out_toks = []

        ident_f = new([128, 128], F32)
        ident_b = new([128, 128], BF16)
        ones_b = new([128, 128], BF16)
        sel_b = new([16, 16, 128], BF16)
        iota16 = new([128, 16], F32)
        chp = new([128, 8, 11], F32)
        nba = new([128, 8], F32)
        nbx = new([128, 8], F32)
        cA = new([128, 8], F32)
        c2A = new([128, 8], F32)
        flag = new([128, 1], F32)
        scz = new([128, 8, 2, 16], F32)
        src = new([128, 8, 3, 16], F32)
        sh = new([128, 8, 16], F32)
        wa_b = new([128, 8, 128], BF16)
        wx_b = new([128, 8, 128], BF16)
        hmid = new([128, 8], F32)
        rxhist = new([128, 8, 3], F32)
        ST = new([128, 8, 64], F32)
        mkT = new([128, 8, 256], BF16)
        mv_b = new([128, 2, 1024], BF16)

        R1 = alloc(16 * TT * 2)
        R2 = alloc(16 * TT * 2)
        R3 = alloc(8 * TT * 2)
        R5 = alloc(16 * TT * 2)
        xT = Buf(view(R1, [128, 16, TT], BF16))
        ycT = Buf(view(R2, [128, 8, TT], BF16))
        yrT = Buf(view(R2 + 8 * TT * 2, [128, 8, TT], BF16))
        yaT = Buf(view(R3, [128, 8, TT], BF16))
        xprevT = Buf(view(R5, [128, 16, 1024], BF16))
        qT = Buf(view(R5, [128, 8, TT], BF16))
        mergedT = Buf(view(R5, [128, 16, TT], BF16))
        PH2 = state["off"]

        slabs = [new([128, 16 * 128], BF16) for _ in range(6)]
        sstate = {"i": 0}

        conv = {"i": 0, "r": Res()}
        CONV_ROWS = 128
        NCONV = 2 * 16384 // CONV_ROWS

        def conv_step(n):
            for _ in range(n):
                i = conv["i"]
                if i >= NCONV:
                    return
                conv["i"] += 1
                src_t, c0_ = (i_pu, 0) if i % 2 == 0 else (i_pv, D)
                r0 = (i // 2) * CONV_ROWS
                dma("pool", puv_bf[r0:r0 + CONV_ROWS, c0_:c0_ + D], src_t[r0:r0 + CONV_ROWS, :], [], [])

        def load_slab(src_ap, kc):
            sb = slabs[sstate["i"] % len(slabs)]
            sstate["i"] += 1
            dma("pool", sb.ap[:, 0:kc * 128], src_ap, [], [sb.r])
            if sstate["i"] > KNOB["conv_from"]:
                if KNOB["conv_n"] is None:
                    conv_step(2 if sstate["i"] % 3 == 0 else 1)
                else:
                    conv_step(KNOB["conv_n"])
            return sb, sb.ap[:, 0:kc * 128].rearrange("p (k c) -> p k c", c=128)

        s4 = [new([128, TT + 5], F32) for _ in range(5)]
        s2 = [new([128, 512], F32) for _ in range(6)]
        hb = [new([128, 512], F32) for _ in range(2)]
        s2b = [new([128, 512], BF16) for _ in range(4)]
        st4 = {"i": 0}
        st2 = {"i": 0}
        st2b = {"i": 0}

        def t4():
            b = s4[st4["i"] % len(s4)]
            st4["i"] += 1
            return b

        def t2():
            b = s2[st2["i"] % len(s2)]
            st2["i"] += 1
            return b

        def t2b():
            b = s2b[st2b["i"] % len(s2b)]
            st2b["i"] += 1
            return b

        dma("sp", ident_f.ap, i_ident, [], [ident_f.r])
        dma("sp", iota16.ap, i_iota, [], [iota16.r])
        dma("sp", chp.ap, i_chp.rearrange("p (a b) -> p a b", b=11), [], [chp.r])
        dma("sp", flag.ap, i_flag, [], [flag.r])
        dma("sp", scz.ap, i_scz.rearrange("p (a b c) -> p a b c", b=2, c=16), [], [scz.r])
        dma("sp", src.ap, i_src.rearrange("p (a b c) -> p a b c", b=3, c=16), [], [src.r])
        dma("sp", sh.ap, i_sh.rearrange("p (a b) -> p a b", b=16), [], [sh.r])
        dma("pool", sel_b.ap, i_sel.rearrange("p (a b) -> p a b", b=128), [], [sel_b.r])
        dma("pool", wa_b.ap, i_wa.rearrange("p (a b) -> p a b", b=128), [], [wa_b.r])
        dma("pool", wx_b.ap, i_wx.rearrange("p (a b) -> p a b", b=128), [], [wx_b.r])
        for g in range(4):
            dma("pool", xprevT.ap[:, 4 * g:4 * g + 4, :],
                i_xprevT[512 * g:512 * g + 512, :].rearrange("(k p) n -> p k n", p=128), [], [xprevT.r])
        for g in range(4):
            dma("pool", xT.ap[:, 4 * g:4 * g + 4, :],
                i_xT[512 * g:512 * g + 512, :].rearrange("(k p) n -> p k n", p=128), [], [xT.r])
        cp("dve", ident_b.ap, ident_f.ap, [ident_f.r], [ident_b.r])
        memset("dve", ones_b.ap, 1.0, [ones_b.r])
        memset("dve", ST.ap, 0.0, [ST.r])
        ts("dve", nba.ap, chp.ap[:, :, 8], -1.0, None, ALU.mult, None, [chp.r], [nba.r])
        ts("dve", nbx.ap, chp.ap[:, :, 9], -1.0, None, ALU.mult, None, [chp.r], [nbx.r])
        tmpc = new([128, 8], F32)
        act(tmpc.ap, chp.ap[:, :, 10], AF.Exp, [chp.r], [tmpc.r], scale=-1.0)
        act(tmpc.ap, tmpc.ap, AF.Ln, [tmpc.r], [tmpc.r], bias=1.0)
        ts("dve", cA.ap, tmpc.ap, -8.0, None, ALU.mult, None, [tmpc.r], [cA.r])
        ts("dve", c2A.ap, tmpc.ap, -16.0, None, ALU.mult, None, [tmpc.r], [c2A.r])

        def phase(name):
            if PHASES.index(name) > PHASES.index(stop_after):
                raise _Stop()

        def body():
            def sigmoid_from_psum(ps, nbias_ap, n, out_buf):
                act(out_buf.ap[:, 0:n], ps.ap[:, 0:n], AF.Exp, [ps.r, nba.r, nbx.r], [out_buf.r], bias=nbias_ap, scale=-1.0)
                act(out_buf.ap[:, 0:n], out_buf.ap[:, 0:n], AF.Ln, [out_buf.r], [out_buf.r], bias=1.0)
                act(out_buf.ap[:, 0:n], out_buf.ap[:, 0:n], AF.Exp, [out_buf.r], [out_buf.r], scale=-1.0)

            def rg_block(j, xr_ap, xr_res, n, h_init, h_out_buf, h_init_res=None):
                xb = t2b()
                cp("act", xb.ap[:, 0:n], xr_ap, [xr_res], [xb.r])
                pr = pb()
                mm(pr.ap[:, 0:n], wa_b.ap[:, j, :], xb.ap[:, 0:n], True, True, [wa_b.r, xb.r], [pr.r])
                pi = pb()
                mm(pi.ap[:, 0:n], wx_b.ap[:, j, :], xb.ap[:, 0:n], True, True, [wx_b.r, xb.r], [pi.r])
                r = t2()
                sigmoid_from_psum(pr, nba.ap[:, j:j + 1], n, r)
                ig = t2()
                sigmoid_from_psum(pi, nbx.ap[:, j:j + 1], n, ig)
                a2 = t2()
                act(a2.ap[:, 0:n], r.ap[:, 0:n], AF.Exp, [r.r, c2A.r], [a2.r], scale=c2A.ap[:, j:j + 1])
                a = t2()
                act(a.ap[:, 0:n], r.ap[:, 0:n], AF.Exp, [r.r, cA.r], [a.r], scale=cA.ap[:, j:j + 1])
                ts("dve", a2.ap[:, 0:n], a2.ap[:, 0:n], -1.0, 1.0, ALU.mult, ALU.add, [a2.r], [a2.r])
                ts("dve", a2.ap[:, 0:n], a2.ap[:, 0:n], 1e-30, None, ALU.max, None, [a2.r], [a2.r])
                act(a2.ap[:, 0:n], a2.ap[:, 0:n], AF.Ln, [a2.r], [a2.r])
                act(a2.ap[:, 0:n], a2.ap[:, 0:n], AF.Exp, [a2.r], [a2.r], scale=0.5)
                tt("dve", ig.ap[:, 0:n], ig.ap[:, 0:n], xr_ap, ALU.mult, [ig.r, xr_res], [ig.r])
                tt("dve", ig.ap[:, 0:n], ig.ap[:, 0:n], a2.ap[:, 0:n], ALU.mult, [ig.r, a2.r], [ig.r])
                S.op("dve", lambda e: e.tensor_tensor_scan(out=h_out_buf.ap[:, 0:n], data0=a.ap[:, 0:n],
                                                           data1=ig.ap[:, 0:n], initial=h_init,
                                                           op0=ALU.mult, op1=ALU.add),
                     [a.r, ig.r, hmid.r] + ([h_init_res] if h_init_res is not None else []), [h_out_buf.r])

            def rg_gates(j, xr, blocks):
                nb = len(blocks)
                xb = [t2b() for _ in range(nb)]
                for b, (c0, n) in enumerate(blocks):
                    cp("act", xb[b].ap[:, 0:n], xr.ap[:, c0:c0 + n], [xr.r], [xb[b].r])
                pr = [pb() for _ in range(nb)]
                pi = [pb() for _ in range(nb)]
                for b, (c0, n) in enumerate(blocks):
                    mm(pr[b].ap[:, 0:n], wa_b.ap[:, j, :], xb[b].ap[:, 0:n], True, True, [wa_b.r, xb[b].r], [pr[b].r])
                    mm(pi[b].ap[:, 0:n], wx_b.ap[:, j, :], xb[b].ap[:, 0:n], True, True, [wx_b.r, xb[b].r], [pi[b].r])
                r = [t2() for _ in range(nb)]
                ig = [t2() for _ in range(nb)]
                a2 = [t2() for _ in range(nb)]
                a = [t2() for _ in range(nb)]
                for b, (c0, n) in enumerate(blocks):
                    act(r[b].ap[:, 0:n], pr[b].ap[:, 0:n], AF.Exp, [pr[b].r, nba.r], [r[b].r], bias=nba.ap[:, j:j + 1], scale=-1.0)
                    act(ig[b].ap[:, 0:n], pi[b].ap[:, 0:n], AF.Exp, [pi[b].r, nbx.r], [ig[b].r], bias=nbx.ap[:, j:j + 1], scale=-1.0)
                for b, (c0, n) in enumerate(blocks):
                    act(r[b].ap[:, 0:n], r[b].ap[:, 0:n], AF.Ln, [r[b].r], [r[b].r], bias=1.0)
                    act(ig[b].ap[:, 0:n], ig[b].ap[:, 0:n], AF.Ln, [ig[b].r], [ig[b].r], bias=1.0)
                for b, (c0, n) in enumerate(blocks):
                    act(r[b].ap[:, 0:n], r[b].ap[:, 0:n], AF.Exp, [r[b].r], [r[b].r], scale=-1.0)
                    act(ig[b].ap[:, 0:n], ig[b].ap[:, 0:n], AF.Exp, [ig[b].r], [ig[b].r], scale=-1.0)
                for b, (c0, n) in enumerate(blocks):
                    act(a2[b].ap[:, 0:n], r[b].ap[:, 0:n], AF.Exp, [r[b].r, c2A.r], [a2[b].r], scale=c2A.ap[:, j:j + 1])
                    act(a[b].ap[:, 0:n], r[b].ap[:, 0:n], AF.Exp, [r[b].r, cA.r], [a[b].r], scale=cA.ap[:, j:j + 1])
                for b, (c0, n) in enumerate(blocks):
                    ts("dve", a2[b].ap[:, 0:n], a2[b].ap[:, 0:n], -1.0, 1.0, ALU.mult, ALU.add, [a2[b].r], [a2[b].r])
                    tt("dve", ig[b].ap[:, 0:n], ig[b].ap[:, 0:n], xr.ap[:, c0:c0 + n], ALU.mult, [ig[b].r, xr.r], [ig[b].r])
                for b, (c0, n) in enumerate(blocks):
                    ts("dve", a2[b].ap[:, 0:n], a2[b].ap[:, 0:n], 1e-30, None, ALU.max, None, [a2[b].r], [a2[b].r])
                for b, (c0, n) in enumerate(blocks):
                    act(a2[b].ap[:, 0:n], a2[b].ap[:, 0:n], AF.Ln, [a2[b].r], [a2[b].r])
                for b, (c0, n) in enumerate(blocks):
                    act(a2[b].ap[:, 0:n], a2[b].ap[:, 0:n], AF.Exp, [a2[b].r], [a2[b].r], scale=0.5)
                for b, (c0, n) in enumerate(blocks):
                    tt("dve", ig[b].ap[:, 0:n], ig[b].ap[:, 0:n], a2[b].ap[:, 0:n], ALU.mult, [ig[b].r, a2[b].r], [ig[b].r])
                return a, ig

            def scan(a_b, u_b, n, h_init, h_out, extra_reads):
                S.op("dve", lambda e: e.tensor_tensor_scan(out=h_out.ap[:, 0:n], data0=a_b.ap[:, 0:n],
                                                           data1=u_b.ap[:, 0:n], initial=h_init,
                                                           op0=ALU.mult, op1=ALU.add),
                     [a_b.r, u_b.r] + extra_reads, [h_out.r])

            def dwconv4(j, rxf, n, xr):
                w = lambda k: chp.ap[:, j, 3 + k:4 + k]
                ts("dve", xr.ap[:, 0:n], rxf.ap[:, 3:3 + n], w(3), chp.ap[:, j, 7:8], ALU.mult, ALU.add,
                   [rxf.r, chp.r], [xr.r])
                for k in range(3):
                    stt(xr.ap[:, 0:n], rxf.ap[:, k:k + n], w(k), xr.ap[:, 0:n], ALU.mult, ALU.add,
                        [rxf.r, chp.r, xr.r], [xr.r])

            phase("MK")
            memT = Buf(view(R2, [128, 16, 256], BF16))
            for g in range(4):
                dma("pool", memT.ap[:, 4 * g:4 * g + 4, :],
                    i_memT[512 * g:512 * g + 512, :].rearrange("(k p) n -> p k n", p=128), [], [memT.r])
            mk_tok = Buf(view(R3, [128, 2, 1024], F32))
            mv_tok = Buf(view(R3 + 8192, [128, 2, 1024], F32))
            for c8 in range(8):
                if KNOB["mk"] < 1:
                    break
                sb, w = load_slab(i_wmk[c8], 16)
                if KNOB["mk"] < 2:
                    continue
                ps = pb()
                for k in range(16):
                    mm(ps.ap[:, 0:256], w[:, k, :], memT.ap[:, k, :], k == 0, k == 15, [sb.r, memT.r], [ps.r])
                if KNOB["mk"] < 3:
                    continue
                cp("act", mkT.ap[:, c8, :], ps.ap[:, 0:256], [ps.r], [mkT.r])
                if KNOB["mk"] < 4:
                    continue
                ps2 = pb()
                for mc in range(2):
                    for k in range(16):
                        mm(ps2.ap[:, mc * 128:mc * 128 + 128], memT.ap[:, k, mc * 128:mc * 128 + 128], w[:, k, :],
                           k == 0, k == 15, [sb.r, memT.r], [ps2.r])
                if KNOB["mk"] < 5:
                    continue
                cp("dve", mk_tok.ap[:, :, c8 * 128:c8 * 128 + 128],
                   ps2.ap[:, 0:256].rearrange("p (a b) -> p a b", b=128), [ps2.r], [mk_tok.r])
            if KNOB["mk"] < 6:
                raise _Stop()
            out_toks.append(dma("sp", o_mk.rearrange("(c p) n -> p c n", p=128), mk_tok.ap, [mk_tok.r], []))
            if KNOB["mk"] < 7:
                raise _Stop()
            for c8 in range(8):
                sb, w = load_slab(i_wmv[c8], 16)
                ps2 = pb()
                for mc in range(2):
                    for k in range(16):
                        mm(ps2.ap[:, mc * 128:mc * 128 + 128], memT.ap[:, k, mc * 128:mc * 128 + 128], w[:, k, :],
                           k == 0, k == 15, [sb.r, memT.r], [ps2.r])
                cp("dve", mv_tok.ap[:, :, c8 * 128:c8 * 128 + 128],
                   ps2.ap[:, 0:256].rearrange("p (a b) -> p a b", b=128), [ps2.r], [mv_tok.r])
                if KNOB["mk"] < 8:
                    continue
                for mc in range(2):
                    cp("act", mv_b.ap[:, mc, c8 * 128:c8 * 128 + 128], mv_tok.ap[:, mc, c8 * 128:c8 * 128 + 128],
                       [mv_tok.r], [mv_b.r])
            if KNOB["mk"] < 9:
                raise _Stop()
            out_toks.append(dma("sp", o_mv.rearrange("(c p) n -> p c n", p=128), mv_tok.ap, [mv_tok.r], []))

            phase("B0")
            S.barrier()
            s2_base = list(s2)
            s2b_base = list(s2b)
            s4_base = list(s4)
            s4[:] = s4_base + [Buf(view(R3 + i * 4224, [128, TT + 5], F32), R3 + i * 4224) for i in range(2)]
            s2[:] = s2_base + [Buf(view(R3 + 8448 + i * 2048, [128, 512], F32), R3 + 8448 + i * 2048) for i in range(4)]

            def zchunk(cidx, consume):
                sb, w = load_slab(i_win[cidx], 16)
                for bi, (c0, n) in enumerate(BLKS):
                    ps = pb()
                    for k in range(16):
                        mm(ps.ap[:, 0:n], w[:, k, :], xT.ap[:, k, c0:c0 + n], k == 0, k == 15, [sb.r, xT.r], [ps.r])
                    consume(bi, c0, n, ps)

            def b0_chunk(j):
                sb, w = load_slab(i_win[24 + j], 16)
                rxf = t4()
                memset("dve", rxf.ap[:, 0:3], 0.0, [rxf.r])
                for b in range(2):
                    ps = pb()
                    for k in range(16):
                        mm(ps.ap, w[:, k, :], xprevT.ap[:, k, b * 512:b * 512 + 512], k == 0, k == 15,
                           [sb.r, xprevT.r], [ps.r])
                    cp("act", rxf.ap[:, 3 + b * 512:3 + b * 512 + 512], ps.ap, [ps.r], [rxf.r])
                cp("dve", rxhist.ap[:, j, :], rxf.ap[:, 1024:1027], [rxf.r], [rxhist.r])
                xr = t4()
                dwconv4(j, rxf, 1024, xr)
                return xr

            def b0_chunk2(j, xr):
                h0 = hb[0]
                h1 = hb[1]
                a_, u_ = rg_gates(j, xr, [(0, 512), (512, 512)])
                scan(a_[0], u_[0], 512, 0.0, h0, [])
                scan(a_[1], u_[1], 512, h0.ap[:, 511:512], h1, [h0.r])
                ts("dve", hmid.ap[:, j:j + 1], h1.ap[:, 511:512], flag.ap[:, 0:1], None, ALU.mult, None,
                   [h1.r, flag.r], [hmid.r])

            def a_chunk(j):
                ccs = t4()
                cz = t4()
                cbs = t4()
                cy = t4()
                zchunk(8 + j, lambda bi, c0, n, ps: cp("act", ccs.ap[:, c0:c0 + n], ps.ap[:, 0:n], [ps.r], [ccs.r]))
                return ccs, cz, cbs, cy

            def a_chunk2(j, ccs, cz, cbs, cy):
                zchunk(16 + j, lambda bi, c0, n, ps: tt("dve", cz.ap[:, c0:c0 + n], ccs.ap[:, c0:c0 + n],
                                                         ps.ap[:, 0:n], ALU.mult, [ccs.r, ps.r], [cz.r]))
                zchunk(j, lambda bi, c0, n, ps: cp("act", cbs.ap[:, c0:c0 + n], ps.ap[:, 0:n], [ps.r], [cbs.r]))
                w = lambda k: chp.ap[:, j, k:k + 1]
                memset("dve", cy.ap[:, 0:2], 0.0, [cy.r])
                ts("dve", cy.ap[:, 2:CS], cz.ap[:, 2:CS], w(2), None, ALU.mult, None, [cz.r, chp.r], [cy.r])
                stt(cy.ap[:, 2:CS], cz.ap[:, 1:CS - 1], w(1), cy.ap[:, 2:CS], ALU.mult, ALU.add, [cz.r, chp.r, cy.r], [cy.r])
                stt(cy.ap[:, 2:CS], cz.ap[:, 0:CS - 2], w(0), cy.ap[:, 2:CS], ALU.mult, ALU.add, [cz.r, chp.r, cy.r], [cy.r])
                ts("dve", cy.ap[:, CS:TT], cz.ap[:, CS:TT], w(2), None, ALU.mult, None, [cz.r, chp.r], [cy.r])
                stt(cy.ap[:, CS:TT], scz.ap[:, j, 1, :], w(1), cy.ap[:, CS:TT], ALU.mult, ALU.add, [scz.r, chp.r, cy.r], [cy.r])
                stt(cy.ap[:, CS:TT], scz.ap[:, j, 0, :], w(0), cy.ap[:, CS:TT], ALU.mult, ALU.add, [scz.r, chp.r, cy.r], [cy.r])
                tt("dve", ycT.ap[:, j, :], cbs.ap[:, 0:TT], cy.ap[:, 0:TT], ALU.mult, [cbs.r, cy.r], [ycT.r])
                cp("act", ST.ap[:, j, 0:2], cz.ap[:, CS - 2:CS], [cz.r], [ST.r])
                cp("act", ST.ap[:, j, 6:22], cz.ap[:, CS:TT], [cz.r], [ST.r])

            phase("A")
            for j in range(8):
                xr_ = b0_chunk(j)
                abufs = a_chunk(j)
                b0_chunk2(j, xr_)
                a_chunk2(j, *abufs)
            dbg("hmid", hmid, [128, 8])
            dbg("ycT", ycT, [128, 8, TT])

            S.barrier()
            s4[:] = s4_base
            s2[:] = s2_base + [Buf(view(R5 + i * 2048, [128, 512], F32), R5 + i * 2048) for i in range(10)]
            s2b[:] = s2b_base + [Buf(view(R5 + 20480 + i * 1024, [128, 512], BF16), R5 + 20480 + i * 1024) for i in range(4)]

            phase("B")
            for j in range(8):
                rxf = t4()
                cp("dve", rxf.ap[:, 0:3], rxhist.ap[:, j, :], [rxhist.r], [rxf.r])
                rxall = t4()
                zchunk(24 + j, lambda bi, c0, n, ps: cp("act", rxall.ap[:, c0:c0 + n], ps.ap[:, 0:n], [ps.r], [rxall.r]))
                cp("dve", rxf.ap[:, 3:3 + 1024], rxall.ap[:, C0:CS], [rxall.r], [rxf.r])
                xr = t4()
                dwconv4(j, rxf, 1024, xr)
                wk = lambda k: chp.ap[:, j, 3 + k:4 + k]
                ts("dve", xr.ap[:, 1024:1040], rxall.ap[:, CS:TT], wk(3), chp.ap[:, j, 7:8], ALU.mult, ALU.add,
                   [rxall.r, chp.r], [xr.r])
                for k in range(3):
                    stt(xr.ap[:, 1024:1040], src.ap[:, j, k, :], wk(k), xr.ap[:, 1024:1040], ALU.mult, ALU.add,
                        [src.r, chp.r, xr.r], [xr.r])
                gl = t4()

                def gelu_consume(bi, c0, n, ps):
                    x = t2()
                    cp("act", x.ap[:, 0:n], ps.ap[:, 0:n], [ps.r], [x.r])
                    p = t2()
                    tt("pool", p.ap[:, 0:n], x.ap[:, 0:n], x.ap[:, 0:n], ALU.mult, [x.r], [p.r])
                    ts("pool", p.ap[:, 0:n], p.ap[:, 0:n], 0.044715, 1.0, ALU.mult, ALU.add, [p.r], [p.r])
                    tt("pool", p.ap[:, 0:n], p.ap[:, 0:n], x.ap[:, 0:n], ALU.mult, [p.r, x.r], [p.r])
                    act(p.ap[:, 0:n], p.ap[:, 0:n], AF.Exp, [p.r], [p.r], scale=-1.5957691216057308)
                    act(p.ap[:, 0:n], p.ap[:, 0:n], AF.Ln, [p.r], [p.r], bias=1.0)
                    act(p.ap[:, 0:n], p.ap[:, 0:n], AF.Exp, [p.r], [p.r], scale=-1.0)
                    tt("dve", gl.ap[:, c0:c0 + n], p.ap[:, 0:n], x.ap[:, 0:n], ALU.mult, [p.r, x.r], [gl.r])

                zchunk(32 + j, gelu_consume)
                h0 = hb[0]
                h1 = hb[1]
                a_, u_ = rg_gates(j, xr, [(0, 512), (512, 512), (1024, 16)])
                scan(a_[0], u_[0], 512, hmid.ap[:, j:j + 1], h0, [hmid.r])
                tt("dve", yrT.ap[:, j, C0:C0 + 512], h0.ap[:, 0:512], gl.ap[:, C0:C0 + 512], ALU.mult,
                   [h0.r, gl.r], [yrT.r])
                scan(a_[1], u_[1], 512, h0.ap[:, 511:512], h1, [h0.r])
                tt("dve", yrT.ap[:, j, C0 + 512:CS], h1.ap[:, 0:512], gl.ap[:, C0 + 512:CS], ALU.mult,
                   [h1.r, gl.r], [yrT.r])
                cp("act", ST.ap[:, j, 5:6], h1.ap[:, 511:512], [h1.r], [ST.r])
                cp("act", ST.ap[:, j, 2:5], rxall.ap[:, CS - 3:CS], [rxall.r], [ST.r])
                cp("act", ST.ap[:, j, 22:38], rxall.ap[:, CS:TT], [rxall.r], [ST.r])
                hs = t2()
                tt("dve", hs.ap[:, 0:16], a_[2].ap[:, 0:16], sh.ap[:, j, :], ALU.mult, [a_[2].r, sh.r], [hs.r])
                tt("dve", hs.ap[:, 0:16], hs.ap[:, 0:16], u_[2].ap[:, 0:16], ALU.add, [hs.r, u_[2].r], [hs.r])
                cp("act", ST.ap[:, j, 38:54], hs.ap[:, 0:16], [hs.r], [ST.r])
                tt("dve", yrT.ap[:, j, CS:TT], hs.ap[:, 0:16], gl.ap[:, CS:TT], ALU.mult, [hs.r, gl.r], [yrT.r])
                memset("dve", yrT.ap[:, j, 0:C0], 0.0, [yrT.r])
            dbg("yrT", yrT, [128, 8, TT])

            phase("ST")
            stp = [pb(), pb()]
            for j in range(8):
                b = stp[j // 4]
                S.op("pe", lambda e, b=b, j=j: e.transpose(b.ap[0:54, (j % 4) * 128:(j % 4) * 128 + 128],
                                                            ST.ap[:, j, 0:54], ident_f.ap),
                     [ST.r, ident_f.r], [b.r])
            st_sb = alt(t4(), [54, 1024], F32)
            for hh in range(2):
                cp("dve", st_sb.ap[:, hh * 512:hh * 512 + 512], stp[hh].ap[0:54, :], [stp[hh].r], [st_sb.r])
            out_toks.append(dma("sp", o_st, st_sb.ap, [st_sb.r], []))
            out_toks.append(dma("sp", o_czs[:, 0, :], i_scz_tok[:, 1, :], [], []))
            out_toks.append(dma("sp", o_czs[:, 1, :], st_sb.ap[6:22, :], [st_sb.r], []))
            out_toks.append(dma("sp", o_rcs[:, 0:2, :], i_src_tok[:, 1:3, :], [], []))
            out_toks.append(dma("sp", o_rcs[:, 2, :], st_sb.ap[22:38, :], [st_sb.r], []))

            phase("C")
            S.barrier()
            s2[:] = s2_base
            s2b[:] = s2b_base
            for j in range(8):
                zchunk(40 + j, lambda bi, c0, n, ps, j=j: cp("act", qT.ap[:, j, c0:c0 + n], ps.ap[:, 0:n], [ps.r], [qT.r]))
            qs_b = new([16, 1024], BF16)
            for half in range(2):
                ps = pb()
                for c4 in range(4):
                    sb, w = load_slab(i_win[40 + half * 4 + c4], 16)
                    for k in range(16):
                        mm(ps.ap[0:16, c4 * 128:c4 * 128 + 128], xT.ap[:, k, CS:TT], w[:, k, :], k == 0, k == 15,
                           [sb.r, xT.r], [ps.r])
                cp("act", qs_b.ap[:, half * 512:half * 512 + 512], ps.ap[0:16, :], [ps.r], [qs_b.r])
            for h in range(4):
                for (c0, n) in BLKS:
                    pTs = []
                    for kc in range(2):
                        ps = pb()
                        for dc in range(2):
                            mm(ps.ap[:, 0:n], mkT.ap[:, h * 2 + dc, kc * 128:kc * 128 + 128], qT.ap[:, h * 2 + dc, c0:c0 + n],
                               dc == 0, dc == 1, [mkT.r, qT.r], [ps.r])
                        p = t2b()
                        act(p.ap[:, 0:n], ps.ap[:, 0:n], AF.Exp, [ps.r], [p.r], scale=1.0 / 16.0)
                        pTs.append(p)
                    pd = pb()
                    for kc in range(2):
                        mm(pd.ap[:, 0:n], ones_b.ap, pTs[kc].ap[:, 0:n], kc == 0, kc == 1, [ones_b.r, pTs[kc].r], [pd.r])
                    rec = t2()
                    act(rec.ap[:, 0:n], pd.ap[:, 0:n], AF.Ln, [pd.r], [rec.r])
                    act(rec.ap[:, 0:n], rec.ap[:, 0:n], AF.Exp, [rec.r], [rec.r], scale=-1.0)
                    for dc in range(2):
                        po = pb()
                        for kc in range(2):
                            mm(po.ap[:, 0:n], mv_b.ap[:, kc, h * 256 + dc * 128:h * 256 + dc * 128 + 128], pTs[kc].ap[:, 0:n],
                               kc == 0, kc == 1, [mv_b.r, pTs[kc].r], [po.r])
                        tt("dve", yaT.ap[:, h * 2 + dc, c0:c0 + n], po.ap[:, 0:n], rec.ap[:, 0:n], ALU.mult,
                           [po.r, rec.r], [yaT.r])
            yrow = new([1, 1024], BF16)
            pys = banks[7]
            bstate["reserved"].add(7)
            for t in range(16):
                vb = alt(t4(), [128, 2, 1024], BF16)
                dma("pool", vb.ap, i_cv[t].rearrange("(c p) n -> p c n", p=128), [], [vb.r])
                qb = [pb(), pb()]
                for half in range(2):
                    mm(qb[half].ap, sel_b.ap[:, t, :], qs_b.ap[:, half * 512:half * 512 + 512], True, True,
                       [sel_b.r, qs_b.r], [qb[half].r])
                sc = t2()
                for c in range(2):
                    kb = t4()
                    dma("sp", kb.ap[:, 0:1024], i_ck[t, c * 128:c * 128 + 128, :], [], [kb.r])
                    prod = t4()
                    for half in range(2):
                        tt("dve", prod.ap[:, half * 512:half * 512 + 512], kb.ap[:, half * 512:half * 512 + 512],
                           qb[half].ap, ALU.mult, [kb.r, qb[half].r], [prod.r])
                    S.op("dve", lambda e, c=c, sc=sc, prod=prod: e.tensor_reduce(
                        out=sc.ap[:, c * 4:c * 4 + 4], in_=prod.ap[:, 0:1024].rearrange("p (h d) -> p h d", d=256),
                        axis=AX.X, op=ALU.add), [prod.r], [sc.r])
                pp = t2b()
                act(pp.ap[:, 0:8], sc.ap[:, 0:8], AF.Exp, [sc.r], [pp.r], scale=1.0 / 16.0)
                po = [pb(), pb(), pb()]
                for h in range(4):
                    for c in range(2):
                        mm(po[h // 2].ap[0:1, (h % 2) * 256:(h % 2) * 256 + 256], pp.ap[:, c * 4 + h:c * 4 + h + 1],
                           vb.ap[:, c, h * 256:h * 256 + 256], c == 0, c == 1, [pp.r, vb.r], [po[h // 2].r])
                        mm(po[2].ap[0:1, h:h + 1], pp.ap[:, c * 4 + h:c * 4 + h + 1], ones_b.ap[:, 0:1],
                           c == 0, c == 1, [pp.r, ones_b.r], [po[2].r])
                rec = t2()
                S.op("dve", lambda e, rec=rec, po=po: e.reciprocal(out=rec.ap[0:1, 0:4], in_=po[2].ap[0:1, 0:4]),
                     [po[2].r], [rec.r])
                for half in range(2):
                    S.op("dve", lambda e, half=half, rec=rec, po=po: e.tensor_tensor(
                        out=yrow.ap[0:1, half * 512:half * 512 + 512].rearrange("p (h d) -> p h d", d=256),
                        in0=po[half].ap[0:1, :].rearrange("p (h d) -> p h d", d=256),
                        in1=bass.AP(rec.ap.tensor, rec.ap[0:1, half * 2:half * 2 + 2].offset,
                                    [list(rec.ap[0:1, 0:2].ap[0]), [1, 2], [0, 256]]),
                        op=ALU.mult), [po[half].r, rec.r], [yrow.r])
                for j in range(8):
                    mm(pys.ap[:, j * 16 + t:j * 16 + t + 1], yrow.ap[0:1, j * 128:j * 128 + 128], ones_b.ap[0:1, 0:1],
                       True, True, [yrow.r, ones_b.r], [pys.r])
            cp("dve", yaT.ap[:, :, CS:TT], pys.ap[:, 0:128].rearrange("p (j t) -> p j t", t=16), [pys.r], [yaT.r])
            bstate["reserved"].discard(7)
            dbg("yaT", yaT, [128, 8, TT])

            S.barrier()

            phase("D")
            for m in range(16):
                acc = t4()
                for br, (wsrc, yT) in enumerate(((i_wbc, ycT), (i_wbr, yrT), (i_wba, yaT))):
                    sbg, wg = load_slab(i_win[48 + br * 16 + m], 16)
                    sbp, wp = load_slab(wsrc[m], 8)
                    for (c0, n) in BLKS:
                        pg = pb()
                        for k in range(16):
                            mm(pg.ap[:, 0:n], wg[:, k, :], xT.ap[:, k, c0:c0 + n], k == 0, k == 15, [sbg.r, xT.r], [pg.r])
                        pp_ = pb()
                        for k in range(8):
                            mm(pp_.ap[:, 0:n], wp[:, k, :], yT.ap[:, k, c0:c0 + n], k == 0, k == 7, [sbp.r, yT.r], [pp_.r])
                        sg = t2()
                        act(sg.ap[:, 0:n], pg.ap[:, 0:n], AF.Exp, [pg.r], [sg.r], scale=-1.0)
                        act(sg.ap[:, 0:n], sg.ap[:, 0:n], AF.Ln, [sg.r], [sg.r], bias=1.0)
                        act(sg.ap[:, 0:n], sg.ap[:, 0:n], AF.Exp, [sg.r], [sg.r], scale=-1.0)
                        if br == 0:
                            tt("dve", acc.ap[:, c0:c0 + n], sg.ap[:, 0:n], pp_.ap[:, 0:n], ALU.mult, [sg.r, pp_.r], [acc.r])
                        else:
                            tt("dve", sg.ap[:, 0:n], sg.ap[:, 0:n], pp_.ap[:, 0:n], ALU.mult, [sg.r, pp_.r], [sg.r])
                            if br == 1:
                                tt("dve", acc.ap[:, c0:c0 + n], acc.ap[:, c0:c0 + n], sg.ap[:, 0:n], ALU.add,
                                   [acc.r, sg.r], [acc.r])
                            else:
                                tt("dve", mergedT.ap[:, m, c0:c0 + n], acc.ap[:, c0:c0 + n], sg.ap[:, 0:n], ALU.add,
                                   [acc.r, sg.r], [mergedT.r])
            dbg("mergedT", mergedT, [128, 16, TT])

            phase("E")
            TILES = [(C0 + 128 * i, 128, 128 * i) for i in range(8)] + [(CS, 16, 1024)]
            S.barrier()
            vres = view(R1, [128, 9, D], F32)
            vres_r = [Res() for _ in range(9)]
            for i, (col0, nt, row0) in enumerate(TILES):
                dma("sp", vres[0:nt, i, :], i_xtok[row0:row0 + nt, :], [], [vres_r[i]])
            for cb in range(16):
                sb, w = load_slab(i_wo[cb], 16)
                for i, (col0, nt, row0) in enumerate(TILES):
                    ps = pb()
                    for k in range(16):
                        mm(ps.ap[0:nt, 0:128], mergedT.ap[:, k, col0:col0 + nt], w[:, k, :], k == 0, k == 15,
                           [sb.r, mergedT.r], [ps.r])
                    vsl = vres[0:nt, i, cb * 128:cb * 128 + 128]
                    stt(vsl, vsl, ALPHA, ps.ap[0:nt, 0:128], ALU.mult, ALU.add, [vres_r[i], ps.r], [vres_r[i]])

            S.barrier()
            phase("F1")
            state["off"] = PH2
            x1T = Buf(view(R5, [128, 16, NTOK], BF16))
            qpT = Buf(view(R1, [128, 16, NTOK], BF16))
            keysT = new([128, 16, 128], BF16)
            lng = [new([128, D], F32) for _ in range(2)]
            vtile = [new([128, D], F32) for _ in range(2)]
            stats = new([128, 24], F32)
            mv2 = new([128, 2], F32)
            rstd = new([128, 1], F32)
            F_ONLY = state["off"]
            slabs[:] = [new([128, 16 * 128], BF16) for _ in range(4)]
            x1b = new([128, D], BF16)
            conv_step(NCONV)

            dma("pool", keysT.ap, i_keysT.rearrange("p (a b) -> p a b", b=128), [], [keysT.r])

            def layernorm(vb, nt, gi, out_buf):
                for q4 in range(4):
                    S.op("dve", lambda e, q4=q4: e.bn_stats(out=stats.ap[0:nt, q4 * 6:q4 * 6 + 6], in_=vb.ap[0:nt, q4 * 512:q4 * 512 + 512]),
                         [vb.r], [stats.r])
                S.op("dve", lambda e: e.bn_aggr(out=mv2.ap[0:nt, :], in_=stats.ap[0:nt, :]), [stats.r], [mv2.r])
                ts("dve", rstd.ap[0:nt, :], mv2.ap[0:nt, 1:2], LN_EPS, None, ALU.add, None, [mv2.r], [rstd.r])
                act(rstd.ap[0:nt, :], rstd.ap[0:nt, :], AF.Ln, [rstd.r], [rstd.r])
                act(rstd.ap[0:nt, :], rstd.ap[0:nt, :], AF.Exp, [rstd.r], [rstd.r], scale=-0.5)
                ts("dve", out_buf.ap[0:nt, :], vb.ap[0:nt, :], mv2.ap[0:nt, 0:1], rstd.ap[0:nt, 0:1], ALU.subtract, ALU.mult,
                   [vb.r, mv2.r, rstd.r], [out_buf.r])
                tt("dve", out_buf.ap[0:nt, :], out_buf.ap[0:nt, :], lng[0].ap[0:nt, :], ALU.mult, [out_buf.r, lng[0].r], [out_buf.r])
                tt("dve", out_buf.ap[0:nt, :], out_buf.ap[0:nt, :], lng[1].ap[0:nt, :], ALU.add, [out_buf.r, lng[1].r], [out_buf.r])

            dma("sp", lng[0].ap, bass.AP(i_ln.tensor, 0 * D, [[0, 128], [1, D]]), [], [lng[0].r])
            dma("sp", lng[1].ap, bass.AP(i_ln.tensor, 1 * D, [[0, 128], [1, D]]), [], [lng[1].r])
            TILES2 = [(128 * i, 128) for i in range(8)] + [(1024, 16)]
            def ln1_tile(ti, row0, nt):
                    vb = Buf(vres[:, ti, :], None, vres_r[ti])
                    layernorm(vb, nt, 0, vb)
                    dma("sp", x1scr[row0:row0 + nt, :], vb.ap[0:nt, :], [vb.r], [])
                    cp("act", x1b.ap[0:nt, :], vb.ap[0:nt, :], [vb.r], [x1b.r])
                    for g in range(4):
                        ps = pb()
                        psb = ps.ap.bitcast(BF16)
                        for kk in range(4):
                            k = g * 4 + kk
                            S.op("pe", lambda e, k=k, kk=kk, psb=psb: e.transpose(
                                psb[:, kk * 128:kk * 128 + nt], x1b.ap[0:nt, k * 128:k * 128 + 128], ident_b.ap[0:nt, 0:nt]),
                                [x1b.r, ident_b.r], [ps.r])
                        cp("dve", x1T.ap[:, g * 4:g * 4 + 4, row0:row0 + nt],
                           psb[:, 0:512].rearrange("p (a b) -> p a b", b=128)[:, :, 0:nt], [ps.r], [x1T.r])

            for ti, (row0, nt) in enumerate(TILES2):
                ln1_tile(ti, row0, nt)

            phase("F2")
            S.barrier()
            QBL = [(0, 512), (512, 512), (1024, 16)]
            for hc in range(16):
                sb, w = load_slab(i_wq[hc], 16)
                for (c0, n) in QBL:
                    ps = pb()
                    for k in range(16):
                        mm(ps.ap[:, 0:n], w[:, k, :], x1T.ap[:, k, c0:c0 + n], k == 0, k == 15, [sb.r, x1T.r], [ps.r])
                    cp("act", qpT.ap[:, hc, c0:c0 + n], ps.ap[:, 0:n], [ps.r], [qpT.r])

            phase("G")
            dma("sp", lng[0].ap, bass.AP(i_ln.tensor, 2 * D, [[0, 128], [1, D]]), [], [lng[0].r])
            dma("sp", lng[1].ap, bass.AP(i_ln.tensor, 3 * D, [[0, 128], [1, D]]), [], [lng[1].r])
            S.barrier()
            state["off"] = F_ONLY
            state["lim"] = ARENA_BYTES
            scs = new([128, D], F32)
            scs2 = new([128, D], F32)
            vals1 = new([128, 16, 16], F32)
            idx1 = new([128, 16, 16], U32)
            idx1f = new([128, 16, 16], F32)
            vals2 = new([128, 8, 16], F32)
            pos = new([128, 8, 16], U32)
            posf = new([128, 128], F32)
            posAf = new([128, 128], F32)
            posBf = new([128, 128], F32)
            thr16 = new([128, 16], F32)
            ts("dve", thr16.ap, iota16.ap, 16.0, 16.0, ALU.mult, ALU.add, [iota16.r], [thr16.r])
            selI = new([128, 128], F32)
            selJ = new([128, 128], F32)
            gw2 = [new([128, 128], F32) for _ in range(2)]
            gsum = new([128, 8], F32)
            actv = new([128, 128], F32)
            diag = [new([128, 128], BF16) for _ in range(4)]
            state["off"] = R2
            state["lim"] = R5 + 16 * TT * 2
            NPB = 6
            NVB = 7
            pbufs = [new([128, 2 * D], BF16) for _ in range(NPB)]
            pbV_r = [Res() for _ in range(NPB)]
            vgb = [new([128, D], BF16) for _ in range(NVB)]
            eidx2 = [new([128, 128], U32) for _ in range(2)]
            cco2 = [new([128, 128], F32) for _ in range(2)]
            tA = new([128, 128], F32)
            gA = new([128, 128], F32)
            rA = new([128, 128], F32)

            def bc(buf, off_elems, dims):
                return bass.AP(buf.ap.tensor, buf.ap.offset + off_elems, [list(buf.ap.ap[0])] + dims)

            def top16(src_ap, src_res, scratch_ap, scratch_res, vout, iout, res_v, res_i, nt):
                S.op("dve", lambda e: e.max(out=vout[0:nt, 0:8], in_=src_ap), [src_res], [res_v])
                S.op("dve", lambda e: e.max_index(out=iout[0:nt, 0:8], in_max=vout[0:nt, 0:8], in_values=src_ap),
                     [src_res, res_v], [res_i])
                S.op("dve", lambda e: e.match_replace(out=scratch_ap, in_to_replace=vout[0:nt, 0:8], in_values=src_ap,
                                                      imm_value=NEG), [src_res, res_v], [scratch_res])
                S.op("dve", lambda e: e.max(out=vout[0:nt, 8:16], in_=scratch_ap), [scratch_res], [res_v])
                S.op("dve", lambda e: e.max_index(out=iout[0:nt, 8:16], in_max=vout[0:nt, 8:16], in_values=scratch_ap),
                     [scratch_res, res_v], [res_i])

            def route(ti, row0, nt):
                    x1 = vtile[ti % 2]
                    eidx = eidx2[ti % 2]
                    gw = gw2[ti % 2]
                    dma("sp", x1.ap[0:nt, :], x1scr[row0:row0 + nt, :], [], [x1.r])
                    for g in range(4):
                        ps = banks[g]
                        for kk in range(4):
                            hc = g * 4 + kk
                            mm(ps.ap[0:nt, kk * 128:kk * 128 + 128], qpT.ap[:, hc, row0:row0 + nt], keysT.ap[:, hc, :], True, True,
                               [qpT.r, keysT.r], [ps.r])
                        cp("act", scs.ap[0:nt, g * 512:g * 512 + 512], ps.ap[0:nt, :], [ps.r], [scs.r])
                    for hc in range(16):
                        top16(scs.ap[0:nt, hc * 128:hc * 128 + 128], scs.r, scs2.ap[0:nt, hc * 128:hc * 128 + 128], scs2.r,
                              vals1.ap[:, hc, :], idx1.ap[:, hc, :], vals1.r, idx1.r, nt)
                        yield
                    cp("dve", idx1f.ap[0:nt], idx1.ap[0:nt], [idx1.r], [idx1f.r])
                    pstep = [list(vals1.ap.ap[0])[0], nt]
                    S.op("dve", lambda e, pstep=pstep: e.tensor_tensor(
                        out=scs.ap[0:nt, :].rearrange("p (h a b) -> p h a b", a=16, b=16),
                        in0=bass.AP(vals1.ap.tensor, vals1.ap.offset, [pstep, [32, 8], [1, 16], [0, 16]]),
                        in1=bass.AP(vals1.ap.tensor, vals1.ap.offset + 16, [pstep, [32, 8], [0, 16], [1, 16]]),
                        op=ALU.add), [vals1.r, scs.r], [scs.r])
                    for h in range(8):
                        top16(scs.ap[0:nt, h * 256:h * 256 + 256], scs.r, scs2.ap[0:nt, h * 256:h * 256 + 256], scs2.r,
                              vals2.ap[:, h, :], pos.ap[:, h, :], vals2.r, pos.r, nt)
                        yield
                    cp("dve", posf.ap[0:nt, :], pos.ap[0:nt].rearrange("p h k -> p (h k)"), [pos.r], [posf.r])
                    pfp = [list(posf.ap.ap[0])[0], nt]
                    tpp = [list(thr16.ap.ap[0])[0], nt]
                    S.op("dve", lambda e, pfp=pfp, tpp=tpp: e.tensor_tensor(
                        out=scs2.ap[0:nt, :].rearrange("p (s a) -> p s a", a=16),
                        in0=bass.AP(posf.ap.tensor, posf.ap.offset, [pfp, [1, 128], [0, 16]]),
                        in1=bass.AP(thr16.ap.tensor, thr16.ap.offset, [tpp, [0, 128], [1, 16]]),
                        op=ALU.is_ge), [posf.r, thr16.r, scs2.r], [scs2.r])
                    S.op("dve", lambda e: e.tensor_reduce(
                        out=posAf.ap[0:nt, :], in_=scs2.ap[0:nt, :].rearrange("p (s a) -> p s a", a=16),
                        axis=AX.X, op=ALU.add), [scs2.r], [posAf.r])
                    stt(posBf.ap[0:nt, :], posAf.ap[0:nt, :], -16.0, posf.ap[0:nt, :], ALU.mult, ALU.add,
                        [posAf.r, posf.r], [posBf.r])
                    ipart = [list(iota16.ap.ap[0])[0], nt]
                    for (pf, half_off, outb) in ((posAf, 0, selI), (posBf, 16, selJ)):
                        pp2 = [list(pf.ap.ap[0])[0], nt]
                        S.op("dve", lambda e, pf=pf, pp2=pp2: e.tensor_tensor(
                            out=scs2.ap[0:nt, :].rearrange("p (s a) -> p s a", a=16),
                            in0=bass.AP(pf.ap.tensor, pf.ap.offset, [pp2, [1, 128], [0, 16]]),
                            in1=bass.AP(iota16.ap.tensor, iota16.ap.offset, [ipart, [0, 128], [1, 16]]),
                            op=ALU.is_equal), [pf.r, iota16.r, scs2.r], [scs2.r])
                        ip2 = [list(idx1f.ap.ap[0])[0], nt]
                        S.op("dve", lambda e, half_off=half_off, ip2=ip2: e.tensor_tensor(
                            out=scs2.ap[0:nt, :].rearrange("p (h k a) -> p h k a", k=16, a=16),
                            in0=scs2.ap[0:nt, :].rearrange("p (h k a) -> p h k a", k=16, a=16),
                            in1=bass.AP(idx1f.ap.tensor, idx1f.ap.offset + half_off, [ip2, [32, 8], [0, 16], [1, 16]]),
                            op=ALU.mult), [scs2.r, idx1f.r], [scs2.r])
                        S.op("dve", lambda e, outb=outb: e.tensor_reduce(
                            out=outb.ap[0:nt, :], in_=scs2.ap[0:nt, :].rearrange("p (s a) -> p s a", a=16),
                            axis=AX.X, op=ALU.add), [scs2.r], [outb.r])
                    stt(selI.ap[0:nt, :], selI.ap[0:nt, :], 128.0, selJ.ap[0:nt, :], ALU.mult, ALU.add, [selI.r, selJ.r], [selI.r])
                    yield
                    cp("dve", eidx.ap[0:nt, :], selI.ap[0:nt, :], [selI.r], [eidx.r])
                    yield
                    vp = [list(vals2.ap.ap[0])[0], nt]
                    S.op("dve", lambda e, vp=vp: e.tensor_tensor(
                        out=gw.ap[0:nt, :].rearrange("p (h k) -> p h k", k=16), in0=vals2.ap[0:nt],
                        in1=bass.AP(vals2.ap.tensor, vals2.ap.offset, [vp, [16, 8], [0, 16]]), op=ALU.subtract),
                        [vals2.r], [gw.r])
                    act(gw.ap[0:nt, :], gw.ap[0:nt, :], AF.Exp, [gw.r], [gw.r])
                    S.op("dve", lambda e: e.tensor_reduce(out=gsum.ap[0:nt, :], in_=gw.ap[0:nt, :].rearrange("p (h k) -> p h k", k=16),
                                                          axis=AX.X, op=ALU.add), [gw.r], [gsum.r])
                    S.op("dve", lambda e: e.reciprocal(out=gsum.ap[0:nt, :], in_=gsum.ap[0:nt, :]), [gsum.r], [gsum.r])
                    gp = [list(gsum.ap.ap[0])[0], nt]
                    S.op("dve", lambda e, gp=gp: e.tensor_tensor(
                        out=gw.ap[0:nt, :].rearrange("p (h k) -> p h k", k=16), in0=gw.ap[0:nt, :].rearrange("p (h k) -> p h k", k=16),
                        in1=bass.AP(gsum.ap.tensor, gsum.ap.offset, [gp, [1, 8], [0, 16]]), op=ALU.mult),
                        [gw.r, gsum.r], [gw.r])
                    yield


            ybanks = banks[4:8]
            GS = 2
            NG = 128 // GS
            ucnt = {"i": 0}
            vcnt = {"i": 0}
            dcnt = {"i": 0}

            selm = new([128, 16], F32)
            selc = [new([128, 16], BF16) for _ in range(4)]
            eidxP = new([128, 16], U32)
            gwP = new([128, 16], F32)
            dma("sp", selm.ap, i_selm, [], [selm.r])

            def peer_tile(ti, row0, nt, gen, packed=False):
                x1 = vtile[ti % 2]
                eidx = eidx2[ti % 2]
                gw = gw2[ti % 2]
                cco = cco2[ti % 2]
                P = nt
                nslots = 128
                xd = x1
                if packed:
                    P = 128
                    nslots = 16
                    xd = alt(scs, [128, D], F32)
                    for h in range(8):
                        dma("sp", xd.ap[h * 16:h * 16 + 16, :], x1scr[row0:row0 + 16, :], [], [xd.r])
                        dma("sp", eidxP.ap[h * 16:h * 16 + 16, :], eidx.ap[0:16, h * 16:h * 16 + 16], [eidx.r], [eidxP.r])
                        dma("sp", gwP.ap[h * 16:h * 16 + 16, :], gw.ap[0:16, h * 16:h * 16 + 16], [gw.r], [gwP.r])
                    eidx = eidxP
                    gw = gwP
                ng = nslots // GS
                actv_r = [Res() for _ in range(ng)]
                tA_r = [Res() for _ in range(ng)]
                gA_r = [Res() for _ in range(ng)]
                rA_r = [Res() for _ in range(ng)]
                cco_r = [Res() for _ in range(ng)]
                vslots = {}

                def stage_c(g):
                    cs = slice(g * GS, g * GS + GS)
                    tt("dve", cco.ap[0:P, cs], rA.ap[0:P, cs], gA.ap[0:P, cs], ALU.mult, [rA_r[g], gA_r[g], cco.r], [cco_r[g]])
                    for s_ in range(g * GS, g * GS + GS):
                        vb = vslots.pop(s_)
                        if packed:
                            dg = selc[dcnt["i"] % 4]
                            dcnt["i"] += 1
                            S.op("act", lambda e, dg=dg, s_=s_: e.activation(
                                out=dg.ap[:, :], in_=selm.ap[:, :], func=AF.Copy,
                                scale=cco.ap[0:P, s_:s_ + 1]), [selm.r, cco_r[g]], [dg.r])
                            lhs = dg.ap[:, :]
                        else:
                            dg = diag[dcnt["i"] % 4]
                            dcnt["i"] += 1
                            S.op("act", lambda e, dg=dg, s_=s_: e.activation(
                                out=dg.ap[0:nt, 0:nt], in_=ident_f.ap[0:nt, 0:nt], func=AF.Copy,
                                scale=cco.ap[0:nt, s_:s_ + 1]), [ident_f.r, cco_r[g]], [dg.r])
                            lhs = dg.ap[0:nt, 0:nt]
                        for q4 in range(4):
                            mm(ybanks[q4].ap[0:nt, :], lhs, vb.ap[0:P, q4 * 512:q4 * 512 + 512],
                               s_ == 0, s_ == nslots - 1, [dg.r, vb.r], [ybanks[q4].r])

                for s_ in range(nslots):
                    g = s_ // GS
                    pi_ = ucnt["i"] % NPB
                    pb_ = pbufs[pi_]
                    rV = pbV_r[pi_]
                    ucnt["i"] += 1
                    vb = vgb[vcnt["i"] % NVB]
                    vcnt["i"] += 1
                    vslots[s_] = vb
                    S.dma("pool", lambda e, pb_=pb_, s_=s_: e.indirect_dma_start(
                        out=pb_.ap[0:P, :], out_offset=None, in_=puv_bf,
                        in_offset=bass.IndirectOffsetOnAxis(ap=eidx.ap[0:P, s_:s_ + 1], axis=0)),
                        [eidx.r], [pb_.r, rV])
                    stt(pb_.ap[0:P, 0:D], pb_.ap[0:P, 0:D], 1.0, xd.ap[0:P, :], ALU.mult, ALU.mult,
                        [pb_.r, xd.r], [pb_.r] + ([actv_r[g]] if s_ % GS == GS - 1 else []),
                        accum_out=actv.ap[0:P, s_:s_ + 1])
                    cp("act", vb.ap[0:P, :], pb_.ap[0:P, D:2 * D], [rV], [vb.r])
                    if s_ % GS == GS - 1:
                        cs = slice(g * GS, g * GS + GS)
                        a_ = actv.ap[0:P, cs]
                        tt("dve", tA.ap[0:P, cs], a_, a_, ALU.mult, [actv_r[g]], [tA_r[g]])
                        ts("dve", tA.ap[0:P, cs], tA.ap[0:P, cs], 0.044715, 1.0, ALU.mult, ALU.add, [tA_r[g]], [tA_r[g]])
                        tt("dve", tA.ap[0:P, cs], tA.ap[0:P, cs], a_, ALU.mult, [tA_r[g], actv_r[g]], [tA_r[g]])
                        tt("dve", gA.ap[0:P, cs], gw.ap[0:P, cs], a_, ALU.mult, [gw.r, actv_r[g]], [gA_r[g]])
                        act(rA.ap[0:P, cs], tA.ap[0:P, cs], AF.Exp, [tA_r[g]], [rA_r[g]], scale=-1.5957691216057308)
                        act(rA.ap[0:P, cs], rA.ap[0:P, cs], AF.Ln, [rA_r[g]], [rA_r[g]], bias=1.0)
                        act(rA.ap[0:P, cs], rA.ap[0:P, cs], AF.Exp, [rA_r[g]], [rA_r[g]], scale=-1.0)
                        if g >= 1:
                            stage_c(g - 1)
                    if s_ >= 8 and s_ % 2 == 0:
                        next(gen, None)
                stage_c(ng - 1)
                for _ in gen:
                    pass
                for q4 in range(4):
                    stt(x1.ap[0:nt, q4 * 512:q4 * 512 + 512], x1.ap[0:nt, q4 * 512:q4 * 512 + 512], ALPHA,
                        ybanks[q4].ap[0:nt, :], ALU.mult, ALU.add, [x1.r, ybanks[q4].r], [x1.r])
                layernorm(x1, nt, 1, x1)
                out_toks.append(dma("sp", o_y[row0:row0 + nt, :], x1.ap[0:nt, :], [x1.r], []))

            NT_ = len(TILES2)
            for _ in route(0, *TILES2[0]):
                pass
            for it in range(NT_):
                gen = route(it + 1, *TILES2[it + 1]) if it + 1 < NT_ else iter(())
                peer_tile(it, TILES2[it][0], TILES2[it][1], gen, packed=(TILES2[it][1] == 16))

        try:
            body()
        except _Stop:
            pass
        out_toks += list(dbg_out.values())
        S.wait_all("sp", out_toks)
        S.barrier()
        with nc.Block() as block:
            S.emit(block)
    return nc


def _tile_w(w, kc):
    n = w.shape[1] // 128
    return np.ascontiguousarray(w.reshape(kc, 128, n, 128).transpose(2, 1, 0, 3)).reshape(n, 128, kc * 128)


def _fm(a):
    lead = a.shape[:-1]
    a = a.reshape(-1, 8, 128)
    return np.ascontiguousarray(a.transpose(2, 1, 0)).reshape(128, 8, *lead)


_PROGRAM = {}


def prep_inputs(inp):
    f = np.float32
    g = lambda k: np.asarray(inp[k], dtype=f)
    x_prompt, x_sample, mem = g("x_prompt"), g("x_sample"), g("mem_prompt")
    shared = {
        "win": _tile_w(g("w_in")[0], 16),
        "wa": np.ascontiguousarray(g("rg_wa")[0].transpose(1, 0, 2)).reshape(128, 1024),
        "wx": np.ascontiguousarray(g("rg_wx")[0].transpose(1, 0, 2)).reshape(128, 1024),
        "wmk": _tile_w(g("w_mk")[0], 16),
        "wmv": _tile_w(g("w_mv")[0], 16),
        "wbc": _tile_w(g("w_br_conv")[0], 8),
        "wbr": _tile_w(g("w_br_rnn")[0], 8),
        "wba": _tile_w(g("w_br_attn")[0], 8),
        "wo": _tile_w(g("w_o")[0], 16),
        "wq": _tile_w(g("peer_wq")[0], 16),
        "keysT": np.ascontiguousarray(g("peer_keys")[0].reshape(16, 128, 128).transpose(2, 0, 1)).reshape(128, 2048),
        "ln": np.stack([g("ln1_g")[0], g("ln1_b")[0], g("ln2_g")[0], g("ln2_b")[0]]),
        "pu": g("peer_u")[0],
        "pv": g("peer_v")[0],
        "ident": np.eye(128, dtype=f),
        "sel": np.ascontiguousarray(np.broadcast_to(np.eye(16, dtype=f)[:, :, None], (16, 16, 128))).reshape(16, 2048),
        "iota16": np.ascontiguousarray(np.broadcast_to(np.arange(16, dtype=f)[None, :], (128, 16))),
        "selm": np.ascontiguousarray(np.tile(np.eye(16, dtype=f), (8, 1))),
    }
    chp = np.concatenate([_fm(g("conv_w")[0]), _fm(g("rg_conv_w")[0]), _fm(g("rg_conv_b")), _fm(g("rg_ba")),
                          _fm(g("rg_bx")), _fm(g("rg_lambda"))], axis=2)
    shared["chp"] = np.ascontiguousarray(chp).reshape(128, 88)
    maps = []
    for c in range(NCORES):
        b, half = c // 2, c % 2
        cur = x_prompt[b, half * 1024:(half + 1) * 1024]
        prev = x_prompt[b, 0:1024] if half == 1 else np.zeros((1024, D), f)
        xs = x_sample[c * 16:(c + 1) * 16, 0]
        xall = np.concatenate([prev[-3:], cur, xs], axis=0)
        m = dict(shared)
        m["xT"] = np.ascontiguousarray(xall.T)
        m["xprevT"] = np.ascontiguousarray(prev.T)
        m["xtok"] = np.ascontiguousarray(np.concatenate([cur, xs], axis=0))
        m["flag"] = np.full((128, 1), float(half), f)
        m["memT"] = np.ascontiguousarray(mem[b].T)
        m["ck"] = np.ascontiguousarray(g("cache_mem_k")[0, c * 16:(c + 1) * 16].reshape(16, 256, 1024))
        m["cv"] = np.ascontiguousarray(g("cache_mem_v")[0, c * 16:(c + 1) * 16].reshape(16, 256, 1024))
        scz = g("state_conv_z")[0, c * 16:(c + 1) * 16]
        src = g("state_rglru_conv")[0, c * 16:(c + 1) * 16]
        sh = g("state_rglru_h")[0, c * 16:(c + 1) * 16]
        m["scz"] = np.ascontiguousarray(_fm(scz.transpose(1, 0, 2))).reshape(128, 8 * 2 * 16)
        m["src"] = np.ascontiguousarray(_fm(src.transpose(1, 0, 2))).reshape(128, 8 * 3 * 16)
        m["sh"] = np.ascontiguousarray(_fm(sh)).reshape(128, 8 * 16)
        m["scz_tok"] = np.ascontiguousarray(scz)
        m["src_tok"] = np.ascontiguousarray(src)
        maps.append(m)
    return maps


def assemble(results):
    f = np.float32
    y_prompt = np.zeros((4, 2048, D), f)
    y_sample = np.zeros((128, 1, D), f)
    mk = np.zeros((1, 4, 256, 4, 256), f)
    mv = np.zeros((1, 4, 256, 4, 256), f)
    czp = np.zeros((1, 4, 2, 1024), f)
    rcp = np.zeros((1, 4, 3, 1024), f)
    hp = np.zeros((1, 4, 1024), f)
    czs = np.zeros((1, 128, 2, 1024), f)
    rcs = np.zeros((1, 128, 3, 1024), f)
    hs = np.zeros((1, 128, 1024), f)
    for c, r in enumerate(results):
        b, half = c // 2, c % 2
        y = r["y"]
        y_prompt[b, half * 1024:(half + 1) * 1024] = y[0:1024]
        y_sample[c * 16:(c + 1) * 16, 0] = y[1024:1040]
        stv = r["st_out"]
        if half == 1:
            czp[0, b] = stv[0:2]
            rcp[0, b] = stv[2:5]
            hp[0, b] = stv[5]
        else:
            mk[0, b] = r["mk_out"].reshape(256, 4, 256)
            mv[0, b] = r["mv_out"].reshape(256, 4, 256)
        czs[0, c * 16:(c + 1) * 16] = r["czs_out"]
        rcs[0, c * 16:(c + 1) * 16] = r["rcs_out"]
        hs[0, c * 16:(c + 1) * 16] = stv[38:54]
    return (y_prompt, y_sample, mk, mv, czp, rcp, hp, czs, rcs, hs)


def kernel(**inputs):
    if "nc" not in _PROGRAM:
        _PROGRAM["nc"] = build_program()
    maps = prep_inputs(inputs)
    res = run_bass_kernel_spmd(_PROGRAM["nc"], maps, core_ids=list(range(NCORES)))
    return assemble(res.results)
```

```python
from contextlib import ExitStack
import numpy as np
import concourse.bass as bass
import concourse.mybir as mybir
from concourse.bass_utils import run_bass_kernel_spmd

F32 = mybir.dt.float32
BF16 = mybir.dt.bfloat16
I32 = mybir.dt.int32
U32 = mybir.dt.uint32
U8 = mybir.dt.uint8
ALU = mybir.AluOpType
AF = mybir.ActivationFunctionType
AX = mybir.AxisListType

NCORES = 8
D = 2048
TT = 1043
C0 = 3
CS = 1027
NTOK = 1040
BLKS = [(0, 512), (512, 512), (1024, 19)]
ALPHA = 2.0 ** 0.25
LN_EPS = 1e-5
NEG = -1.0e30


class Res:
    __slots__ = ("w", "rs", "x")

    def __init__(self):
        self.w = None
        self.rs = {}
        self.x = False


class Stream:
    __slots__ = ("name", "sem", "inc", "n")

    def __init__(self, name, sem, inc):
        self.name = name
        self.sem = sem
        self.inc = inc
        self.n = 0


class Sched:
    ENGS = ("pe", "act", "dve", "pool", "sp")

    def __init__(self, nc, stack):
        self.nc = nc
        self.items = {e: [] for e in self.ENGS}
        self.clock = {e: {} for e in self.ENGS}
        self.cstream = {}
        self.all_streams = []
        for e in self.ENGS:
            if e == "sp":
                continue
            s = Stream(e, stack.enter_context(nc.semaphore("c_" + e)), 1)
            self.cstream[e] = s
            self.all_streams.append(s)
        self.dstreams = {}
        self.dcount = {}
        for q, k in (("sp", 8), ("pool", 8), ("act", 2)):
            self.dstreams[q] = [Stream(f"d_{q}{i}", stack.enter_context(nc.semaphore(f"d_{q}{i}")), 16)
                                for i in range(k)]
            self.all_streams += self.dstreams[q]
            self.dcount[q] = 0
        self.nops = 0

    def _collect(self, eng, reads, writes, is_dma=False):
        need = {}

        def add(tok):
            if tok is None:
                return
            s, n, _ = tok
            if need.get(s, (0, None))[0] < n:
                need[s] = (n, tok)

        own = self.cstream.get(eng)
        for r in reads:
            add(r.w)
            if r.x:
                for t in r.rs.values():
                    if t[0] is not own:
                        add(t)
        for w in writes:
            add(w.w)
            for t in w.rs.values():
                if eng == "pe" and (not is_dma) and t[0] is own:
                    continue
                add(t)
        clk = self.clock[eng]
        for s, (n, tok) in need.items():
            if clk.get(s.name, 0) >= n:
                continue
            if eng == "pe" and s is own:
                continue
            self.items[eng].append(("w", s.sem, n * s.inc))
            for k, v in tok[2].items():
                if clk.get(k, 0) < v:
                    clk[k] = v
            clk[s.name] = n

    def _finish(self, eng, stream, fn, reads, writes):
        stream.n += 1
        tok = (stream, stream.n, dict(self.clock[eng]))
        self.items[eng].append(("i", fn, stream.sem, stream.inc))
        for r in reads:
            r.rs[stream] = tok
        for w in writes:
            w.w = tok
            w.rs = {}
        self.nops += 1
        return tok

    def op(self, eng, fn, reads=(), writes=()):
        self._collect(eng, reads, writes)
        return self._finish(eng, self.cstream[eng], fn, reads, writes)

    def dma(self, q, fn, reads=(), writes=()):
        ds = self.dstreams[q]
        st = ds[self.dcount[q] % len(ds)]
        self.dcount[q] += 1
        clk = self.clock[q]
        if st.n > 0 and clk.get(st.name, 0) < st.n:
            self.items[q].append(("w", st.sem, st.n * st.inc))
            clk[st.name] = st.n
        self._collect(q, reads, writes, is_dma=True)
        return self._finish(q, st, fn, reads, writes)

    def wait_all(self, eng, toks):
        clk = self.clock[eng]
        for tok in toks:
            s, n, _ = tok
            if n == 0 or clk.get(s.name, 0) >= n:
                continue
            self.items[eng].append(("w", s.sem, n * s.inc))
            clk[s.name] = n

    def barrier(self):
        toks = [(s, s.n, {}) for s in self.all_streams]
        for e in self.ENGS:
            self.wait_all(e, toks)

    def emit(self, block):
        def run(e, items):
            for it in items:
                if it[0] == "w":
                    e.wait_ge(it[1], it[2])
                else:
                    it[1](e).then_inc(it[2], it[3])

        items = self.items

        @block.sync
        def _(e):
            run(e, items["sp"])

        @block.tensor
        def _(e):
            run(e, items["pe"])

        @block.scalar
        def _(e):
            run(e, items["act"])

        @block.vector
        def _(e):
            run(e, items["dve"])

        @block.gpsimd
        def _(e):
            run(e, items["pool"])


class Buf:
    __slots__ = ("ap", "r", "off")

    def __init__(self, ap, off=None, r=None):
        self.ap = ap
        self.r = r if r is not None else Res()
        self.off = off


class _Stop(Exception):
    pass


KNOB = {"mk": 9, "conv_from": 24, "conv_n": None}
PHASES = ["SETUP", "MK", "B0", "A", "B", "ST", "C", "D", "E", "F1", "F2", "G"]


def build_program(debug=(), stop_after="G"):
    nc = bass.Bass("TRN2", target_bir_lowering=False)

    def din(name, shape, dt=F32):
        return nc.dram_tensor(name, list(shape), dt, kind="ExternalInput").ap()

    def dout(name, shape, dt=F32):
        return nc.dram_tensor(name, list(shape), dt, kind="ExternalOutput").ap()

    i_xT = din("xT", [D, TT])
    i_xprevT = din("xprevT", [D, 1024])
    i_xtok = din("xtok", [NTOK, D])
    i_flag = din("flag", [128, 1])
    i_memT = din("memT", [D, 256])
    i_ck = din("ck", [16, 256, 1024])
    i_cv = din("cv", [16, 256, 1024])
    i_scz = din("scz", [128, 8 * 2 * 16])
    i_src = din("src", [128, 8 * 3 * 16])
    i_sh = din("sh", [128, 8 * 16])
    i_scz_tok = din("scz_tok", [16, 2, 1024])
    i_src_tok = din("src_tok", [16, 3, 1024])
    i_chp = din("chp", [128, 8 * 11])
    i_win = din("win", [96, 128, 16 * 128])
    i_wa = din("wa", [128, 8 * 128])
    i_wx = din("wx", [128, 8 * 128])
    i_wmk = din("wmk", [8, 128, 16 * 128])
    i_wmv = din("wmv", [8, 128, 16 * 128])
    i_wbc = din("wbc", [16, 128, 8 * 128])
    i_wbr = din("wbr", [16, 128, 8 * 128])
    i_wba = din("wba", [16, 128, 8 * 128])
    i_wo = din("wo", [16, 128, 16 * 128])
    i_wq = din("wq", [16, 128, 16 * 128])
    i_keysT = din("keysT", [128, 16 * 128])
    i_ln = din("ln", [4, D])
    i_pu = din("pu", [16384, D])
    i_pv = din("pv", [16384, D])
    i_ident = din("ident", [128, 128])
    i_sel = din("sel", [16, 16 * 128])
    i_iota = din("iota16", [128, 16])
    i_selm = din("selm", [128, 16])

    o_y = dout("y", [NTOK, D])
    o_mk = dout("mk_out", [256, 1024])
    o_mv = dout("mv_out", [256, 1024])
    o_st = dout("st_out", [54, 1024])
    o_czs = dout("czs_out", [16, 2, 1024])
    o_rcs = dout("rcs_out", [16, 3, 1024])
    dbg_out = {}

    vscr = nc.dram_tensor("vscr", [NTOK, D], F32, kind="Internal").ap()
    x1scr = nc.dram_tensor("x1scr", [NTOK, D], F32, kind="Internal").ap()
    puv_bf = nc.dram_tensor("puv_bf", [16384, 2 * D], BF16, kind="Internal").ap()

    st = ExitStack()
    with st:
        S = Sched(nc, st)
        ARENA_BYTES = 207 * 1024
        arena_t = st.enter_context(nc.sbuf_tensor("arena", [128, ARENA_BYTES], U8))
        state = {"off": 0, "lim": ARENA_BYTES}

        def alloc(nbytes):
            off = state["off"]
            state["off"] = off + (nbytes + 63) // 64 * 64
            assert state["off"] <= state["lim"], ("arena overflow", state["off"], state["lim"])
            return off

        def view(off, shape, dt, parts=128):
            es = 2 if dt == BF16 else 4
            n = int(np.prod(shape[1:]))
            v = arena_t[0:shape[0], off:off + n * es].bitcast(dt)
            if len(shape) == 3:
                v = v.rearrange("p (a b) -> p a b", b=shape[2])
            elif len(shape) == 4:
                v = v.rearrange("p (a b c) -> p a b c", b=shape[2], c=shape[3])
            return v

        def new(shape, dt):
            es = 2 if dt == BF16 else 4
            off = alloc(int(np.prod(shape[1:])) * es)
            return Buf(view(off, shape, dt), off)

        def alt(buf, shape, dt):
            return Buf(view(buf.off, shape, dt), buf.off, buf.r)

        banks = [Buf(st.enter_context(nc.psum_tensor(f"bank{i}", [128, 512], F32))[:]) for i in range(8)]
        for b_ in banks:
            b_.r.x = True
        bstate = {"i": 0, "reserved": set()}

        def pb():
            while True:
                i = bstate["i"] % 8
                bstate["i"] += 1
                if i not in bstate["reserved"]:
                    return banks[i]

        def mm(out, lhsT, rhs, start, stop, reads, writes):
            S.op("pe", lambda e: e.matmul(out, lhsT, rhs, start=start, stop=stop), reads, writes)

        def act(out, in_, func, reads, writes, bias=0.0, scale=1.0, accum_out=None):
            if accum_out is None:
                S.op("act", lambda e: e.activation(out=out, in_=in_, func=func, bias=bias, scale=scale),
                     reads, writes)
            else:
                S.op("act", lambda e: e.activation(out=out, in_=in_, func=func, bias=bias, scale=scale,
                                                   accum_out=accum_out), reads, writes)

        def tt(eng, out, in0, in1, op, reads, writes):
            S.op(eng, lambda e: e.tensor_tensor(out=out, in0=in0, in1=in1, op=op), reads, writes)

        def ts(eng, out, in0, s1, s2, op0, op1, reads, writes):
            if op1 is None:
                S.op(eng, lambda e: e.tensor_scalar(out=out, in0=in0, scalar1=s1, scalar2=None, op0=op0),
                     reads, writes)
            else:
                S.op(eng, lambda e: e.tensor_scalar(out=out, in0=in0, scalar1=s1, scalar2=s2, op0=op0, op1=op1),
                     reads, writes)

        def stt(out, in0, scalar, in1, op0, op1, reads, writes, accum_out=None):
            if accum_out is None:
                S.op("dve", lambda e: e.scalar_tensor_tensor(out=out, in0=in0, scalar=scalar, in1=in1,
                                                             op0=op0, op1=op1), reads, writes)
            else:
                S.op("dve", lambda e: e.scalar_tensor_tensor(out=out, in0=in0, scalar=scalar, in1=in1,
                                                             op0=op0, op1=op1, accum_out=accum_out),
                     reads, writes)

        def cp(eng, out, in_, reads, writes):
            if eng == "act":
                S.op("act", lambda e: e.activation(out=out, in_=in_, func=AF.Copy), reads, writes)
            else:
                S.op(eng, lambda e: e.tensor_copy(out=out, in_=in_), reads, writes)

        def memset(eng, ap, val, writes):
            S.op(eng, lambda e: e.memset(ap, val), (), writes)

        def dma(q, out, in_, reads, writes):
            return S.dma(q, lambda e: e.dma_start(out=out, in_=in_), reads, writes)

        def dbg(name, buf, shape):
            if name in debug:
                o = dout("dbg_" + name, shape, buf.ap.dtype)
                dbg_out[name] = dma("sp", o, buf.ap, [buf.r], [])

        out_toks = []

        ident_f = new([128, 128], F32)
        ident_b = new([128, 128], BF16)
        ones_b = new([128, 128], BF16)
        sel_b = new([16, 16, 128], BF16)
        iota16 = new([128, 16], F32)
        chp = new([128, 8, 11], F32)
        nba = new([128, 8], F32)
        nbx = new([128, 8], F32)
        cA = new([128, 8], F32)
        c2A = new([128, 8], F32)
        flag = new([128, 1], F32)
        scz = new([128, 8, 2, 16], F32)
        src = new([128, 8, 3, 16], F32)
        sh = new([128, 8, 16], F32)
        wa_b = new([128, 8, 128], BF16)
        wx_b = new([128, 8, 128], BF16)
        hmid = new([128, 8], F32)
        rxhist = new([128, 8, 3], F32)
        ST = new([128, 8, 64], F32)
        mkT = new([128, 8, 256], BF16)
        mv_b = new([128, 2, 1024], BF16)

        R1 = alloc(16 * TT * 2)
        R2 = alloc(16 * TT * 2)
        R3 = alloc(8 * TT * 2)
        R5 = alloc(16 * TT * 2)
        xT = Buf(view(R1, [128, 16, TT], BF16))
        ycT = Buf(view(R2, [128, 8, TT], BF16))
        yrT = Buf(view(R2 + 8 * TT * 2, [128, 8, TT], BF16))
        yaT = Buf(view(R3, [128, 8, TT], BF16))
        xprevT = Buf(view(R5, [128, 16, 1024], BF16))
        qT = Buf(view(R5, [128, 8, TT], BF16))
        mergedT = Buf(view(R5, [128, 16, TT], BF16))
        PH2 = state["off"]

        slabs = [new([128, 16 * 128], BF16) for _ in range(6)]
        sstate = {"i": 0}

        conv = {"i": 0, "r": Res()}
        CONV_ROWS = 128
        NCONV = 2 * 16384 // CONV_ROWS

        def conv_step(n):
            for _ in range(n):
                i = conv["i"]
                if i >= NCONV:
                    return
                conv["i"] += 1
                src_t, c0_ = (i_pu, 0) if i % 2 == 0 else (i_pv, D)
                r0 = (i // 2) * CONV_ROWS
                dma("pool", puv_bf[r0:r0 + CONV_ROWS, c0_:c0_ + D], src_t[r0:r0 + CONV_ROWS, :], [], [])

        def load_slab(src_ap, kc):
            sb = slabs[sstate["i"] % len(slabs)]
            sstate["i"] += 1
            dma("pool", sb.ap[:, 0:kc * 128], src_ap, [], [sb.r])
            if sstate["i"] > KNOB["conv_from"]:
                if KNOB["conv_n"] is None:
                    conv_step(2 if sstate["i"] % 3 == 0 else 1)
                else:
                    conv_step(KNOB["conv_n"])
            return sb, sb.ap[:, 0:kc * 128].rearrange("p (k c) -> p k c", c=128)

        s4 = [new([128, TT + 5], F32) for _ in range(5)]
        s2 = [new([128, 512], F32) for _ in range(6)]
        hb = [new([128, 512], F32) for _ in range(2)]
        s2b = [new([128, 512], BF16) for _ in range(4)]
        st4 = {"i": 0}
        st2 = {"i": 0}
        st2b = {"i": 0}

        def t4():
            b = s4[st4["i"] % len(s4)]
            st4["i"] += 1
            return b

        def t2():
            b = s2[st2["i"] % len(s2)]
            st2["i"] += 1
            return b

        def t2b():
            b = s2b[st2b["i"] % len(s2b)]
            st2b["i"] += 1
            return b

        dma("sp", ident_f.ap, i_ident, [], [ident_f.r])
        dma("sp", iota16.ap, i_iota, [], [iota16.r])
        dma("sp", chp.ap, i_chp.rearrange("p (a b) -> p a b", b=11), [], [chp.r])
        dma("sp", flag.ap, i_flag, [], [flag.r])
        dma("sp", scz.ap, i_scz.rearrange("p (a b c) -> p a b c", b=2, c=16), [], [scz.r])
        dma("sp", src.ap, i_src.rearrange("p (a b c) -> p a b c", b=3, c=16), [], [src.r])
        dma("sp", sh.ap, i_sh.rearrange("p (a b) -> p a b", b=16), [], [sh.r])
        dma("pool", sel_b.ap, i_sel.rearrange("p (a b) -> p a b", b=128), [], [sel_b.r])
        dma("pool", wa_b.ap, i_wa.rearrange("p (a b) -> p a b", b=128), [], [wa_b.r])
        dma("pool", wx_b.ap, i_wx.rearrange("p (a b) -> p a b", b=128), [], [wx_b.r])
        for g in range(4):
            dma("pool", xprevT.ap[:, 4 * g:4 * g + 4, :],
                i_xprevT[512 * g:512 * g + 512, :].rearrange("(k p) n -> p k n", p=128), [], [xprevT.r])
        for g in range(4):
            dma("pool", xT.ap[:, 4 * g:4 * g + 4, :],
                i_xT[512 * g:512 * g + 512, :].rearrange("(k p) n -> p k n", p=128), [], [xT.r])
        cp("dve", ident_b.ap, ident_f.ap, [ident_f.r], [ident_b.r])
        memset("dve", ones_b.ap, 1.0, [ones_b.r])
        memset("dve", ST.ap, 0.0, [ST.r])
        ts("dve", nba.ap, chp.ap[:, :, 8], -1.0, None, ALU.mult, None, [chp.r], [nba.r])
        ts("dve", nbx.ap, chp.ap[:, :, 9], -1.0, None, ALU.mult, None, [chp.r], [nbx.r])
        tmpc = new([128, 8], F32)
        act(tmpc.ap, chp.ap[:, :, 10], AF.Exp, [chp.r], [tmpc.r], scale=-1.0)
        act(tmpc.ap, tmpc.ap, AF.Ln, [tmpc.r], [tmpc.r], bias=1.0)
        ts("dve", cA.ap, tmpc.ap, -8.0, None, ALU.mult, None, [tmpc.r], [cA.r])
        ts("dve", c2A.ap, tmpc.ap, -16.0, None, ALU.mult, None, [tmpc.r], [c2A.r])

        def phase(name):
            if PHASES.index(name) > PHASES.index(stop_after):
                raise _Stop()

        def body():
            def sigmoid_from_psum(ps, nbias_ap, n, out_buf):
                act(out_buf.ap[:, 0:n], ps.ap[:, 0:n], AF.Exp, [ps.r, nba.r, nbx.r], [out_buf.r], bias=nbias_ap, scale=-1.0)
                act(out_buf.ap[:, 0:n], out_buf.ap[:, 0:n], AF.Ln, [out_buf.r], [out_buf.r], bias=1.0)
                act(out_buf.ap[:, 0:n], out_buf.ap[:, 0:n], AF.Exp, [out_buf.r], [out_buf.r], scale=-1.0)

            def rg_block(j, xr_ap, xr_res, n, h_init, h_out_buf, h_init_res=None):
                xb = t2b()
                cp("act", xb.ap[:, 0:n], xr_ap, [xr_res], [xb.r])
                pr = pb()
                mm(pr.ap[:, 0:n], wa_b.ap[:, j, :], xb.ap[:, 0:n], True, True, [wa_b.r, xb.r], [pr.r])
                pi = pb()
                mm(pi.ap[:, 0:n], wx_b.ap[:, j, :], xb.ap[:, 0:n], True, True, [wx_b.r, xb.r], [pi.r])
                r = t2()
                sigmoid_from_psum(pr, nba.ap[:, j:j + 1], n, r)
                ig = t2()
                sigmoid_from_psum(pi, nbx.ap[:, j:j + 1], n, ig)
                a2 = t2()
                act(a2.ap[:, 0:n], r.ap[:, 0:n], AF.Exp, [r.r, c2A.r], [a2.r], scale=c2A.ap[:, j:j + 1])
                a = t2()
                act(a.ap[:, 0:n], r.ap[:, 0:n], AF.Exp, [r.r, cA.r], [a.r], scale=cA.ap[:, j:j + 1])
                ts("dve", a2.ap[:, 0:n], a2.ap[:, 0:n], -1.0, 1.0, ALU.mult, ALU.add, [a2.r], [a2.r])
                ts("dve", a2.ap[:, 0:n], a2.ap[:, 0:n], 1e-30, None, ALU.max, None, [a2.r], [a2.r])
                act(a2.ap[:, 0:n], a2.ap[:, 0:n], AF.Ln, [a2.r], [a2.r])
                act(a2.ap[:, 0:n], a2.ap[:, 0:n], AF.Exp, [a2.r], [a2.r], scale=0.5)
                tt("dve", ig.ap[:, 0:n], ig.ap[:, 0:n], xr_ap, ALU.mult, [ig.r, xr_res], [ig.r])
                tt("dve", ig.ap[:, 0:n], ig.ap[:, 0:n], a2.ap[:, 0:n], ALU.mult, [ig.r, a2.r], [ig.r])
                S.op("dve", lambda e: e.tensor_tensor_scan(out=h_out_buf.ap[:, 0:n], data0=a.ap[:, 0:n],
                                                           data1=ig.ap[:, 0:n], initial=h_init,
                                                           op0=ALU.mult, op1=ALU.add),
                     [a.r, ig.r, hmid.r] + ([h_init_res] if h_init_res is not None else []), [h_out_buf.r])

            def rg_gates(j, xr, blocks):
                nb = len(blocks)
                xb = [t2b() for _ in range(nb)]
                for b, (c0, n) in enumerate(blocks):
                    cp("act", xb[b].ap[:, 0:n], xr.ap[:, c0:c0 + n], [xr.r], [xb[b].r])
                pr = [pb() for _ in range(nb)]
                pi = [pb() for _ in range(nb)]
                for b, (c0, n) in enumerate(blocks):
                    mm(pr[b].ap[:, 0:n], wa_b.ap[:, j, :], xb[b].ap[:, 0:n], True, True, [wa_b.r, xb[b].r], [pr[b].r])
                    mm(pi[b].ap[:, 0:n], wx_b.ap[:, j, :], xb[b].ap[:, 0:n], True, True, [wx_b.r, xb[b].r], [pi[b].r])
                r = [t2() for _ in range(nb)]
                ig = [t2() for _ in range(nb)]
                a2 = [t2() for _ in range(nb)]
                a = [t2() for _ in range(nb)]
                for b, (c0, n) in enumerate(blocks):
                    act(r[b].ap[:, 0:n], pr[b].ap[:, 0:n], AF.Exp, [pr[b].r, nba.r], [r[b].r], bias=nba.ap[:, j:j + 1], scale=-1.0)
                    act(ig[b].ap[:, 0:n], pi[b].ap[:, 0:n], AF.Exp, [pi[b].r, nbx.r], [ig[b].r], bias=nbx.ap[:, j:j + 1], scale=-1.0)
                for b, (c0, n) in enumerate(blocks):
                    act(r[b].ap[:, 0:n], r[b].ap[:, 0:n], AF.Ln, [r[b].r], [r[b].r], bias=1.0)
                    act(ig[b].ap[:, 0:n], ig[b].ap[:, 0:n], AF.Ln, [ig[b].r], [ig[b].r], bias=1.0)
                for b, (c0, n) in enumerate(blocks):
                    act(r[b].ap[:, 0:n], r[b].ap[:, 0:n], AF.Exp, [r[b].r], [r[b].r], scale=-1.0)
                    act(ig[b].ap[:, 0:n], ig[b].ap[:, 0:n], AF.Exp, [ig[b].r], [ig[b].r], scale=-1.0)
                for b, (c0, n) in enumerate(blocks):
                    act(a2[b].ap[:, 0:n], r[b].ap[:, 0:n], AF.Exp, [r[b].r, c2A.r], [a2[b].r], scale=c2A.ap[:, j:j + 1])
                    act(a[b].ap[:, 0:n], r[b].ap[:, 0:n], AF.Exp, [r[b].r, cA.r], [a[b].r], scale=cA.ap[:, j:j + 1])
                for b, (c0, n) in enumerate(blocks):
                    ts("dve", a2[b].ap[:, 0:n], a2[b].ap[:, 0:n], -1.0, 1.0, ALU.mult, ALU.add, [a2[b].r], [a2[b].r])
                    tt("dve", ig[b].ap[:, 0:n], ig[b].ap[:, 0:n], xr.ap[:, c0:c0 + n], ALU.mult, [ig[b].r, xr.r], [ig[b].r])
                for b, (c0, n) in enumerate(blocks):
                    ts("dve", a2[b].ap[:, 0:n], a2[b].ap[:, 0:n], 1e-30, None, ALU.max, None, [a2[b].r], [a2[b].r])
                for b, (c0, n) in enumerate(blocks):
                    act(a2[b].ap[:, 0:n], a2[b].ap[:, 0:n], AF.Ln, [a2[b].r], [a2[b].r])
                for b, (c0, n) in enumerate(blocks):
                    act(a2[b].ap[:, 0:n], a2[b].ap[:, 0:n], AF.Exp, [a2[b].r], [a2[b].r], scale=0.5)
                for b, (c0, n) in enumerate(blocks):
                    tt("dve", ig[b].ap[:, 0:n], ig[b].ap[:, 0:n], a2[b].ap[:, 0:n], ALU.mult, [ig[b].r, a2[b].r], [ig[b].r])
                return a, ig

            def scan(a_b, u_b, n, h_init, h_out, extra_reads):
                S.op("dve", lambda e: e.tensor_tensor_scan(out=h_out.ap[:, 0:n], data0=a_b.ap[:, 0:n],
                                                           data1=u_b.ap[:, 0:n], initial=h_init,
                                                           op0=ALU.mult, op1=ALU.add),
                     [a_b.r, u_b.r] + extra_reads, [h_out.r])

            def dwconv4(j, rxf, n, xr):
                w = lambda k: chp.ap[:, j, 3 + k:4 + k]
                ts("dve", xr.ap[:, 0:n], rxf.ap[:, 3:3 + n], w(3), chp.ap[:, j, 7:8], ALU.mult, ALU.add,
                   [rxf.r, chp.r], [xr.r])
                for k in range(3):
                    stt(xr.ap[:, 0:n], rxf.ap[:, k:k + n], w(k), xr.ap[:, 0:n], ALU.mult, ALU.add,
                        [rxf.r, chp.r, xr.r], [xr.r])

            phase("MK")
            memT = Buf(view(R2, [128, 16, 256], BF16))
            for g in range(4):
                dma("pool", memT.ap[:, 4 * g:4 * g + 4, :],
                    i_memT[512 * g:512 * g + 512, :].rearrange("(k p) n -> p k n", p=128), [], [memT.r])
            mk_tok = Buf(view(R3, [128, 2, 1024], F32))
            mv_tok = Buf(view(R3 + 8192, [128, 2, 1024], F32))
            for c8 in range(8):
                if KNOB["mk"] < 1:
                    break
                sb, w = load_slab(i_wmk[c8], 16)
                if KNOB["mk"] < 2:
                    continue
                ps = pb()
                for k in range(16):
                    mm(ps.ap[:, 0:256], w[:, k, :], memT.ap[:, k, :], k == 0, k == 15, [sb.r, memT.r], [ps.r])
                if KNOB["mk"] < 3:
                    continue
                cp("act", mkT.ap[:, c8, :], ps.ap[:, 0:256], [ps.r], [mkT.r])
                if KNOB["mk"] < 4:
                    continue
                ps2 = pb()
                for mc in range(2):
                    for k in range(16):
                        mm(ps2.ap[:, mc * 128:mc * 128 + 128], memT.ap[:, k, mc * 128:mc * 128 + 128], w[:, k, :],
                           k == 0, k == 15, [sb.r, memT.r], [ps2.r])
                if KNOB["mk"] < 5:
                    continue
                cp("dve", mk_tok.ap[:, :, c8 * 128:c8 * 128 + 128],
                   ps2.ap[:, 0:256].rearrange("p (a b) -> p a b", b=128), [ps2.r], [mk_tok.r])
            if KNOB["mk"] < 6:
                raise _Stop()
            out_toks.append(dma("sp", o_mk.rearrange("(c p) n -> p c n", p=128), mk_tok.ap, [mk_tok.r], []))
            if KNOB["mk"] < 7:
                raise _Stop()
            for c8 in range(8):
                sb, w = load_slab(i_wmv[c8], 16)
                ps2 = pb()
                for mc in range(2):
                    for k in range(16):
                        mm(ps2.ap[:, mc * 128:mc * 128 + 128], memT.ap[:, k, mc * 128:mc * 128 + 128], w[:, k, :],
                           k == 0, k == 15, [sb.r, memT.r], [ps2.r])
                cp("dve", mv_tok.ap[:, :, c8 * 128:c8 * 128 + 128],
                   ps2.ap[:, 0:256].rearrange("p (a b) -> p a b", b=128), [ps2.r], [mv_tok.r])
                if KNOB["mk"] < 8:
                    continue
                for mc in range(2):
                    cp("act", mv_b.ap[:, mc, c8 * 128:c8 * 128 + 128], mv_tok.ap[:, mc, c8 * 128:c8 * 128 + 128],
                       [mv_tok.r], [mv_b.r])
            if KNOB["mk"] < 9:
                raise _Stop()
            out_toks.append(dma("sp", o_mv.rearrange("(c p) n -> p c n", p=128), mv_tok.ap, [mv_tok.r], []))

            phase("B0")
            S.barrier()
            s2_base = list(s2)
            s2b_base = list(s2b)
            s4_base = list(s4)
            s4[:] = s4_base + [Buf(view(R3 + i * 4224, [128, TT + 5], F32), R3 + i * 4224) for i in range(2)]
            s2[:] = s2_base + [Buf(view(R3 + 8448 + i * 2048, [128, 512], F32), R3 + 8448 + i * 2048) for i in range(4)]

            def zchunk(cidx, consume):
                sb, w = load_slab(i_win[cidx], 16)
                for bi, (c0, n) in enumerate(BLKS):
                    ps = pb()
                    for k in range(16):
                        mm(ps.ap[:, 0:n], w[:, k, :], xT.ap[:, k, c0:c0 + n], k == 0, k == 15, [sb.r, xT.r], [ps.r])
                    consume(bi, c0, n, ps)

            def b0_chunk(j):
                sb, w = load_slab(i_win[24 + j], 16)
                rxf = t4()
                memset("dve", rxf.ap[:, 0:3], 0.0, [rxf.r])
                for b in range(2):
                    ps = pb()
                    for k in range(16):
                        mm(ps.ap, w[:, k, :], xprevT.ap[:, k, b * 512:b * 512 + 512], k == 0, k == 15,
                           [sb.r, xprevT.r], [ps.r])
                    cp("act", rxf.ap[:, 3 + b * 512:3 + b * 512 + 512], ps.ap, [ps.r], [rxf.r])
                cp("dve", rxhist.ap[:, j, :], rxf.ap[:, 1024:1027], [rxf.r], [rxhist.r])
                xr = t4()
                dwconv4(j, rxf, 1024, xr)
                return xr

            def b0_chunk2(j, xr):
                h0 = hb[0]
                h1 = hb[1]
                a_, u_ = rg_gates(j, xr, [(0, 512), (512, 512)])
                scan(a_[0], u_[0], 512, 0.0, h0, [])
                scan(a_[1], u_[1], 512, h0.ap[:, 511:512], h1, [h0.r])
                ts("dve", hmid.ap[:, j:j + 1], h1.ap[:, 511:512], flag.ap[:, 0:1], None, ALU.mult, None,
                   [h1.r, flag.r], [hmid.r])

            def a_chunk(j):
                ccs = t4()
                cz = t4()
                cbs = t4()
                cy = t4()
                zchunk(8 + j, lambda bi, c0, n, ps: cp("act", ccs.ap[:, c0:c0 + n], ps.ap[:, 0:n], [ps.r], [ccs.r]))
                return ccs, cz, cbs, cy

            def a_chunk2(j, ccs, cz, cbs, cy):
                zchunk(16 + j, lambda bi, c0, n, ps: tt("dve", cz.ap[:, c0:c0 + n], ccs.ap[:, c0:c0 + n],
                                                         ps.ap[:, 0:n], ALU.mult, [ccs.r, ps.r], [cz.r]))
                zchunk(j, lambda bi, c0, n, ps: cp("act", cbs.ap[:, c0:c0 + n], ps.ap[:, 0:n], [ps.r], [cbs.r]))
                w = lambda k: chp.ap[:, j, k:k + 1]
                memset("dve", cy.ap[:, 0:2], 0.0, [cy.r])
                ts("dve", cy.ap[:, 2:CS], cz.ap[:, 2:CS], w(2), None, ALU.mult, None, [cz.r, chp.r], [cy.r])
                stt(cy.ap[:, 2:CS], cz.ap[:, 1:CS - 1], w(1), cy.ap[:, 2:CS], ALU.mult, ALU.add, [cz.r, chp.r, cy.r], [cy.r])
                stt(cy.ap[:, 2:CS], cz.ap[:, 0:CS - 2], w(0), cy.ap[:, 2:CS], ALU.mult, ALU.add, [cz.r, chp.r, cy.r], [cy.r])
                ts("dve", cy.ap[:, CS:TT], cz.ap[:, CS:TT], w(2), None, ALU.mult, None, [cz.r, chp.r], [cy.r])
                stt(cy.ap[:, CS:TT], scz.ap[:, j, 1, :], w(1), cy.ap[:, CS:TT], ALU.mult, ALU.add, [scz.r, chp.r, cy.r], [cy.r])
                stt(cy.ap[:, CS:TT], scz.ap[:, j, 0, :], w(0), cy.ap[:, CS:TT], ALU.mult, ALU.add, [scz.r, chp.r, cy.r], [cy.r])
                tt("dve", ycT.ap[:, j, :], cbs.ap[:, 0:TT], cy.ap[:, 0:TT], ALU.mult, [cbs.r, cy.r], [ycT.r])
                cp("act", ST.ap[:, j, 0:2], cz.ap[:, CS - 2:CS], [cz.r], [ST.r])
                cp("act", ST.ap[:, j, 6:22], cz.ap[:, CS:TT], [cz.r], [ST.r])

            phase("A")
            for j in range(8):
                xr_ = b0_chunk(j)
                abufs = a_chunk(j)
                b0_chunk2(j, xr_)
                a_chunk2(j, *abufs)
            dbg("hmid", hmid, [128, 8])
            dbg("ycT", ycT, [128, 8, TT])

            S.barrier()
            s4[:] = s4_base
            s2[:] = s2_base + [Buf(view(R5 + i * 2048, [128, 512], F32), R5 + i * 2048) for i in range(10)]
            s2b[:] = s2b_base + [Buf(view(R5 + 20480 + i * 1024, [128, 512], BF16), R5 + 20480 + i * 1024) for i in range(4)]

            phase("B")
            for j in range(8):
                rxf = t4()
                cp("dve", rxf.ap[:, 0:3], rxhist.ap[:, j, :], [rxhist.r], [rxf.r])
                rxall = t4()
                zchunk(24 + j, lambda bi, c0, n, ps: cp("act", rxall.ap[:, c0:c0 + n], ps.ap[:, 0:n], [ps.r], [rxall.r]))
                cp("dve", rxf.ap[:, 3:3 + 1024], rxall.ap[:, C0:CS], [rxall.r], [rxf.r])
                xr = t4()
                dwconv4(j, rxf, 1024, xr)
                wk = lambda k: chp.ap[:, j, 3 + k:4 + k]
                ts("dve", xr.ap[:, 1024:1040], rxall.ap[:, CS:TT], wk(3), chp.ap[:, j, 7:8], ALU.mult, ALU.add,
                   [rxall.r, chp.r], [xr.r])
                for k in range(3):
                    stt(xr.ap[:, 1024:1040], src.ap[:, j, k, :], wk(k), xr.ap[:, 1024:1040], ALU.mult, ALU.add,
                        [src.r, chp.r, xr.r], [xr.r])
                gl = t4()

                def gelu_consume(bi, c0, n, ps):
                    x = t2()
                    cp("act", x.ap[:, 0:n], ps.ap[:, 0:n], [ps.r], [x.r])
                    p = t2()
                    tt("pool", p.ap[:, 0:n], x.ap[:, 0:n], x.ap[:, 0:n], ALU.mult, [x.r], [p.r])
                    ts("pool", p.ap[:, 0:n], p.ap[:, 0:n], 0.044715, 1.0, ALU.mult, ALU.add, [p.r], [p.r])
                    tt("pool", p.ap[:, 0:n], p.ap[:, 0:n], x.ap[:, 0:n], ALU.mult, [p.r, x.r], [p.r])
                    act(p.ap[:, 0:n], p.ap[:, 0:n], AF.Exp, [p.r], [p.r], scale=-1.5957691216057308)
                    act(p.ap[:, 0:n], p.ap[:, 0:n], AF.Ln, [p.r], [p.r], bias=1.0)
                    act(p.ap[:, 0:n], p.ap[:, 0:n], AF.Exp, [p.r], [p.r], scale=-1.0)
                    tt("dve", gl.ap[:, c0:c0 + n], p.ap[:, 0:n], x.ap[:, 0:n], ALU.mult, [p.r, x.r], [gl.r])

                zchunk(32 + j, gelu_consume)
                h0 = hb[0]
                h1 = hb[1]
                a_, u_ = rg_gates(j, xr, [(0, 512), (512, 512), (1024, 16)])
                scan(a_[0], u_[0], 512, hmid.ap[:, j:j + 1], h0, [hmid.r])
                tt("dve", yrT.ap[:, j, C0:C0 + 512], h0.ap[:, 0:512], gl.ap[:, C0:C0 + 512], ALU.mult,
                   [h0.r, gl.r], [yrT.r])
                scan(a_[1], u_[1], 512, h0.ap[:, 511:512], h1, [h0.r])
                tt("dve", yrT.ap[:, j, C0 + 512:CS], h1.ap[:, 0:512], gl.ap[:, C0 + 512:CS], ALU.mult,
                   [h1.r, gl.r], [yrT.r])
                cp("act", ST.ap[:, j, 5:6], h1.ap[:, 511:512], [h1.r], [ST.r])
                cp("act", ST.ap[:, j, 2:5], rxall.ap[:, CS - 3:CS], [rxall.r], [ST.r])
                cp("act", ST.ap[:, j, 22:38], rxall.ap[:, CS:TT], [rxall.r], [ST.r])
                hs = t2()
                tt("dve", hs.ap[:, 0:16], a_[2].ap[:, 0:16], sh.ap[:, j, :], ALU.mult, [a_[2].r, sh.r], [hs.r])
                tt("dve", hs.ap[:, 0:16], hs.ap[:, 0:16], u_[2].ap[:, 0:16], ALU.add, [hs.r, u_[2].r], [hs.r])
                cp("act", ST.ap[:, j, 38:54], hs.ap[:, 0:16], [hs.r], [ST.r])
                tt("dve", yrT.ap[:, j, CS:TT], hs.ap[:, 0:16], gl.ap[:, CS:TT], ALU.mult, [hs.r, gl.r], [yrT.r])
                memset("dve", yrT.ap[:, j, 0:C0], 0.0, [yrT.r])
            dbg("yrT", yrT, [128, 8, TT])

            phase("ST")
            stp = [pb(), pb()]
            for j in range(8):
                b = stp[j // 4]
                S.op("pe", lambda e, b=b, j=j: e.transpose(b.ap[0:54, (j % 4) * 128:(j % 4) * 128 + 128],
                                                            ST.ap[:, j, 0:54], ident_f.ap),
                     [ST.r, ident_f.r], [b.r])
            st_sb = alt(t4(), [54, 1024], F32)
            for hh in range(2):
                cp("dve", st_sb.ap[:, hh * 512:hh * 512 + 512], stp[hh].ap[0:54, :], [stp[hh].r], [st_sb.r])
            out_toks.append(dma("sp", o_st, st_sb.ap, [st_sb.r], []))
            out_toks.append(dma("sp", o_czs[:, 0, :], i_scz_tok[:, 1, :], [], []))
            out_toks.append(dma("sp", o_czs[:, 1, :], st_sb.ap[6:22, :], [st_sb.r], []))
            out_toks.append(dma("sp", o_rcs[:, 0:2, :], i_src_tok[:, 1:3, :], [], []))
            out_toks.append(dma("sp", o_rcs[:, 2, :], st_sb.ap[22:38, :], [st_sb.r], []))

            phase("C")
            S.barrier()
            s2[:] = s2_base
            s2b[:] = s2b_base
            for j in range(8):
                zchunk(40 + j, lambda bi, c0, n, ps, j=j: cp("act", qT.ap[:, j, c0:c0 + n], ps.ap[:, 0:n], [ps.r], [qT.r]))
            qs_b = new([16, 1024], BF16)
            for half in range(2):
                ps = pb()
                for c4 in range(4):
                    sb, w = load_slab(i_win[40 + half * 4 + c4], 16)
                    for k in range(16):
                        mm(ps.ap[0:16, c4 * 128:c4 * 128 + 128], xT.ap[:, k, CS:TT], w[:, k, :], k == 0, k == 15,
                           [sb.r, xT.r], [ps.r])
                cp("act", qs_b.ap[:, half * 512:half * 512 + 512], ps.ap[0:16, :], [ps.r], [qs_b.r])
            for h in range(4):
                for (c0, n) in BLKS:
                    pTs = []
                    for kc in range(2):
                        ps = pb()
                        for dc in range(2):
                            mm(ps.ap[:, 0:n], mkT.ap[:, h * 2 + dc, kc * 128:kc * 128 + 128], qT.ap[:, h * 2 + dc, c0:c0 + n],
                               dc == 0, dc == 1, [mkT.r, qT.r], [ps.r])
                        p = t2b()
                        act(p.ap[:, 0:n], ps.ap[:, 0:n], AF.Exp, [ps.r], [p.r], scale=1.0 / 16.0)
                        pTs.append(p)
                    pd = pb()
                    for kc in range(2):
                        mm(pd.ap[:, 0:n], ones_b.ap, pTs[kc].ap[:, 0:n], kc == 0, kc == 1, [ones_b.r, pTs[kc].r], [pd.r])
                    rec = t2()
                    act(rec.ap[:, 0:n], pd.ap[:, 0:n], AF.Ln, [pd.r], [rec.r])
                    act(rec.ap[:, 0:n], rec.ap[:, 0:n], AF.Exp, [rec.r], [rec.r], scale=-1.0)
                    for dc in range(2):
                        po = pb()
                        for kc in range(2):
                            mm(po.ap[:, 0:n], mv_b.ap[:, kc, h * 256 + dc * 128:h * 256 + dc * 128 + 128], pTs[kc].ap[:, 0:n],
                               kc == 0, kc == 1, [mv_b.r, pTs[kc].r], [po.r])
                        tt("dve", yaT.ap[:, h * 2 + dc, c0:c0 + n], po.ap[:, 0:n], rec.ap[:, 0:n], ALU.mult,
                           [po.r, rec.r], [yaT.r])
            yrow = new([1, 1024], BF16)
            pys = banks[7]
            bstate["reserved"].add(7)
            for t in range(16):
                vb = alt(t4(), [128, 2, 1024], BF16)
                dma("pool", vb.ap, i_cv[t].rearrange("(c p) n -> p c n", p=128), [], [vb.r])
                qb = [pb(), pb()]
                for half in range(2):
                    mm(qb[half].ap, sel_b.ap[:, t, :], qs_b.ap[:, half * 512:half * 512 + 512], True, True,
                       [sel_b.r, qs_b.r], [qb[half].r])
                sc = t2()
                for c in range(2):
                    kb = t4()
                    dma("sp", kb.ap[:, 0:1024], i_ck[t, c * 128:c * 128 + 128, :], [], [kb.r])
                    prod = t4()
                    for half in range(2):
                        tt("dve", prod.ap[:, half * 512:half * 512 + 512], kb.ap[:, half * 512:half * 512 + 512],
                           qb[half].ap, ALU.mult, [kb.r, qb[half].r], [prod.r])
                    S.op("dve", lambda e, c=c, sc=sc, prod=prod: e.tensor_reduce(
                        out=sc.ap[:, c * 4:c * 4 + 4], in_=prod.ap[:, 0:1024].rearrange("p (h d) -> p h d", d=256),
                        axis=AX.X, op=ALU.add), [prod.r], [sc.r])
                pp = t2b()
                act(pp.ap[:, 0:8], sc.ap[:, 0:8], AF.Exp, [sc.r], [pp.r], scale=1.0 / 16.0)
                po = [pb(), pb(), pb()]
                for h in range(4):
                    for c in range(2):
                        mm(po[h // 2].ap[0:1, (h % 2) * 256:(h % 2) * 256 + 256], pp.ap[:, c * 4 + h:c * 4 + h + 1],
                           vb.ap[:, c, h * 256:h * 256 + 256], c == 0, c == 1, [pp.r, vb.r], [po[h // 2].r])
                        mm(po[2].ap[0:1, h:h + 1], pp.ap[:, c * 4 + h:c * 4 + h + 1], ones_b.ap[:, 0:1],
                           c == 0, c == 1, [pp.r, ones_b.r], [po[2].r])
                rec = t2()
                S.op("dve", lambda e, rec=rec, po=po: e.reciprocal(out=rec.ap[0:1, 0:4], in_=po[2].ap[0:1, 0:4]),
                     [po[2].r], [rec.r])
                for half in range(2):
                    S.op("dve", lambda e, half=half, rec=rec, po=po: e.tensor_tensor(
                        out=yrow.ap[0:1, half * 512:half * 512 + 512].rearrange("p (h d) -> p h d", d=256),
                        in0=po[half].ap[0:1, :].rearrange("p (h d) -> p h d", d=256),
                        in1=bass.AP(rec.ap.tensor, rec.ap[0:1, half * 2:half * 2 + 2].offset,
                                    [list(rec.ap[0:1, 0:2].ap[0]), [1, 2], [0, 256]]),
                        op=ALU.mult), [po[half].r, rec.r], [yrow.r])
                for j in range(8):
                    mm(pys.ap[:, j * 16 + t:j * 16 + t + 1], yrow.ap[0:1, j * 128:j * 128 + 128], ones_b.ap[0:1, 0:1],
                       True, True, [yrow.r, ones_b.r], [pys.r])
            cp("dve", yaT.ap[:, :, CS:TT], pys.ap[:, 0:128].rearrange("p (j t) -> p j t", t=16), [pys.r], [yaT.r])
            bstate["reserved"].discard(7)
            dbg("yaT", yaT, [128, 8, TT])

            S.barrier()

            phase("D")
            for m in range(16):
                acc = t4()
                for br, (wsrc, yT) in enumerate(((i_wbc, ycT), (i_wbr, yrT), (i_wba, yaT))):
                    sbg, wg = load_slab(i_win[48 + br * 16 + m], 16)
                    sbp, wp = load_slab(wsrc[m], 8)
                    for (c0, n) in BLKS:
                        pg = pb()
                        for k in range(16):
                            mm(pg.ap[:, 0:n], wg[:, k, :], xT.ap[:, k, c0:c0 + n], k == 0, k == 15, [sbg.r, xT.r], [pg.r])
                        pp_ = pb()
                        for k in range(8):
                            mm(pp_.ap[:, 0:n], wp[:, k, :], yT.ap[:, k, c0:c0 + n], k == 0, k == 7, [sbp.r, yT.r], [pp_.r])
                        sg = t2()
                        act(sg.ap[:, 0:n], pg.ap[:, 0:n], AF.Exp, [pg.r], [sg.r], scale=-1.0)
                        act(sg.ap[:, 0:n], sg.ap[:, 0:n], AF.Ln, [sg.r], [sg.r], bias=1.0)
                        act(sg.ap[:, 0:n], sg.ap[:, 0:n], AF.Exp, [sg.r], [sg.r], scale=-1.0)
                        if br == 0:
                            tt("dve", acc.ap[:, c0:c0 + n], sg.ap[:, 0:n], pp_.ap[:, 0:n], ALU.mult, [sg.r, pp_.r], [acc.r])
                        else:
                            tt("dve", sg.ap[:, 0:n], sg.ap[:, 0:n], pp_.ap[:, 0:n], ALU.mult, [sg.r, pp_.r], [sg.r])
                            if br == 1:
                                tt("dve", acc.ap[:, c0:c0 + n], acc.ap[:, c0:c0 + n], sg.ap[:, 0:n], ALU.add,
                                   [acc.r, sg.r], [acc.r])
                            else:
                                tt("dve", mergedT.ap[:, m, c0:c0 + n], acc.ap[:, c0:c0 + n], sg.ap[:, 0:n], ALU.add,
                                   [acc.r, sg.r], [mergedT.r])
            dbg("mergedT", mergedT, [128, 16, TT])

            phase("E")
            TILES = [(C0 + 128 * i, 128, 128 * i) for i in range(8)] + [(CS, 16, 1024)]
            S.barrier()
            vres = view(R1, [128, 9, D], F32)
            vres_r = [Res() for _ in range(9)]
            for i, (col0, nt, row0) in enumerate(TILES):
                dma("sp", vres[0:nt, i, :], i_xtok[row0:row0 + nt, :], [], [vres_r[i]])
            for cb in range(16):
                sb, w = load_slab(i_wo[cb], 16)
                for i, (col0, nt, row0) in enumerate(TILES):
                    ps = pb()
                    for k in range(16):
                        mm(ps.ap[0:nt, 0:128], mergedT.ap[:, k, col0:col0 + nt], w[:, k, :], k == 0, k == 15,
                           [sb.r, mergedT.r], [ps.r])
                    vsl = vres[0:nt, i, cb * 128:cb * 128 + 128]
                    stt(vsl, vsl, ALPHA, ps.ap[0:nt, 0:128], ALU.mult, ALU.add, [vres_r[i], ps.r], [vres_r[i]])

            S.barrier()
            phase("F1")
            state["off"] = PH2
            x1T = Buf(view(R5, [128, 16, NTOK], BF16))
            qpT = Buf(view(R1, [128, 16, NTOK], BF16))
            keysT = new([128, 16, 128], BF16)
            lng = [new([128, D], F32) for _ in range(2)]
            vtile = [new([128, D], F32) for _ in range(2)]
            stats = new([128, 24], F32)
            mv2 = new([128, 2], F32)
            rstd = new([128, 1], F32)
            F_ONLY = state["off"]
            slabs[:] = [new([128, 16 * 128], BF16) for _ in range(4)]
            x1b = new([128, D], BF16)
            conv_step(NCONV)

            dma("pool", keysT.ap, i_keysT.rearrange("p (a b) -> p a b", b=128), [], [keysT.r])

            def layernorm(vb, nt, gi, out_buf):
                for q4 in range(4):
                    S.op("dve", lambda e, q4=q4: e.bn_stats(out=stats.ap[0:nt, q4 * 6:q4 * 6 + 6], in_=vb.ap[0:nt, q4 * 512:q4 * 512 + 512]),
                         [vb.r], [stats.r])
                S.op("dve", lambda e: e.bn_aggr(out=mv2.ap[0:nt, :], in_=stats.ap[0:nt, :]), [stats.r], [mv2.r])
                ts("dve", rstd.ap[0:nt, :], mv2.ap[0:nt, 1:2], LN_EPS, None, ALU.add, None, [mv2.r], [rstd.r])
                act(rstd.ap[0:nt, :], rstd.ap[0:nt, :], AF.Ln, [rstd.r], [rstd.r])
                act(rstd.ap[0:nt, :], rstd.ap[0:nt, :], AF.Exp, [rstd.r], [rstd.r], scale=-0.5)
                ts("dve", out_buf.ap[0:nt, :], vb.ap[0:nt, :], mv2.ap[0:nt, 0:1], rstd.ap[0:nt, 0:1], ALU.subtract, ALU.mult,
                   [vb.r, mv2.r, rstd.r], [out_buf.r])
                tt("dve", out_buf.ap[0:nt, :], out_buf.ap[0:nt, :], lng[0].ap[0:nt, :], ALU.mult, [out_buf.r, lng[0].r], [out_buf.r])
                tt("dve", out_buf.ap[0:nt, :], out_buf.ap[0:nt, :], lng[1].ap[0:nt, :], ALU.add, [out_buf.r, lng[1].r], [out_buf.r])

            dma("sp", lng[0].ap, bass.AP(i_ln.tensor, 0 * D, [[0, 128], [1, D]]), [], [lng[0].r])
            dma("sp", lng[1].ap, bass.AP(i_ln.tensor, 1 * D, [[0, 128], [1, D]]), [], [lng[1].r])
            TILES2 = [(128 * i, 128) for i in range(8)] + [(1024, 16)]
            def ln1_tile(ti, row0, nt):
                    vb = Buf(vres[:, ti, :], None, vres_r[ti])
                    layernorm(vb, nt, 0, vb)
                    dma("sp", x1scr[row0:row0 + nt, :], vb.ap[0:nt, :], [vb.r], [])
                    cp("act", x1b.ap[0:nt, :], vb.ap[0:nt, :], [vb.r], [x1b.r])
                    for g in range(4):
                        ps = pb()
                        psb = ps.ap.bitcast(BF16)
                        for kk in range(4):
                            k = g * 4 + kk
                            S.op("pe", lambda e, k=k, kk=kk, psb=psb: e.transpose(
                                psb[:, kk * 128:kk * 128 + nt], x1b.ap[0:nt, k * 128:k * 128 + 128], ident_b.ap[0:nt, 0:nt]),
                                [x1b.r, ident_b.r], [ps.r])
                        cp("dve", x1T.ap[:, g * 4:g * 4 + 4, row0:row0 + nt],
                           psb[:, 0:512].rearrange("p (a b) -> p a b", b=128)[:, :, 0:nt], [ps.r], [x1T.r])

            for ti, (row0, nt) in enumerate(TILES2):
                ln1_tile(ti, row0, nt)

            phase("F2")
            S.barrier()
            QBL = [(0, 512), (512, 512), (1024, 16)]
            for hc in range(16):
                sb, w = load_slab(i_wq[hc], 16)
                for (c0, n) in QBL:
                    ps = pb()
                    for k in range(16):
                        mm(ps.ap[:, 0:n], w[:, k, :], x1T.ap[:, k, c0:c0 + n], k == 0, k == 15, [sb.r, x1T.r], [ps.r])
                    cp("act", qpT.ap[:, hc, c0:c0 + n], ps.ap[:, 0:n], [ps.r], [qpT.r])

            phase("G")
            dma("sp", lng[0].ap, bass.AP(i_ln.tensor, 2 * D, [[0, 128], [1, D]]), [], [lng[0].r])
            dma("sp", lng[1].ap, bass.AP(i_ln.tensor, 3 * D, [[0, 128], [1, D]]), [], [lng[1].r])
            S.barrier()
            state["off"] = F_ONLY
            state["lim"] = ARENA_BYTES
            scs = new([128, D], F32)
            scs2 = new([128, D], F32)
            vals1 = new([128, 16, 16], F32)
            idx1 = new([128, 16, 16], U32)
            idx1f = new([128, 16, 16], F32)
            vals2 = new([128, 8, 16], F32)
            pos = new([128, 8, 16], U32)
            posf = new([128, 128], F32)
            posAf = new([128, 128], F32)
            posBf = new([128, 128], F32)
            thr16 = new([128, 16], F32)
            ts("dve", thr16.ap, iota16.ap, 16.0, 16.0, ALU.mult, ALU.add, [iota16.r], [thr16.r])
            selI = new([128, 128], F32)
            selJ = new([128, 128], F32)
            gw2 = [new([128, 128], F32) for _ in range(2)]
            gsum = new([128, 8], F32)
            actv = new([128, 128], F32)
            diag = [new([128, 128], BF16) for _ in range(4)]
            state["off"] = R2
            state["lim"] = R5 + 16 * TT * 2
            NPB = 7
            NVB = 5
            pbufs = [new([128, 2 * D], BF16) for _ in range(NPB)]
            pbV_r = [Res() for _ in range(NPB)]
            vgb = [new([128, D], BF16) for _ in range(NVB)]
            eidx2 = [new([128, 128], U32) for _ in range(2)]
            cco2 = [new([128, 128], F32) for _ in range(2)]
            tA = new([128, 128], F32)
            gA = new([128, 128], F32)
            rA = new([128, 128], F32)

            def bc(buf, off_elems, dims):
                return bass.AP(buf.ap.tensor, buf.ap.offset + off_elems, [list(buf.ap.ap[0])] + dims)

            def top16(src_ap, src_res, scratch_ap, scratch_res, vout, iout, res_v, res_i, nt):
                S.op("dve", lambda e: e.max(out=vout[0:nt, 0:8], in_=src_ap), [src_res], [res_v])
                S.op("dve", lambda e: e.max_index(out=iout[0:nt, 0:8], in_max=vout[0:nt, 0:8], in_values=src_ap),
                     [src_res, res_v], [res_i])
                S.op("dve", lambda e: e.match_replace(out=scratch_ap, in_to_replace=vout[0:nt, 0:8], in_values=src_ap,
                                                      imm_value=NEG), [src_res, res_v], [scratch_res])
                S.op("dve", lambda e: e.max(out=vout[0:nt, 8:16], in_=scratch_ap), [scratch_res], [res_v])
                S.op("dve", lambda e: e.max_index(out=iout[0:nt, 8:16], in_max=vout[0:nt, 8:16], in_values=scratch_ap),
                     [scratch_res, res_v], [res_i])

            def route(ti, row0, nt):
                    x1 = vtile[ti % 2]
                    eidx = eidx2[ti % 2]
                    gw = gw2[ti % 2]
                    dma("sp", x1.ap[0:nt, :], x1scr[row0:row0 + nt, :], [], [x1.r])
                    for g in range(4):
                        ps = banks[g]
                        for kk in range(4):
                            hc = g * 4 + kk
                            mm(ps.ap[0:nt, kk * 128:kk * 128 + 128], qpT.ap[:, hc, row0:row0 + nt], keysT.ap[:, hc, :], True, True,
                               [qpT.r, keysT.r], [ps.r])
                        cp("act", scs.ap[0:nt, g * 512:g * 512 + 512], ps.ap[0:nt, :], [ps.r], [scs.r])
                    for hc in range(16):
                        top16(scs.ap[0:nt, hc * 128:hc * 128 + 128], scs.r, scs2.ap[0:nt, hc * 128:hc * 128 + 128], scs2.r,
                              vals1.ap[:, hc, :], idx1.ap[:, hc, :], vals1.r, idx1.r, nt)
                        yield
                    cp("dve", idx1f.ap[0:nt], idx1.ap[0:nt], [idx1.r], [idx1f.r])
                    pstep = [list(vals1.ap.ap[0])[0], nt]
                    S.op("dve", lambda e, pstep=pstep: e.tensor_tensor(
                        out=scs.ap[0:nt, :].rearrange("p (h a b) -> p h a b", a=16, b=16),
                        in0=bass.AP(vals1.ap.tensor, vals1.ap.offset, [pstep, [32, 8], [1, 16], [0, 16]]),
                        in1=bass.AP(vals1.ap.tensor, vals1.ap.offset + 16, [pstep, [32, 8], [0, 16], [1, 16]]),
                        op=ALU.add), [vals1.r, scs.r], [scs.r])
                    for h in range(8):
                        top16(scs.ap[0:nt, h * 256:h * 256 + 256], scs.r, scs2.ap[0:nt, h * 256:h * 256 + 256], scs2.r,
                              vals2.ap[:, h, :], pos.ap[:, h, :], vals2.r, pos.r, nt)
                        yield
                    cp("dve", posf.ap[0:nt, :], pos.ap[0:nt].rearrange("p h k -> p (h k)"), [pos.r], [posf.r])
                    pfp = [list(posf.ap.ap[0])[0], nt]
                    tpp = [list(thr16.ap.ap[0])[0], nt]
                    S.op("dve", lambda e, pfp=pfp, tpp=tpp: e.tensor_tensor(
                        out=scs2.ap[0:nt, :].rearrange("p (s a) -> p s a", a=16),
                        in0=bass.AP(posf.ap.tensor, posf.ap.offset, [pfp, [1, 128], [0, 16]]),
                        in1=bass.AP(thr16.ap.tensor, thr16.ap.offset, [tpp, [0, 128], [1, 16]]),
                        op=ALU.is_ge), [posf.r, thr16.r, scs2.r], [scs2.r])
                    S.op("dve", lambda e: e.tensor_reduce(
                        out=posAf.ap[0:nt, :], in_=scs2.ap[0:nt, :].rearrange("p (s a) -> p s a", a=16),
                        axis=AX.X, op=ALU.add), [scs2.r], [posAf.r])
                    stt(posBf.ap[0:nt, :], posAf.ap[0:nt, :], -16.0, posf.ap[0:nt, :], ALU.mult, ALU.add,
                        [posAf.r, posf.r], [posBf.r])
                    ipart = [list(iota16.ap.ap[0])[0], nt]
                    for (pf, half_off, outb) in ((posAf, 0, selI), (posBf, 16, selJ)):
                        pp2 = [list(pf.ap.ap[0])[0], nt]
                        S.op("dve", lambda e, pf=pf, pp2=pp2: e.tensor_tensor(
                            out=scs2.ap[0:nt, :].rearrange("p (s a) -> p s a", a=16),
                            in0=bass.AP(pf.ap.tensor, pf.ap.offset, [pp2, [1, 128], [0, 16]]),
                            in1=bass.AP(iota16.ap.tensor, iota16.ap.offset, [ipart, [0, 128], [1, 16]]),
                            op=ALU.is_equal), [pf.r, iota16.r, scs2.r], [scs2.r])
                        ip2 = [list(idx1f.ap.ap[0])[0], nt]
                        S.op("dve", lambda e, half_off=half_off, ip2=ip2: e.tensor_tensor(
                            out=scs2.ap[0:nt, :].rearrange("p (h k a) -> p h k a", k=16, a=16),
                            in0=scs2.ap[0:nt, :].rearrange("p (h k a) -> p h k a", k=16, a=16),
                            in1=bass.AP(idx1f.ap.tensor, idx1f.ap.offset + half_off, [ip2, [32, 8], [0, 16], [1, 16]]),
                            op=ALU.mult), [scs2.r, idx1f.r], [scs2.r])
                        S.op("dve", lambda e, outb=outb: e.tensor_reduce(
                            out=outb.ap[0:nt, :], in_=scs2.ap[0:nt, :].rearrange("p (s a) -> p s a", a=16),
                            axis=AX.X, op=ALU.add), [scs2.r], [outb.r])
                    stt(selI.ap[0:nt, :], selI.ap[0:nt, :], 128.0, selJ.ap[0:nt, :], ALU.mult, ALU.add, [selI.r, selJ.r], [selI.r])
                    yield
                    cp("dve", eidx.ap[0:nt, :], selI.ap[0:nt, :], [selI.r], [eidx.r])
                    yield
                    vp = [list(vals2.ap.ap[0])[0], nt]
                    S.op("dve", lambda e, vp=vp: e.tensor_tensor(
                        out=gw.ap[0:nt, :].rearrange("p (h k) -> p h k", k=16), in0=vals2.ap[0:nt],
                        in1=bass.AP(vals2.ap.tensor, vals2.ap.offset, [vp, [16, 8], [0, 16]]), op=ALU.subtract),
                        [vals2.r], [gw.r])
                    act(gw.ap[0:nt, :], gw.ap[0:nt, :], AF.Exp, [gw.r], [gw.r])
                    S.op("dve", lambda e: e.tensor_reduce(out=gsum.ap[0:nt, :], in_=gw.ap[0:nt, :].rearrange("p (h k) -> p h k", k=16),
                                                          axis=AX.X, op=ALU.add), [gw.r], [gsum.r])
                    S.op("dve", lambda e: e.reciprocal(out=gsum.ap[0:nt, :], in_=gsum.ap[0:nt, :]), [gsum.r], [gsum.r])
                    gp = [list(gsum.ap.ap[0])[0], nt]
                    S.op("dve", lambda e, gp=gp: e.tensor_tensor(
                        out=gw.ap[0:nt, :].rearrange("p (h k) -> p h k", k=16), in0=gw.ap[0:nt, :].rearrange("p (h k) -> p h k", k=16),
                        in1=bass.AP(gsum.ap.tensor, gsum.ap.offset, [gp, [1, 8], [0, 16]]), op=ALU.mult),
                        [gw.r, gsum.r], [gw.r])
                    yield


            ybanks = banks[4:8]
            GS = 2
            NG = 128 // GS
            ucnt = {"i": 0}
            vcnt = {"i": 0}
            dcnt = {"i": 0}

            selm = new([128, 16], F32)
            selc = [new([128, 16], BF16) for _ in range(4)]
            eidxP = new([128, 16], U32)
            gwP = new([128, 16], F32)
            dma("sp", selm.ap, i_selm, [], [selm.r])

            def peer_tile(ti, row0, nt, gen, packed=False):
                x1 = vtile[ti % 2]
                eidx = eidx2[ti % 2]
                gw = gw2[ti % 2]
                cco = cco2[ti % 2]
                P = nt
                nslots = 128
                xd = x1
                if packed:
                    P = 128
                    nslots = 16
                    xd = alt(scs, [128, D], F32)
                    for h in range(8):
                        dma("sp", xd.ap[h * 16:h * 16 + 16, :], x1scr[row0:row0 + 16, :], [], [xd.r])
                        dma("sp", eidxP.ap[h * 16:h * 16 + 16, :], eidx.ap[0:16, h * 16:h * 16 + 16], [eidx.r], [eidxP.r])
                        dma("sp", gwP.ap[h * 16:h * 16 + 16, :], gw.ap[0:16, h * 16:h * 16 + 16], [gw.r], [gwP.r])
                    eidx = eidxP
                    gw = gwP
                ng = nslots // GS
                actv_r = [Res() for _ in range(ng)]
                tA_r = [Res() for _ in range(ng)]
                gA_r = [Res() for _ in range(ng)]
                rA_r = [Res() for _ in range(ng)]
                cco_r = [Res() for _ in range(ng)]
                vslots = {}

                def stage_c(g):
                    cs = slice(g * GS, g * GS + GS)
                    tt("dve", cco.ap[0:P, cs], rA.ap[0:P, cs], gA.ap[0:P, cs], ALU.mult, [rA_r[g], gA_r[g], cco.r], [cco_r[g]])
                    for s_ in range(g * GS, g * GS + GS):
                        vb = vslots.pop(s_)
                        if packed:
                            dg = selc[dcnt["i"] % 4]
                            dcnt["i"] += 1
                            S.op("act", lambda e, dg=dg, s_=s_: e.activation(
                                out=dg.ap[:, :], in_=selm.ap[:, :], func=AF.Copy,
                                scale=cco.ap[0:P, s_:s_ + 1]), [selm.r, cco_r[g]], [dg.r])
                            lhs = dg.ap[:, :]
                        else:
                            dg = diag[dcnt["i"] % 4]
                            dcnt["i"] += 1
                            S.op("act", lambda e, dg=dg, s_=s_: e.activation(
                                out=dg.ap[0:nt, 0:nt], in_=ident_f.ap[0:nt, 0:nt], func=AF.Copy,
                                scale=cco.ap[0:nt, s_:s_ + 1]), [ident_f.r, cco_r[g]], [dg.r])
                            lhs = dg.ap[0:nt, 0:nt]
                        for q4 in range(4):
                            mm(ybanks[q4].ap[0:nt, :], lhs, vb.ap[0:P, q4 * 512:q4 * 512 + 512],
                               s_ == 0, s_ == nslots - 1, [dg.r, vb.r], [ybanks[q4].r])

                for s_ in range(nslots):
                    g = s_ // GS
                    pi_ = ucnt["i"] % NPB
                    pb_ = pbufs[pi_]
                    rV = pbV_r[pi_]
                    ucnt["i"] += 1
                    vb = vgb[vcnt["i"] % NVB]
                    vcnt["i"] += 1
                    vslots[s_] = vb
                    S.dma("pool", lambda e, pb_=pb_, s_=s_: e.indirect_dma_start(
                        out=pb_.ap[0:P, :], out_offset=None, in_=puv_bf,
                        in_offset=bass.IndirectOffsetOnAxis(ap=eidx.ap[0:P, s_:s_ + 1], axis=0)),
                        [eidx.r], [pb_.r, rV])
                    stt(pb_.ap[0:P, 0:D], pb_.ap[0:P, 0:D], 1.0, xd.ap[0:P, :], ALU.mult, ALU.mult,
                        [pb_.r, xd.r], [pb_.r] + ([actv_r[g]] if s_ % GS == GS - 1 else []),
                        accum_out=actv.ap[0:P, s_:s_ + 1])
                    cp("act", vb.ap[0:P, :], pb_.ap[0:P, D:2 * D], [rV], [vb.r])
                    if s_ % GS == GS - 1:
                        cs = slice(g * GS, g * GS + GS)
                        a_ = actv.ap[0:P, cs]
                        tt("dve", tA.ap[0:P, cs], a_, a_, ALU.mult, [actv_r[g]], [tA_r[g]])
                        ts("dve", tA.ap[0:P, cs], tA.ap[0:P, cs], 0.044715, 1.0, ALU.mult, ALU.add, [tA_r[g]], [tA_r[g]])
                        tt("dve", tA.ap[0:P, cs], tA.ap[0:P, cs], a_, ALU.mult, [tA_r[g], actv_r[g]], [tA_r[g]])
                        tt("dve", gA.ap[0:P, cs], gw.ap[0:P, cs], a_, ALU.mult, [gw.r, actv_r[g]], [gA_r[g]])
                        act(rA.ap[0:P, cs], tA.ap[0:P, cs], AF.Exp, [tA_r[g]], [rA_r[g]], scale=-1.5957691216057308)
                        act(rA.ap[0:P, cs], rA.ap[0:P, cs], AF.Ln, [rA_r[g]], [rA_r[g]], bias=1.0)
                        act(rA.ap[0:P, cs], rA.ap[0:P, cs], AF.Exp, [rA_r[g]], [rA_r[g]], scale=-1.0)
                        if g >= 1:
                            stage_c(g - 1)
                    if s_ >= 8 and s_ % 2 == 0:
                        next(gen, None)
                stage_c(ng - 1)
                for _ in gen:
                    pass
                for q4 in range(4):
                    stt(x1.ap[0:nt, q4 * 512:q4 * 512 + 512], x1.ap[0:nt, q4 * 512:q4 * 512 + 512], ALPHA,
                        ybanks[q4].ap[0:nt, :], ALU.mult, ALU.add, [x1.r, ybanks[q4].r], [x1.r])
                layernorm(x1, nt, 1, x1)
                out_toks.append(dma("sp", o_y[row0:row0 + nt, :], x1.ap[0:nt, :], [x1.r], []))

            NT_ = len(TILES2)
            for _ in route(0, *TILES2[0]):
                pass
            for it in range(NT_):
                gen = route(it + 1, *TILES2[it + 1]) if it + 1 < NT_ else iter(())
                peer_tile(it, TILES2[it][0], TILES2[it][1], gen, packed=(TILES2[it][1] == 16))

        try:
            body()
        except _Stop:
            pass
        out_toks += list(dbg_out.values())
        S.wait_all("sp", out_toks)
        S.barrier()
        with nc.Block() as block:
            S.emit(block)
    return nc


def _tile_w(w, kc):
    n = w.shape[1] // 128
    return np.ascontiguousarray(w.reshape(kc, 128, n, 128).transpose(2, 1, 0, 3)).reshape(n, 128, kc * 128)


def _fm(a):
    lead = a.shape[:-1]
    a = a.reshape(-1, 8, 128)
    return np.ascontiguousarray(a.transpose(2, 1, 0)).reshape(128, 8, *lead)


_PROGRAM = {}


def prep_inputs(inp):
    f = np.float32
    g = lambda k: np.asarray(inp[k], dtype=f)
    x_prompt, x_sample, mem = g("x_prompt"), g("x_sample"), g("mem_prompt")
    shared = {
        "win": _tile_w(g("w_in")[0], 16),
        "wa": np.ascontiguousarray(g("rg_wa")[0].transpose(1, 0, 2)).reshape(128, 1024),
        "wx": np.ascontiguousarray(g("rg_wx")[0].transpose(1, 0, 2)).reshape(128, 1024),
        "wmk": _tile_w(g("w_mk")[0], 16),
        "wmv": _tile_w(g("w_mv")[0], 16),
        "wbc": _tile_w(g("w_br_conv")[0], 8),
        "wbr": _tile_w(g("w_br_rnn")[0], 8),
        "wba": _tile_w(g("w_br_attn")[0], 8),
        "wo": _tile_w(g("w_o")[0], 16),
        "wq": _tile_w(g("peer_wq")[0], 16),
        "keysT": np.ascontiguousarray(g("peer_keys")[0].reshape(16, 128, 128).transpose(2, 0, 1)).reshape(128, 2048),
        "ln": np.stack([g("ln1_g")[0], g("ln1_b")[0], g("ln2_g")[0], g("ln2_b")[0]]),
        "pu": g("peer_u")[0],
        "pv": g("peer_v")[0],
        "ident": np.eye(128, dtype=f),
        "sel": np.ascontiguousarray(np.broadcast_to(np.eye(16, dtype=f)[:, :, None], (16, 16, 128))).reshape(16, 2048),
        "iota16": np.ascontiguousarray(np.broadcast_to(np.arange(16, dtype=f)[None, :], (128, 16))),
        "selm": np.ascontiguousarray(np.tile(np.eye(16, dtype=f), (8, 1))),
    }
    chp = np.concatenate([_fm(g("conv_w")[0]), _fm(g("rg_conv_w")[0]), _fm(g("rg_conv_b")), _fm(g("rg_ba")),
                          _fm(g("rg_bx")), _fm(g("rg_lambda"))], axis=2)
    shared["chp"] = np.ascontiguousarray(chp).reshape(128, 88)
    maps = []
    for c in range(NCORES):
        b, half = c // 2, c % 2
        cur = x_prompt[b, half * 1024:(half + 1) * 1024]
        prev = x_prompt[b, 0:1024] if half == 1 else np.zeros((1024, D), f)
        xs = x_sample[c * 16:(c + 1) * 16, 0]
        xall = np.concatenate([prev[-3:], cur, xs], axis=0)
        m = dict(shared)
        m["xT"] = np.ascontiguousarray(xall.T)
        m["xprevT"] = np.ascontiguousarray(prev.T)
        m["xtok"] = np.ascontiguousarray(np.concatenate([cur, xs], axis=0))
        m["flag"] = np.full((128, 1), float(half), f)
        m["memT"] = np.ascontiguousarray(mem[b].T)
        m["ck"] = np.ascontiguousarray(g("cache_mem_k")[0, c * 16:(c + 1) * 16].reshape(16, 256, 1024))
        m["cv"] = np.ascontiguousarray(g("cache_mem_v")[0, c * 16:(c + 1) * 16].reshape(16, 256, 1024))
        scz = g("state_conv_z")[0, c * 16:(c + 1) * 16]
        src = g("state_rglru_conv")[0, c * 16:(c + 1) * 16]
        sh = g("state_rglru_h")[0, c * 16:(c + 1) * 16]
        m["scz"] = np.ascontiguousarray(_fm(scz.transpose(1, 0, 2))).reshape(128, 8 * 2 * 16)
        m["src"] = np.ascontiguousarray(_fm(src.transpose(1, 0, 2))).reshape(128, 8 * 3 * 16)
        m["sh"] = np.ascontiguousarray(_fm(sh)).reshape(128, 8 * 16)
        m["scz_tok"] = np.ascontiguousarray(scz)
        m["src_tok"] = np.ascontiguousarray(src)
        maps.append(m)
    return maps


def assemble(results):
    f = np.float32
    y_prompt = np.zeros((4, 2048, D), f)
    y_sample = np.zeros((128, 1, D), f)
    mk = np.zeros((1, 4, 256, 4, 256), f)
    mv = np.zeros((1, 4, 256, 4, 256), f)
    czp = np.zeros((1, 4, 2, 1024), f)
    rcp = np.zeros((1, 4, 3, 1024), f)
    hp = np.zeros((1, 4, 1024), f)
    czs = np.zeros((1, 128, 2, 1024), f)
    rcs = np.zeros((1, 128, 3, 1024), f)
    hs = np.zeros((1, 128, 1024), f)
    for c, r in enumerate(results):
        b, half = c // 2, c % 2
        y = r["y"]
        y_prompt[b, half * 1024:(half + 1) * 1024] = y[0:1024]
        y_sample[c * 16:(c + 1) * 16, 0] = y[1024:1040]
        stv = r["st_out"]
        if half == 1:
            czp[0, b] = stv[0:2]
            rcp[0, b] = stv[2:5]
            hp[0, b] = stv[5]
        else:
            mk[0, b] = r["mk_out"].reshape(256, 4, 256)
            mv[0, b] = r["mv_out"].reshape(256, 4, 256)
        czs[0, c * 16:(c + 1) * 16] = r["czs_out"]
        rcs[0, c * 16:(c + 1) * 16] = r["rcs_out"]
        hs[0, c * 16:(c + 1) * 16] = stv[38:54]
    return (y_prompt, y_sample, mk, mv, czp, rcp, hp, czs, rcs, hs)


def kernel(**inputs):
    if "nc" not in _PROGRAM:
        _PROGRAM["nc"] = build_program()
    maps = prep_inputs(inputs)
    res = run_bass_kernel_spmd(_PROGRAM["nc"], maps, core_ids=list(range(NCORES)))
    return assemble(res.results)
```
